# Optimizing a Trainium2 kernel written in Bass

```python
import math
import jax, jax.numpy as jnp
from jax import lax
import numpy as np

D_MODEL = 1024
BATCH = 8
SEQ = 2048
DEPTH = 1

SSM_WIDTH = D_MODEL // 2
SSM_GROUP = 16
SSM_GROUPS = SSM_WIDTH // SSM_GROUP
SSM_STATE = 64
HEAD_DIM = 64
ATTN_SLOTS = (D_MODEL // 2) // HEAD_DIM
ATTN_PATTERNS = ((128, 1), (512, 4), (2048, 16))
N_PATTERNS = len(ATTN_PATTERNS)
N_Q_HEADS = ATTN_SLOTS * N_PATTERNS
ATTN_WIDTH = ATTN_SLOTS * HEAD_DIM
ROPE_THETA = 500000.0
ROPE_DIM = HEAD_DIM // 4
BAND_BLOCK = 64
N_BRANCHES = 2
EPS = 1e-6
NEG_INF = -1e30
COL_SIZES = (SSM_WIDTH, SSM_WIDTH, N_Q_HEADS * HEAD_DIM, ATTN_WIDTH, ATTN_WIDTH, ATTN_WIDTH, N_BRANCHES * D_MODEL)
IN_WIDTH = int(sum(COL_SIZES))
SPLITS = tuple(int(s) for s in np.cumsum(COL_SIZES)[:-1])

kernel_name = "hybrid_s5_dilated_attn_gated_block"


def rms_norm(x, w):
    xf = x.astype(jnp.float32)
    var = jnp.mean(xf * xf, axis=-1, keepdims=True)
    return xf * lax.rsqrt(var + EPS) * w.astype(jnp.float32)


def rope_partial(t, pos):
    inv = ROPE_THETA ** (-jnp.arange(0, ROPE_DIM, 2, dtype=jnp.float32) / ROPE_DIM)
    ang = pos[:, None] * inv[None, :]
    shape = (1, ang.shape[0]) + (1,) * (t.ndim - 3) + (ang.shape[1],)
    cos = jnp.cos(ang).reshape(shape)
    sin = jnp.sin(ang).reshape(shape)
    half = ROPE_DIM // 2
    t1 = t[..., :half]
    t2 = t[..., half:ROPE_DIM]
    rot = jnp.concatenate([t1 * cos - t2 * sin, t2 * cos + t1 * sin], axis=-1)
    return jnp.concatenate([rot, t[..., ROPE_DIM:]], axis=-1)


def dilated_band_attention(q, k, v, window, dilation):
    bsz, L, H, D = q.shape
    d = dilation
    half = window // (2 * d)
    n = L // d
    nb = -(-n // BAND_BLOCK)
    npad = nb * BAND_BLOCK
    nsub = bsz * d

    def to_sub(t):
        return t.reshape(bsz, n, d, H, D).transpose(0, 2, 3, 1, 4).reshape(nsub, H, n, D)

    qs, ks, vs = to_sub(q), to_sub(k), to_sub(v)
    qb = jnp.pad(qs, ((0, 0), (0, 0), (0, npad - n), (0, 0))).reshape(nsub, H, nb, BAND_BLOCK, D)

    def neighbours(t):
        tp = jnp.pad(t, ((0, 0), (0, 0), (BAND_BLOCK, npad - n + BAND_BLOCK), (0, 0)))
        tp = tp.reshape(nsub, H, nb + 2, BAND_BLOCK, D)
        return jnp.concatenate([tp[:, :, 0:nb], tp[:, :, 1:nb + 1], tp[:, :, 2:nb + 2]], axis=3)

    kb, vb = neighbours(ks), neighbours(vs)
    qi = jnp.arange(nb)[:, None, None] * BAND_BLOCK + jnp.arange(BAND_BLOCK)[None, :, None]
    ki = (jnp.arange(nb)[:, None, None] - 1) * BAND_BLOCK + jnp.arange(3 * BAND_BLOCK)[None, None, :]
    valid = (jnp.abs(ki - qi) <= half) & (ki >= 0) & (ki < n)

    s = jnp.einsum('nhbqd,nhbkd->nhbqk', qb, kb) * (1.0 / math.sqrt(D))
    s = jnp.where(valid, s, NEG_INF)
    m = jnp.max(s, axis=-1, keepdims=True)
    p = jnp.where(valid, jnp.exp(s - m), 0.0)
    den = jnp.sum(p, axis=-1, keepdims=True)
    o = jnp.einsum('nhbqk,nhbkd->nhbqd', p, vb) / den
    lse = (m + jnp.log(den))[..., 0]

    o = o.reshape(nsub, H, npad, D)[:, :, :n]
    lse = lse.reshape(nsub, H, npad)[:, :, :n]
    o = o.reshape(bsz, d, H, n, D).transpose(0, 3, 1, 2, 4).reshape(bsz, L, H, D)
    lse = lse.reshape(bsz, d, H, n).transpose(0, 3, 1, 2).reshape(bsz, L, H)
    return o, lse


def _complex_scan_op(e1, e2):
    ar1, ai1, br1, bi1 = e1
    ar2, ai2, br2, bi2 = e2
    ar = ar2 * ar1 - ai2 * ai1
    ai = ar2 * ai1 + ai2 * ar1
    br = ar2 * br1 - ai2 * bi1 + br2
    bi = ar2 * bi1 + ai2 * br1 + bi2
    return (ar, ai, br, bi)


def bidir_s5(u, lam_re, lam_im, log_dt, b_re, b_im, c_re, c_im, d_skip):
    bsz, L, W = u.shape
    ug = u.reshape(bsz, L, SSM_GROUPS, SSM_GROUP)
    y = jnp.zeros_like(ug)
    for direction in range(2):
        lr = jnp.minimum(lam_re[direction].astype(jnp.float32), -1e-4)
        li = lam_im[direction].astype(jnp.float32)
        dt = jnp.exp(log_dt[direction].astype(jnp.float32))[:, None]
        er = jnp.exp(lr * dt)
        ab_re = er * jnp.cos(li * dt)
        ab_im = er * jnp.sin(li * dt)
        nr = ab_re - 1.0
        ni = ab_im
        mag = lr * lr + li * li
        f_re = (nr * lr + ni * li) / mag
        f_im = (ni * lr - nr * li) / mag
        br = b_re[direction].astype(jnp.float32)
        bi = b_im[direction].astype(jnp.float32)
        bb_re = f_re[..., None] * br - f_im[..., None] * bi
        bb_im = f_re[..., None] * bi + f_im[..., None] * br
        bu_re = jnp.einsum('blgh,gph->blgp', ug, bb_re)
        bu_im = jnp.einsum('blgh,gph->blgp', ug, bb_im)
        a_re = jnp.broadcast_to(ab_re, bu_re.shape)
        a_im = jnp.broadcast_to(ab_im, bu_im.shape)
        _, _, xs_re, xs_im = lax.associative_scan(
            _complex_scan_op, (a_re, a_im, bu_re, bu_im), reverse=(direction == 1), axis=1)
        y = y + jnp.einsum('blgp,ghp->blgh', xs_re, c_re[direction].astype(jnp.float32)) \
              - jnp.einsum('blgp,ghp->blgh', xs_im, c_im[direction].astype(jnp.float32))
    return y.reshape(bsz, L, W) + d_skip.astype(jnp.float32) * u


def setup_inputs(seed: int = 0) -> dict:
    key = jax.random.key(seed)
    ks = jax.random.split(key, 20)
    f32 = jnp.float32
    G, P, H = SSM_GROUPS, SSM_STATE, SSM_GROUP
    nrm = lambda k, shape, scale: jax.random.normal(k, shape, f32) * scale
    x = jax.random.normal(ks[0], (BATCH, SEQ, D_MODEL), f32)
    norm_w = 1.0 + nrm(ks[1], (DEPTH, D_MODEL), 0.02)
    w_in = nrm(ks[2], (DEPTH, D_MODEL, IN_WIDTH), D_MODEL ** -0.5)
    b_gate = nrm(ks[3], (DEPTH, N_BRANCHES * D_MODEL), 0.02)
    q_norm_w = 1.0 + nrm(ks[4], (DEPTH, HEAD_DIM), 0.02)
    k_norm_w = 1.0 + nrm(ks[5], (DEPTH, HEAD_DIM), 0.02)
    ssm_lam_re = -0.5 + nrm(ks[6], (DEPTH, 2, G, P), 0.01)
    ssm_lam_im = jnp.pi * jnp.arange(P, dtype=f32)[None, None, None, :] + nrm(ks[7], (DEPTH, 2, G, P), 0.01)
    ssm_log_dt = jax.random.uniform(ks[8], (DEPTH, 2, G), f32, math.log(1e-3), math.log(1e-1))
    ssm_b_re = nrm(ks[9], (DEPTH, 2, G, P, H), (2.0 * H) ** -0.5)
    ssm_b_im = nrm(ks[10], (DEPTH, 2, G, P, H), (2.0 * H) ** -0.5)
    ssm_c_re = nrm(ks[11], (DEPTH, 2, G, H, P), (P / 2.0) ** -0.5)
    ssm_c_im = nrm(ks[12], (DEPTH, 2, G, H, P), (P / 2.0) ** -0.5)
    ssm_d = 1.0 + nrm(ks[13], (DEPTH, SSM_WIDTH), 0.1)
    w_glu = nrm(ks[14], (DEPTH, SSM_WIDTH, 2 * SSM_WIDTH), SSM_WIDTH ** -0.5)
    b_glu = nrm(ks[15], (DEPTH, 2 * SSM_WIDTH), 0.02)
    w_proj_ssm = nrm(ks[16], (DEPTH, SSM_WIDTH, D_MODEL), SSM_WIDTH ** -0.5)
    w_proj_attn = nrm(ks[17], (DEPTH, ATTN_WIDTH, D_MODEL), ATTN_WIDTH ** -0.5)
    w_out = nrm(ks[18], (DEPTH, D_MODEL, D_MODEL), D_MODEL ** -0.5)
    return {"x": x, "norm_w": norm_w, "w_in": w_in, "b_gate": b_gate,
            "q_norm_w": q_norm_w, "k_norm_w": k_norm_w,
            "ssm_lam_re": ssm_lam_re, "ssm_lam_im": ssm_lam_im, "ssm_log_dt": ssm_log_dt,
            "ssm_b_re": ssm_b_re, "ssm_b_im": ssm_b_im, "ssm_c_re": ssm_c_re, "ssm_c_im": ssm_c_im,
            "ssm_d": ssm_d, "w_glu": w_glu, "b_glu": b_glu,
            "w_proj_ssm": w_proj_ssm, "w_proj_attn": w_proj_attn, "w_out": w_out}


def reference(x, norm_w, w_in, b_gate, q_norm_w, k_norm_w, ssm_lam_re, ssm_lam_im, ssm_log_dt,
              ssm_b_re, ssm_b_im, ssm_c_re, ssm_c_im, ssm_d, w_glu, b_glu,
              w_proj_ssm, w_proj_attn, w_out):
    bsz, L, _ = x.shape
    pos = jnp.arange(L, dtype=jnp.float32)
    for layer in range(DEPTH):
        h = rms_norm(x, norm_w[layer])
        proj = h @ w_in[layer].astype(jnp.float32)
        u_a, z_a, q, k, v, z_b, g = jnp.split(proj, SPLITS, axis=-1)
        g_a, g_b = jnp.split(g + b_gate[layer].astype(jnp.float32), N_BRANCHES, axis=-1)

        q = q.reshape(bsz, L, N_PATTERNS, ATTN_SLOTS, HEAD_DIM)
        k = k.reshape(bsz, L, ATTN_SLOTS, HEAD_DIM)
        v = v.reshape(bsz, L, ATTN_SLOTS, HEAD_DIM)
        q = rope_partial(rms_norm(q, q_norm_w[layer]), pos)
        k = rope_partial(rms_norm(k, k_norm_w[layer]), pos)
        outs, lses = [], []
        for p_idx, (window, dilation) in enumerate(ATTN_PATTERNS):
            o, lse = dilated_band_attention(q[:, :, p_idx], k, v, window, dilation)
            outs.append(o)
            lses.append(lse)
        wts = jax.nn.softmax(jnp.stack(lses, axis=0), axis=0)
        attn = jnp.einsum('gblh,gblhd->blhd', wts, jnp.stack(outs, axis=0)).reshape(bsz, L, ATTN_WIDTH)
        y_b = (attn * jax.nn.silu(z_b)) @ w_proj_attn[layer].astype(jnp.float32)

        y_s = bidir_s5(u_a, ssm_lam_re[layer], ssm_lam_im[layer], ssm_log_dt[layer],
                       ssm_b_re[layer], ssm_b_im[layer], ssm_c_re[layer], ssm_c_im[layer], ssm_d[layer])
        y_s = jax.nn.gelu(y_s)
        glu_val, glu_gate = jnp.split(y_s @ w_glu[layer].astype(jnp.float32) + b_glu[layer].astype(jnp.float32), 2, axis=-1)
        y_a = (glu_val * jax.nn.sigmoid(glu_gate) * jax.nn.silu(z_a)) @ w_proj_ssm[layer].astype(jnp.float32)

        mix = jax.nn.sigmoid(g_a) * y_a + jax.nn.sigmoid(g_b) * y_b
        x = x + (mix @ w_out[layer].astype(jnp.float32)).astype(x.dtype)
    return x
```

```python
import math
import numpy as np
import ml_dtypes
import concourse.bass as bass
import concourse.mybir as mybir
from concourse.bass_utils import run_bass_kernel_spmd

F32 = mybir.dt.float32
BF16 = mybir.dt.bfloat16
I32 = mybir.dt.int32
AF = mybir.ActivationFunctionType
ALU = mybir.AluOpType
AX = mybir.AxisListType

L = 2048
D = 1024
EPS = 1e-6
NCORES = 8
DEBUG = {}
STOP = None
BAN = None
GELU = AF.Gelu_apprx_tanh


class Buf:
    __slots__ = ("name", "ws", "r")

    def __init__(self, name):
        self.name = name
        self.ws = []
        self.r = []


class Op:
    __slots__ = ("eng", "fn", "deps", "signal", "sigval", "dma", "dsem", "dval", "dprev", "perm")

    def __init__(self, eng, fn, dma=False):
        self.eng = eng
        self.fn = fn
        self.deps = set()
        self.signal = False
        self.sigval = 0
        self.dma = dma
        self.dsem = None
        self.dval = 0
        self.dprev = None
        self.perm = False


class Prog:
    ENGS = ("pe", "act", "dve", "pool", "sp")

    def __init__(self, nc, n_dma_sems=24):
        self.nc = nc
        self.e = {"pe": nc.tensor, "act": nc.scalar, "dve": nc.vector, "pool": nc.gpsimd, "sp": nc.sync}
        self.ops = {k: [] for k in self.ENGS}
        self.allops = []
        self.n_dma_sems = n_dma_sems
        self.dma_rr = 0
        self.dma_last = [None] * n_dma_sems
        self.dma_cnt = [0] * n_dma_sems
        self.n_unique = 0

    def op(self, eng, fn, reads=(), writes=(), dma=False):
        o = Op(eng, fn, dma)
        for b in reads:
            for w in b.ws:
                o.deps.add(w)
        for b in writes:
            for w in b.ws:
                if not (dma and w.dma):
                    o.deps.add(w)
            for r in b.r:
                o.deps.add(r)
        for b in writes:
            if dma:
                b.ws = [w for w in b.ws if w.dma] + [o]
            else:
                b.ws = [o]
            b.r = []
        for b in reads:
            if o not in b.ws:
                b.r.append(o)
        o.deps.discard(o)
        if dma and eng == "pool":
            o.dsem = ("u", self.n_unique)
            self.n_unique += 1
            o.dval = 16
        elif dma:
            k = self.dma_rr
            self.dma_rr = (k + 1) % self.n_dma_sems
            o.dsem = k
            self.dma_cnt[k] += 1
            o.dval = 16 * self.dma_cnt[k]
            o.dprev = self.dma_last[k]
            self.dma_last[k] = o
            if o.dprev is not None:
                o.deps.add(o.dprev)
        self.ops[eng].append(o)
        self.allops.append(o)
        return o

    def barrier(self, bufs):
        pass

    def emit(self):
        nc = self.nc
        sems = {k: nc.semaphore("s_" + k).__enter__() for k in self.ENGS}
        dsems = {i: nc.semaphore("d_%d" % i).__enter__() for i in range(self.n_dma_sems)}
        for i in range(self.n_unique):
            dsems[("u", i)] = nc.semaphore("u_%d" % i).__enter__()
        for o in self.allops:
            for d in o.deps:
                if not d.dma:
                    d.signal = True
        for k in self.ENGS:
            c = 0
            for o in self.ops[k]:
                if o.signal and not o.dma:
                    c += 1
                    o.sigval = c
        for k in self.ENGS:
            eng = self.e[k]
            seen = {}
            for o in self.ops[k]:
                need = {}
                for d in o.deps:
                    if d.dma:
                        key = ("d", d.dsem)
                        val = d.dval
                    else:
                        key = ("e", d.eng)
                        val = d.sigval
                    if need.get(key, 0) < val:
                        need[key] = val
                for key, val in need.items():
                    if seen.get(key, 0) >= val:
                        continue
                    seen[key] = val
                    s = dsems[key[1]] if key[0] == "d" else sems[key[1]]
                    eng.wait_ge(s, val)
                ins = o.fn()
                if o.dma:
                    ins.then_inc(dsems[o.dsem], 16)
                elif o.signal:
                    ins.then_inc(sems[k], 1)
                    seen[("e", k)] = max(seen.get(("e", k), 0), 0)


def host_consts():
    c = {}
    ident = np.eye(128, dtype=np.float32)
    c["c_ident"] = ident.astype(ml_dtypes.bfloat16)
    c["c_identj"] = ident[::-1].copy().astype(ml_dtypes.bfloat16)
    c["c_identf"] = ident.copy()
    k = np.arange(128)[:, None]
    q = np.arange(256)[None, :]
    mb01 = np.where(np.abs((q - 64) - k) <= 64, 0.0, -30000.0).astype(np.float32)
    c["c_mb01"] = mb01.astype(ml_dtypes.bfloat16)
    q2 = np.arange(128)[None, :]
    mb2 = np.where(np.abs(q2 - k) <= 64, 0.0, -30000.0).astype(np.float32)
    c["c_mb2"] = mb2.astype(ml_dtypes.bfloat16)
    inv = 500000.0 ** (-np.arange(0, 16, 2, dtype=np.float32) / 16.0)
    pos = np.arange(L, dtype=np.float32)
    ang = pos[:, None] * inv[None, :]
    cos = np.cos(ang).astype(np.float32)
    sin = np.sin(ang).astype(np.float32)
    cc = np.concatenate([cos, cos], axis=1).reshape(16, 128, 16).transpose(1, 0, 2)
    ss = np.concatenate([-sin, sin], axis=1).reshape(16, 128, 16).transpose(1, 0, 2)
    c["c_ropec"] = np.ascontiguousarray(cc)
    c["c_ropes"] = np.ascontiguousarray(ss)
    t1 = np.zeros((128, 9, 32), np.float32)
    for j in range(9):
        t1[:64, j, :] = j
        t1[64:, j, :] = 8 - j
    c["c_tau1"] = t1
    t2 = np.zeros((128, 8, 32), np.float32)
    for s in range(8):
        t2[:64, s, :] = 7 - s
        t2[64:, s, :] = s
    c["c_tau2"] = t2
    e16 = np.zeros((128, 16), np.float32)
    for gq in range(4):
        for h in range(16):
            e16[32 * gq + h, h] = 1.0
    c["c_e16"] = e16
    return c


CONST_SPECS = [("c_ident", [128, 128], BF16), ("c_identj", [128, 128], BF16), ("c_identf", [128, 128], F32),
               ("c_mb01", [128, 256], BF16), ("c_mb2", [128, 128], BF16),
               ("c_ropec", [128, 16, 16], F32), ("c_ropes", [128, 16, 16], F32),
               ("c_tau1", [128, 9, 32], F32), ("c_tau2", [128, 8, 32], F32), ("c_e16", [128, 16], F32)]

IN_SPECS = [("x", [L, D]), ("norm_w", [D]), ("w_in", [D, 6144]), ("b_gate", [2048]), ("q_norm_w", [64]),
            ("k_norm_w", [64]), ("ssm_lam_re", [2, 32, 64]), ("ssm_lam_im", [2, 32, 64]), ("ssm_log_dt", [2, 32]),
            ("ssm_b_re", [2, 32, 64, 16]), ("ssm_b_im", [2, 32, 64, 16]), ("ssm_c_re", [2, 32, 16, 64]),
            ("ssm_c_im", [2, 32, 16, 64]), ("ssm_d", [512]), ("w_glu", [512, 1024]), ("b_glu", [1024]),
            ("w_proj_ssm", [512, 1024]), ("w_proj_attn", [512, 1024]), ("w_out", [D, D])]


class StopBuild(Exception):
    pass


def build(debug=None, stop=None):
    nc = bass.Bass("TRN2", target_bir_lowering=False, dynamic_dma_scratch_size=4096)
    _CACHE['nc'] = nc
    P = Prog(nc)
    dr = {}
    for name, shape in IN_SPECS:
        dr[name] = nc.dram_tensor(name, shape, F32, kind="ExternalInput").ap()
    for name, shape, dt in CONST_SPECS:
        dr[name] = nc.dram_tensor(name, shape, dt, kind="ExternalInput").ap()
    out_d = nc.dram_tensor("out", [L, D], F32, kind="ExternalOutput").ap()
    dbg_d = {}
    if debug:
        for name, shape in debug.items():
            dbg_d[name] = nc.dram_tensor("dbg_" + name, shape, F32, kind="ExternalOutput").ap()

    stacks = {"left": [], "right": []}

    def sb(name, shape, dt, side="right"):
        t = nc.sbuf_tensor(name, shape, dt, side=side)
        h = t.__enter__()
        stacks[side].append(t)
        return h

    def mark(side="right"):
        return len(stacks[side])

    def release_to(n, side="right"):
        while len(stacks[side]) > n:
            stacks[side].pop().__exit__(None, None, None)

    psum = []
    for i in range(8):
        t = nc.psum_tensor("ps%d" % i, [128, 512], F32)
        psum.append((t.__enter__(), Buf("ps%d" % i)))
    ps_rr = [0]

    def ps_next():
        i = ps_rr[0]
        ps_rr[0] = (i + 1) % 8
        return psum[i]

    def dma(eng, out, in_, reads=(), writes=(), **kw):
        q = P.e[eng]
        return P.op(eng, lambda: q.dma_start(out=out, in_=in_, **kw), reads, writes, dma=True)

    def mm(out, lhsT, rhs, start, stop, reads=(), writes=(), tp=None):
        if tp is None:
            return P.op("pe", lambda: nc.tensor.matmul(out, lhsT, rhs, start=start, stop=stop, skip_group_check=True),
                        reads, writes)
        return P.op("pe", lambda: nc.tensor.matmul(out, lhsT, rhs, start=start, stop=stop, skip_group_check=True,
                                                   tile_position=tp), reads, writes)

    def tr32(out, in_, idn, reads=(), writes=()):
        return P.op("pe", lambda: nc.tensor.transpose(out, in_, idn), reads, writes)

    def act(out, in_, func, reads=(), writes=(), **kw):
        return P.op("act", lambda: nc.scalar.activation(out=out, in_=in_, func=func, **kw), reads, writes)

    def tt(eng, out, in0, in1, op, reads=(), writes=()):
        e = P.e[eng]
        return P.op(eng, lambda: e.tensor_tensor(out=out, in0=in0, in1=in1, op=op), reads, writes)

    def ts(eng, out, in0, s1, op0, s2=None, op1=None, reads=(), writes=()):
        e = P.e[eng]
        if op1 is None:
            return P.op(eng, lambda: e.tensor_scalar(out=out, in0=in0, scalar1=s1, scalar2=None, op0=op0), reads, writes)
        return P.op(eng, lambda: e.tensor_scalar(out=out, in0=in0, scalar1=s1, scalar2=s2, op0=op0, op1=op1), reads, writes)

    def stt(out, in0, scalar, in1, op0, op1, reads=(), writes=()):
        return P.op("dve", lambda: nc.vector.scalar_tensor_tensor(out=out, in0=in0, scalar=scalar, in1=in1, op0=op0, op1=op1),
                    reads, writes)

    def recip(out, in_, reads=(), writes=()):
        return P.op("dve", lambda: nc.vector.reciprocal(out=out, in_=in_), reads, writes)

    def cp(eng, out, in_, reads=(), writes=()):
        if eng == "act":
            return P.op("act", lambda: nc.scalar.activation(out=out, in_=in_, func=AF.Copy), reads, writes)
        e = P.e[eng]
        return P.op(eng, lambda: e.tensor_copy(out=out, in_=in_), reads, writes)

    def memset(eng, ap, val, writes=()):
        e = P.e[eng]
        return P.op(eng, lambda: e.memset(ap, val), (), writes)

    def pdone(pb):
        pb.ws = [P.allops[-1]]

    dbg_bufs = []

    def dump(name, src_ap, src_buf, parts, cols):
        if not debug or name not in debug:
            return
        tmp = sb("dbgtmp_" + name, [128, cols], F32, side="left")
        tb = Buf("dbgtmp_" + name)
        cp("dve", tmp[0:parts, :], src_ap, reads=[src_buf], writes=[tb])
        ob = Buf("dbgout_" + name)
        dma("sp", dbg_d[name][0:parts, :], tmp[0:parts, :], reads=[tb], writes=[ob])
        dbg_bufs.append(ob)

    evac_rr = [0]

    def evac_eng():
        evac_rr[0] ^= 1
        return "act" if evac_rr[0] else "dve"

    barrier_mark = [0]

    def global_barrier():
        lasts = []
        for k in Prog.ENGS:
            for o_ in reversed(P.ops[k]):
                if not (o_.dma and o_.perm):
                    lasts.append(o_)
                    break
        n_prev = len(P.allops)
        bb = Buf("barrier")
        o = P.op("sp", lambda: nc.sync.nop(), [], [bb])
        for l in lasts:
            if l is not o:
                o.deps.add(l)
        for d_ in P.allops[barrier_mark[0]:n_prev]:
            if d_.dma and not d_.perm:
                o.deps.add(d_)
        barrier_mark[0] = n_prev
        for k in ("pe", "act", "dve", "pool"):
            P.op(k, (lambda k=k: P.e[k].nop()), [bb], [])
        P.op("sp", lambda: nc.sync.nop(), [bb], [])

    def maybe_stop(name):
        if stop == name:
            P.op("sp", lambda: nc.sync.nop(), dbg_bufs, [])
            P.emit()
            raise StopBuild()

    late_dmas = []

    def load_const(name, side="left", late=False):
        shape, dt = [(s, d) for (n, s, d) in CONST_SPECS if n == name][0]
        t = sb("s_" + name, shape, dt, side=side)
        b = Buf(name)
        if late:
            late_dmas.append(lambda: dma("sp", t[:], dr[name], writes=[b]))
        else:
            dma("sp", t[:], dr[name], writes=[b])
        return t, b

    ident, B_ID = load_const("c_ident")
    identj, B_IDJ = load_const("c_identj", late=True)
    identf, B_IDF = load_const("c_identf")
    mb01, B_mb01 = load_const("c_mb01", late=True)
    mb2, B_mb2 = load_const("c_mb2", late=True)
    ropec, B_ropec = load_const("c_ropec", late=True)
    ropes, B_ropes = load_const("c_ropes", late=True)
    bgate = sb("bgate", [128, 16], F32, side="left")
    B_bgate = Buf("bgate")
    bglu = sb("bglu", [128, 8], F32, side="left")
    B_bglu = Buf("bglu")
    wqk = sb("wqk", [128, 8, 64], F32, side="left")
    B_wqk = Buf("wqk")
    for blk in range(8):
        late_dmas.append(lambda blk=blk: dma("sp", wqk[:, blk, :], (dr["q_norm_w"] if blk < 6 else dr["k_norm_w"]).unsqueeze(0).broadcast_to([128, 64]),
                                             writes=[B_wqk]))
    xnT = sb("xnT", [128, 8, L], BF16, side="left")
    B_xnT = [Buf("xnT%d" % i) for i in range(16)]
    HB = sb("HB", [128, 4, L], BF16, side="left")
    B_HB = Buf("HB")
    B_YH = [Buf("YH%d" % i) for i in range(4)]
    ARENA = sb("ARENA", [128, 8192], BF16, side="left")
    B_wglu = Buf("wglu")
    B_wza = Buf("wza")
    m_left_ssm = mark("left")
    CAre = sb("CAre", [128, 32, 9, 16], BF16, side="left")
    NCAim = sb("NCAim", [128, 32, 9, 16], BF16, side="left")
    Wt = sb("Wt", [128, 32, 128], BF16, side="left")
    Bex = sb("Bex", [128, 32, 2, 128], BF16, side="left")
    wU = sb("wU", [128, 8, 512], BF16, side="left")
    B_wU = Buf("wU")

    m0_left = mark("left")
    normw = sb("normw", [128, D], F32, side="left")
    B_normw = Buf("normw")
    dma("sp", normw[:], dr["norm_w"].unsqueeze(0).broadcast_to([128, D]), writes=[B_normw])
    NXI, NXB = 4, 3
    xin = [sb("xin%d" % i, [128, D], F32, side="left") for i in range(NXI)]
    B_xin = [Buf("xin%d" % i) for i in range(NXI)]
    junk = sb("junk", [128, D], BF16, side="left")
    B_junk = Buf("junk")
    xnb = [sb("xnb%d" % i, [128, D], BF16, side="left") for i in range(NXB)]
    B_xnb = [Buf("xnb%d" % i) for i in range(NXB)]
    stat = sb("stat", [128, 16, 4], F32, side="left")
    B_stat = [Buf("stat%d" % i) for i in range(16)]
    pa_st = {}

    def pa_load(i):
        xt, bx = xin[i % NXI], B_xin[i % NXI]
        dma("act" if i == 0 else "pool", xt[:], dr["x"][128 * i:128 * (i + 1), :], writes=[bx])

    def pa_sq(i):
        xt, bx = xin[i % NXI], B_xin[i % NXI]
        act(junk[:], xt[:], AF.Square, reads=[bx], writes=[B_junk, B_stat[i]], accum_out=stat[:, i, 0:1])

    def pa_ts(i):
        pass

    def pa_sqrt(i):
        act(stat[:, i, 2:3], stat[:, i, 0:1], AF.Sqrt, reads=[B_stat[i]], writes=[B_stat[i]], scale=1.0 / D, bias=EPS)

    def pa_scale(i):
        xt, bx = xin[i % NXI], B_xin[i % NXI]
        xb, bxb = xnb[i % NXB], B_xnb[i % NXB]
        recip(stat[:, i, 3:4], stat[:, i, 2:3], [B_stat[i]], [B_stat[i]])
        stt(xb[:], xt[:], stat[:, i, 3:4], normw[:], ALU.mult, ALU.mult, [bx, B_stat[i], B_normw], [bxb])

    def pa_tr(i):
        xb, bxb = xnb[i % NXB], B_xnb[i % NXB]
        for half in range(2):
            pt, pb = ps_next()
            for j in range(4):
                kt = half * 4 + j
                mm(pt[:, j * 128:(j + 1) * 128], xb[:, kt * 128:(kt + 1) * 128], ident[:], True, True,
                   reads=[bxb, B_ID], writes=[pb] if j == 0 else [])
            pdone(pb)
            pa_st[(i, half)] = (pt, pb)

    def pa_ev(i):
        for half in range(2):
            pt, pb = pa_st[(i, half)]
            cp("act", xnT[:, half * 4:half * 4 + 4, 128 * i:128 * (i + 1)],
               pt[:].rearrange("p (a n) -> p a n", a=4), reads=[pb], writes=[B_xnT[i]])

    def phaseA_gen():
        stages_a = ((pa_load, 0), (pa_sq, 1), (pa_ts, 2), (pa_sqrt, 2), (pa_scale, 3), (pa_tr, 4), (pa_ev, 5))
        for t_ in range(16 + 5):
            for (f_, lag_) in stages_a:
                if 0 <= t_ - lag_ < 16:
                    f_(t_ - lag_)
            yield

    genA = phaseA_gen()
    hook_state = {"on": False, "busy": False, "cnt": 0}
    _orig_op = P.op

    def _hooked_op(eng, fn, reads=(), writes=(), dma=False):
        o = _orig_op(eng, fn, reads, writes, dma)
        if hook_state["on"] and not hook_state["busy"] and eng == "dve":
            hook_state["cnt"] += 1
            if hook_state["cnt"] % 5 == 0:
                hook_state["busy"] = True
                keep = P.allops[-1]
                next(genA, None)
                hook_state["busy"] = False
        return o

    P.op = _hooked_op
    for _ in range(7):
        next(genA, None)
    hook_state["on"] = True
    if debug:
        dump("xnT", xnT[:, 0, :], B_xnT[15], 128, L)
    maybe_stop("A")

    def wload(dst_ap, src_ap, buf, perm=False):
        o_ = dma("pool", dst_ap, src_ap, writes=[buf])
        o_.perm = perm
        return o_

    def w_in_cols(dst, c0, ncols, buf, perm=False):
        wload(dst, dr["w_in"][:, c0:c0 + ncols].rearrange("(kt p) n -> p kt n", p=128), buf, perm)

    w_in_cols(wU[:], 0, 512, B_wU)

    m_ssm = mark()
    B_CA = Buf("CA")
    B_W = Buf("W")
    B_Bex = Buf("Bex")
    A8 = sb("A8", [128, 32, 2], F32)
    A8s = sb("A8s", [128, 32, 2], F32)
    B_A8 = Buf("A8")
    CAY = sb("CAY", [128, 2, 32, 8, 16], BF16)
    B_CAY = Buf("CAY")
    m_pre = mark()
    BBb = sb("BBb", [128, 2, 32, 16], BF16)
    B_BBb = Buf("BBb")
    m_k = mark()
    tau1, B_tau1 = load_const("c_tau1", side="right")
    tau2, B_tau2 = load_const("c_tau2", side="right")
    e16, B_e16 = load_const("c_e16", side="right")
    LL = sb("LL", [32, 2, 128], F32)
    B_LL = Buf("LL")
    for ri, nm in enumerate(("ssm_lam_re", "ssm_lam_im")):
        for d_ in range(2):
            dma("sp", LL[:, ri, d_ * 64:(d_ + 1) * 64], dr[nm][d_], writes=[B_LL])
    LRI = sb("LRI", [128, 2, 32], F32)
    B_LRI = Buf("LRI")
    pt, pb = ps_next()
    for ri in range(2):
        tr32(pt[:, ri * 32:(ri + 1) * 32], LL[:, ri, :], identf[0:32, 0:32], reads=[B_LL, B_IDF], writes=[pb] if ri == 0 else [])
    pdone(pb)
    cp("dve", LRI[:], pt[:, 0:64].rearrange("p (a g) -> p a g", a=2), reads=[pb], writes=[B_LRI])
    DT = sb("DT", [128, 32], F32)
    B_DT = Buf("DT")
    for d_ in range(2):
        dma("sp", DT[d_ * 64:(d_ + 1) * 64, :], dr["ssm_log_dt"][d_:d_ + 1, :].broadcast_to([64, 32]), writes=[B_DT])
    act(DT[:], DT[:], AF.Exp, reads=[B_DT], writes=[B_DT])
    ts("dve", LRI[:, 0, :], LRI[:, 0, :], -1e-4, ALU.min, reads=[B_LRI], writes=[B_LRI])
    E1 = sb("E1", [128, 2, 32], F32)
    B_E1 = Buf("E1")
    tt("dve", E1[:], LRI[:], DT[:].unsqueeze(1).broadcast_to([128, 2, 32]), ALU.mult, reads=[B_LRI, B_DT], writes=[B_E1])

    pex = sb("pw_ex", [128, 9, 32], F32)
    pan = sb("pw_an", [128, 2, 9, 32], F32)
    pki = sb("pw_ki", [128, 2, 9, 32], I32)
    pkf = sb("pw_kf", [128, 2, 9, 32], F32)
    pcm = sb("pw_cm", [128, 2, 9, 32], F32)
    bs = Buf("pw_scratch")

    def power_table(name, tau, btau, nj):
        AR = sb(name + "r", [128, nj, 32], F32)
        AI = sb(name + "i", [128, nj, 32], F32)
        bt = Buf(name)
        ex, an, ki, kf, cm = pex[:, 0:nj], pan[:, :, 0:nj], pki[:, :, 0:nj], pkf[:, :, 0:nj], pcm[:, :, 0:nj]
        e1b = E1[:, 0, :].unsqueeze(1).broadcast_to([128, nj, 32])
        thb = E1[:, 1, :].unsqueeze(1).broadcast_to([128, nj, 32])
        tt("dve", ex, tau[:], e1b, ALU.mult, reads=[btau, B_E1], writes=[bs])
        act(ex, ex, AF.Exp, reads=[bs], writes=[bs])
        tt("dve", an[:, 0], tau[:], thb, ALU.mult, reads=[btau, B_E1, bs], writes=[bs])
        c_hi = float(np.float32(1.0 / (2 * math.pi)))
        c_lo = 1.0 / (2 * math.pi) - c_hi
        ts("dve", cm[:, 0], an[:, 0], c_lo, ALU.mult, reads=[bs], writes=[bs])
        ts("dve", an[:, 1], an[:, 0], c_hi, ALU.mult, 0.25, ALU.add, reads=[bs], writes=[bs])
        ts("dve", an[:, 0], an[:, 0], c_hi, ALU.mult, reads=[bs], writes=[bs])
        tt("dve", an[:, 0], an[:, 0], cm[:, 0], ALU.add, reads=[bs], writes=[bs])
        tt("dve", an[:, 1], an[:, 1], cm[:, 0], ALU.add, reads=[bs], writes=[bs])
        cp("dve", ki, an, reads=[bs], writes=[bs])
        cp("dve", kf, ki, reads=[bs], writes=[bs])
        tt("dve", an, an, kf, ALU.subtract, reads=[bs], writes=[bs])
        ts("dve", cm, an, 0.5, ALU.is_gt, reads=[bs], writes=[bs])
        tt("dve", an, an, cm, ALU.subtract, reads=[bs], writes=[bs])
        ts("dve", cm, an, -0.5, ALU.is_lt, reads=[bs], writes=[bs])
        tt("dve", an, an, cm, ALU.add, reads=[bs], writes=[bs])
        act(an, an, AF.Sin, reads=[bs], writes=[bs], scale=2 * math.pi)
        tt("dve", AI[:], ex, an[:, 0], ALU.mult, reads=[bs], writes=[bt])
        tt("dve", AR[:], ex, an[:, 1], ALU.mult, reads=[bs], writes=[bt])
        return AR, AI, bt

    AR1, AI1, B_A1 = power_table("pw1", tau1, B_tau1, 9)
    AR2, AI2, B_A2 = power_table("pw2", tau2, B_tau2, 8)
    for (lo, hi, j) in ((0, 64, 8), (64, 128, 0)):
        cp("dve", A8[lo:hi, :, :], AR1[lo:hi, j, :].unsqueeze(2).broadcast_to([hi - lo, 32, 2]), reads=[B_A1], writes=[B_A8])
        ts("dve", A8s[lo:hi, :, 0:1], AI1[lo:hi, j, :].unsqueeze(2), -1.0, ALU.mult, reads=[B_A1], writes=[B_A8])
        cp("dve", A8s[lo:hi, :, 1:2], AI1[lo:hi, j, :].unsqueeze(2), reads=[B_A1], writes=[B_A8])
    maybe_stop("PRE1")
    ZZ = sb("ZZ", [128, 8, 32], F32)
    B_ZZ = Buf("ZZ")
    for (lo, hi, j) in ((0, 64, 1), (64, 128, 7)):
        ts("dve", ZZ[lo:hi, 0, :], AR1[lo:hi, j, :], -1.0, ALU.add, reads=[B_A1], writes=[B_ZZ])
        cp("dve", ZZ[lo:hi, 1, :], AI1[lo:hi, j, :], reads=[B_A1], writes=[B_ZZ])
    lr_, li_ = LRI[:, 0, :], LRI[:, 1, :]
    RW = [B_ZZ, B_LRI]
    tt("dve", ZZ[:, 2, :], lr_, lr_, ALU.mult, reads=RW, writes=[B_ZZ])
    tt("dve", ZZ[:, 3, :], li_, li_, ALU.mult, reads=RW, writes=[B_ZZ])
    tt("dve", ZZ[:, 2, :], ZZ[:, 2, :], ZZ[:, 3, :], ALU.add, reads=RW, writes=[B_ZZ])
    recip(ZZ[:, 2, :], ZZ[:, 2, :], RW, [B_ZZ])
    tt("dve", ZZ[:, 4, :], ZZ[:, 0, :], lr_, ALU.mult, reads=RW, writes=[B_ZZ])
    tt("dve", ZZ[:, 6, :], ZZ[:, 1, :], li_, ALU.mult, reads=RW, writes=[B_ZZ])
    tt("dve", ZZ[:, 4, :], ZZ[:, 4, :], ZZ[:, 6, :], ALU.add, reads=RW, writes=[B_ZZ])
    tt("dve", ZZ[:, 4, :], ZZ[:, 4, :], ZZ[:, 2, :], ALU.mult, reads=RW, writes=[B_ZZ])
    tt("dve", ZZ[:, 5, :], ZZ[:, 1, :], lr_, ALU.mult, reads=RW, writes=[B_ZZ])
    tt("dve", ZZ[:, 7, :], ZZ[:, 0, :], li_, ALU.mult, reads=RW, writes=[B_ZZ])
    tt("dve", ZZ[:, 5, :], ZZ[:, 5, :], ZZ[:, 7, :], ALU.subtract, reads=RW, writes=[B_ZZ])
    tt("dve", ZZ[:, 5, :], ZZ[:, 5, :], ZZ[:, 2, :], ALU.mult, reads=RW, writes=[B_ZZ])
    Braw = sb("Braw", [128, 2, 32, 16], F32)
    B_Braw = Buf("Braw")
    for ri, nm in enumerate(("ssm_b_re", "ssm_b_im")):
        for d_ in range(2):
            for g4 in range(0, 32, 4):
                dma("sp", Braw[d_ * 64:(d_ + 1) * 64, ri, g4:g4 + 4], dr[nm][d_][g4:g4 + 4].rearrange("g p h -> p g h"),
                    writes=[B_Braw])
    BB = sb("BB", [128, 2, 32, 16], F32)
    B_BB = Buf("BB")
    SC1 = sb("SC1", [128, 1152], F32)
    SC2 = sb("SC2", [128, 1152], F32)
    B_SC = Buf("SC")
    Tm = SC1[:, 0:1024].rearrange("p (a g h) -> p a g h", a=2, g=32)
    fre = ZZ[:, 4, :].unsqueeze(2).broadcast_to([128, 32, 16])
    fim = ZZ[:, 5, :].unsqueeze(2).broadcast_to([128, 32, 16])
    tt("dve", BB[:, 0], Braw[:, 0], fre, ALU.mult, reads=[B_Braw, B_ZZ], writes=[B_BB])
    tt("dve", Tm[:, 0], Braw[:, 1], fim, ALU.mult, reads=[B_Braw, B_ZZ], writes=[B_SC])
    tt("dve", BB[:, 0], BB[:, 0], Tm[:, 0], ALU.subtract, reads=[B_SC, B_BB], writes=[B_BB])
    tt("dve", BB[:, 1], Braw[:, 1], fre, ALU.mult, reads=[B_Braw, B_ZZ, B_BB], writes=[B_BB])
    tt("dve", Tm[:, 1], Braw[:, 0], fim, ALU.mult, reads=[B_Braw, B_ZZ, B_SC], writes=[B_SC])
    tt("dve", BB[:, 1], BB[:, 1], Tm[:, 1], ALU.add, reads=[B_SC, B_BB], writes=[B_BB])
    cp("dve", BBb[:], BB[:], reads=[B_BB], writes=[B_BBb])
    CC = sb("CC", [128, 2, 4, 128], F32)
    B_CC = Buf("CC")
    for ri, nm in enumerate(("ssm_c_re", "ssm_c_im")):
        for d_ in range(2):
            dma("sp", CC[:, ri, :, d_ * 64:(d_ + 1) * 64],
                dr[nm][d_].rearrange("(gq g8) h p -> (g8 h) gq p", gq=4), writes=[B_CC])
    CT = sb("CT", [128, 2, 32, 16], F32)
    B_CT = Buf("CT")
    for ri in range(2):
        pt, pb = ps_next()
        for gq in range(4):
            tr32(pt[:, gq * 128:(gq + 1) * 128], CC[:, ri, gq, :], identf[:], reads=[B_CC, B_IDF], writes=[pb] if gq == 0 else [])
        pdone(pb)
        cp("dve", CT[:, ri].rearrange("p g h -> p (g h)"), pt[:], reads=[pb], writes=[B_CT])
    maybe_stop("PRE2")
    for f_ in late_dmas:
        f_()
    dma("sp", bgate[:], dr["b_gate"].rearrange("(n p) -> p n", p=128), writes=[B_bgate], allow_slow_non_contiguous=True)
    dma("sp", bglu[:], dr["b_glu"].rearrange("(n p) -> p n", p=128), writes=[B_bglu], allow_slow_non_contiguous=True)
    for gqr in range(4):
        gs = slice(gqr * 8, gqr * 8 + 8)
        T1 = SC1[:].rearrange("p (g j h) -> p g j h", g=8, j=9)
        T2 = SC2[:].rearrange("p (g j h) -> p g j h", g=8, j=9)
        cre = CT[:, 0, gs, :].unsqueeze(2).broadcast_to([128, 8, 9, 16])
        cim = CT[:, 1, gs, :].unsqueeze(2).broadcast_to([128, 8, 9, 16])
        arb = AR1[:, :, gs].rearrange("p j g -> p g j").unsqueeze(3).broadcast_to([128, 8, 9, 16])
        aib = AI1[:, :, gs].rearrange("p j g -> p g j").unsqueeze(3).broadcast_to([128, 8, 9, 16])
        RD = [B_CT, B_A1, B_SC]
        tt("dve", T1, cre, arb, ALU.mult, reads=RD, writes=[B_SC])
        tt("dve", T2, cim, aib, ALU.mult, reads=RD, writes=[B_SC])
        tt("dve", CAre[:, gs], T1, T2, ALU.subtract, reads=RD, writes=[B_CA])
        tt("dve", T1, cre, aib, ALU.mult, reads=RD + [B_CA], writes=[B_SC])
        tt("dve", T2, cim, arb, ALU.mult, reads=RD, writes=[B_SC])
        stt(NCAim[:, gs], T1, -1.0, T2, ALU.mult, ALU.subtract, RD, [B_CA])
    maybe_stop("PRE2b")
    BAq = sb("BAq", [128, 2, 8, 8, 16], BF16)
    ba_cnt = [0]
    B_BAq = Buf("BAq")
    for gqr in range(4):
        gs = slice(gqr * 8, gqr * 8 + 8)
        T1 = SC1[:, 0:1024].rearrange("p (g j h) -> p g j h", g=8, j=8)
        T2 = SC2[:, 0:1024].rearrange("p (g j h) -> p g j h", g=8, j=8)
        bre = BB[:, 0, gs, :].unsqueeze(2).broadcast_to([128, 8, 8, 16])
        bim = BB[:, 1, gs, :].unsqueeze(2).broadcast_to([128, 8, 8, 16])
        arb = AR2[:, :, gs].rearrange("p j g -> p g j").unsqueeze(3).broadcast_to([128, 8, 8, 16])
        aib = AI2[:, :, gs].rearrange("p j g -> p g j").unsqueeze(3).broadcast_to([128, 8, 8, 16])
        RD = [B_BB, B_A2, B_SC]
        ba_ops = [
            lambda: tt("dve", T1, bre, arb, ALU.mult, reads=RD, writes=[B_SC]),
            lambda: tt("dve", T2, bim, aib, ALU.mult, reads=RD, writes=[B_SC]),
            lambda: tt("dve", BAq[:, 0], T1, T2, ALU.subtract, reads=RD, writes=[B_BAq]),
            lambda: tt("dve", T1, bre, aib, ALU.mult, reads=RD + [B_BAq], writes=[B_SC]),
            lambda: tt("dve", T2, bim, arb, ALU.mult, reads=RD, writes=[B_SC]),
            lambda: tt("dve", BAq[:, 1], T1, T2, ALU.add, reads=RD, writes=[B_BAq]),
        ]
        for f_ in ba_ops:
            if BAN is not None and ba_cnt[0] >= BAN:
                maybe_stop("PRE3x")
            f_()
            ba_cnt[0] += 1
        for g2 in range(0, 8, 2):
            if STOP == "PRE3x":
                continue
            pt, pb = ps_next()
            k = 0
            for gg in (g2, g2 + 1):
                for ri in range(2):
                    mm(pt[:, k * 128:(k + 1) * 128], BAq[:, ri, gg].rearrange("p s h -> p (s h)"), ident[:], True, True,
                       reads=[B_BAq, B_ID], writes=[pb] if k == 0 else [])
                    k += 1
            pdone(pb)
            g = gqr * 8 + g2
            cp("act", Bex[:, g:g + 2].rearrange("p g r q -> p (g r q)"), pt[:], reads=[pb], writes=[B_Bex])
    maybe_stop("PRE3")
    maybe_stop("PRE3x")
    hook_state["on"] = False
    for _ in genA:
        pass
    global_barrier()
    release_to(m_k)
    release_to(m0_left, side="left")
    UT = sb("UT", [128, 32, 256], BF16, side="left")
    B_UTg = [Buf("UT%d" % i) for i in range(16)]
    U = sb("U", [128, 2, 32, 8, 16], BF16)
    B_Us = [Buf("U%d" % i) for i in range(16)]
    for ct in range(2):
        for s in range(8):
            pt, pb = ps_next()
            c0 = 1024 * ct + s
            for kt in range(8):
                mm(pt[:], xnT[:, kt, c0:c0 + 1017:8], wU[:, kt, :], kt == 0, kt == 7,
                   reads=B_xnT + [B_wU], writes=[pb] if kt == 0 else [])
            pdone(pb)
            cp("act", U[:, ct, :, s, :], pt[:].rearrange("p (g h) -> p g h", g=32), reads=[pb], writes=[B_Us[ct * 8 + s]])
    if debug:
        dump("U", U[:].rearrange("p a g s c -> p (a g s c)"), B_Us[15], 128, 8192)
    for g0 in range(0, 32, 2):
        pt, pb = ps_next()
        k = 0
        for g in (g0, g0 + 1):
            for ct in range(2):
                mm(pt[:, k * 128:(k + 1) * 128], U[:, ct, g].rearrange("p s h -> p (s h)"), ident[:], True, True,
                   reads=B_Us + [B_ID], writes=[pb] if k == 0 else [])
                k += 1
        pdone(pb)
        cp("act", UT[:, g0:g0 + 2, :].rearrange("p g c -> p (g c)"), pt[:], reads=[pb], writes=[B_UTg[g0 // 2]])
    Kall = sb("Kall", [16, 32, 15, 16], BF16)
    B_Kall_l = [Buf("Kall%d" % i) for i in range(8)]
    dT = sb("dT", [16, 32], F32)
    B_dT = Buf("dT")
    dma("sp", dT[:], dr["ssm_d"].rearrange("(g h) -> h g", h=16), writes=[B_dT], allow_slow_non_contiguous=True)
    Dm = sb("Dm", [16, 32, 16], F32)
    B_Dm = Buf("Dm")
    tt("dve", Dm[:], identf[0:16, 0:16].unsqueeze(1).broadcast_to([16, 32, 16]), dT[:].unsqueeze(2).broadcast_to([16, 32, 16]),
       ALU.mult, reads=[B_IDF, B_dT], writes=[B_Dm])
    Ktmp = sb("Ktmp", [16, 4, 16], F32)
    B_Ktmp = Buf("Ktmp")
    BBz = sb("BBz", [128, 32, 2, 128], BF16)
    B_BBz = Buf("BBz")
    memset("pool", BBz[:], 0.0, writes=[B_BBz])
    for ri in range(2):
        cp("dve", BBz[0:64, :, ri, 0:16], BBb[0:64, ri, :, :], reads=[B_BBb, B_BBz], writes=[B_BBz])
        cp("dve", BBz[64:128, :, ri, 32:48], BBb[64:128, ri, :, :], reads=[B_BBb, B_BBz], writes=[B_BBz])
    CAK = sb("CAK", [128, 2, 32, 8, 16], BF16)
    B_CAK = Buf("CAK")
    for k_, src in enumerate((CAre, NCAim)):
        cp("dve", CAK[0:64, k_], src[0:64, :, 0:8, :], reads=[B_CA], writes=[B_CAK])
        cp("dve", CAK[64:128, k_], src[64:128, :, 1:9, :], reads=[B_CA, B_CAK], writes=[B_CAK])
        cp("act", CAY[0:64, k_], src[0:64, :, 1:9, :], reads=[B_CA], writes=[B_CAY])
        cp("act", CAY[64:128, k_], src[64:128, :, 0:8, :], reads=[B_CA, B_CAY], writes=[B_CAY])
    maybe_stop("PRE3b")
    for g0 in range(0, 32, 4):
        pt, pb = ps_next()
        for gl in range(4):
            g = g0 + gl
            o_ = pt[:, gl * 128:(gl + 1) * 128]
            mm(o_, BBz[:, g, 0, :], CAK[:, 0, g].rearrange("p j h -> p (j h)"), True, False,
               reads=[B_BBz, B_CAK], writes=[pb] if gl == 0 else [])
            mm(o_, BBz[:, g, 1, :], CAK[:, 1, g].rearrange("p j h -> p (j h)"), False, True, reads=[B_BBz, B_CAK])
        pdone(pb)
        pf = pt[0:16, :].rearrange("p (a i h) -> p a i h", a=4, i=8)
        pbw = pt[32:48, :].rearrange("p (a i h) -> p a i h", a=4, i=8)
        gsl = slice(g0, g0 + 4)
        B_Kall = B_Kall_l[g0 // 4]
        cp("act", Kall[:, gsl, 0:7, :], pbw[:, :, 0:7, :], reads=[pb], writes=[B_Kall])
        cp("act", Kall[:, gsl, 8:15, :], pf[:, :, 1:8, :], reads=[pb], writes=[B_Kall])
        tt("dve", Ktmp[:], pf[:, :, 0, :], Dm[:, gsl, :], ALU.add, reads=[pb, B_Dm, B_Ktmp], writes=[B_Ktmp])
        tt("dve", Kall[:, gsl, 7, :], pbw[:, :, 7, :], Ktmp[:], ALU.add, reads=[pb, B_Ktmp], writes=[B_Kall])
        if g0 in (12, 28):
            hsl = slice(g0 - 12, g0 + 4)
            for s_ in range(8):
                dma("sp", Wt[s_ * 16:(s_ + 1) * 16, hsl, :].rearrange("p g (t h) -> p g t h", t=8),
                    Kall[:, hsl, 7 - s_:15 - s_, :], reads=B_Kall_l[(g0 - 12) // 4:(g0 + 4) // 4], writes=[B_W])
    maybe_stop("PRE4")
    if debug:
        dump("Wt", Wt[:].rearrange("p g c -> p (g c)"), B_W, 128, 4096)
        dump("CAre", CAre[:].rearrange("p g j h -> p (g j h)"), B_CA, 128, 4608)
        dump("Bex", Bex[:].rearrange("p g r q -> p (g r q)"), B_Bex, 128, 8192)
        dump("A8", A8[:].rearrange("p g r -> p (g r)"), B_A8, 128, 64)
    global_barrier()
    release_to(m_pre)
    maybe_stop("PRE")

    UY = sb("UY", [128, 8192], BF16)
    B_UY = Buf("UY")
    Ysb = UY[:].rearrange("p (a t c) -> p a t c", a=2, t=8)
    SX = sb("SX", [128, 32, 2, 258], BF16)
    B_SXf = [Buf("SXf%d" % i) for i in range(64)]
    B_SXb = [Buf("SXb%d" % i) for i in range(64)]
    B_SXz = Buf("SXz")
    B_SXall = B_SXf + B_SXb + [B_SXz]
    B_Sg = [Buf("Sg%d" % i) for i in range(32)]
    memset("dve", SX[:, :, :, 0:2], 0.0, writes=B_SXall)
    memset("dve", SX[:, :, :, 256:258], 0.0, writes=B_SXall)
    m_merged = mark()
    WA = ARENA[:, 0:5120].rearrange("p (k b n) -> p k b n", k=8, b=5)
    B_WA = Buf("WA")

    def load_WA(hp_):
        for blk, c0 in enumerate((1024 + hp_ * 128, 1536 + hp_ * 128, 2048 + hp_ * 128, 2560 + hp_ * 128, 3072 + hp_ * 128)):
            wload(WA[:, :, blk, :], dr["w_in"][:, c0:c0 + 128].rearrange("(kt p) n -> p kt n", p=128), B_WA, perm=True)

    load_WA(0)
    KTz = sb("KTz", [128, 2, L], BF16)
    B_KTz = Buf("KTz")
    memset("pool", KTz[:], 0.0, writes=[B_KTz])
    for g in range(32):
        pt, pb = ps_next()
        for ri in range(2):
            mm(pt[:, ri * 256:(ri + 1) * 256], Bex[:, g, ri, :], UT[:, g, :], True, True,
               reads=[B_Bex, B_UTg[g // 2]], writes=[pb] if ri == 0 else [])
        pdone(pb)
        cp("act", SX[0:64, g, :, 2:258], pt[0:64, :].rearrange("p (r c) -> p r c", r=2), reads=[pb, B_SXz], writes=[B_Sg[g]])
        cp("dve", SX[64:128, g, :, 0:256], pt[64:128, :].rearrange("p (r c) -> p r c", r=2), reads=[pb, B_SXz], writes=[B_Sg[g]])
    if debug:
        dump("S", SX[:].rearrange("p g r c -> p (g r c)"), B_SXz, 128, 32 * 2 * 258)
    for ct in range(2):
        for gb in range(8):
            pt, pb = ps_next()
            for gl in range(4):
                g = gb * 4 + gl
                mm(pt[:, gl * 128:(gl + 1) * 128], UT[:, g, 128 * ct:128 * ct + 128], Wt[:, g, :], True, True,
                   reads=[B_UTg[g // 2], B_W], writes=[pb] if gl == 0 else [])
            pdone(pb)
            cp("act", Ysb[:, ct, :, gb * 64:(gb + 1) * 64].rearrange("p t (g h) -> p g t h", g=4),
               pt[:].rearrange("p (g t h) -> p g t h", g=4, t=8), reads=[pb], writes=[B_UY])
    global_barrier()
    release_to(m_left_ssm, side="left")
    NSB, NSD = 8, 3
    STG = sb("STG", [128, NSD, NSB, 32, 2, 3], F32)
    B_stgA = [Buf("stgA%d" % i) for i in range(NSD)]
    B_stgS = [Buf("stgS%d" % i) for i in range(NSD)]
    NEWR = sb("NEWR", [128, NSD, NSB, 32, 2], F32)
    B_new = [Buf("new%d" % i) for i in range(NSD)]
    zst = sb("zst", [128, 32, 2], F32)
    B_zst = Buf("zst")
    memset("dve", zst[:], 0.0, writes=[B_zst])
    M8 = sb("M8", [128, 32, 2, 2], F32)
    B_M8 = Buf("M8")
    cp("dve", M8[:, :, 0, 0:1], A8[:, :, 0:1], reads=[B_A8], writes=[B_M8])
    cp("dve", M8[:, :, 1, 1:2], A8[:, :, 0:1], reads=[B_A8, B_M8], writes=[B_M8])
    cp("dve", M8[:, :, 0, 1:2], A8s[:, :, 0:1], reads=[B_A8, B_M8], writes=[B_M8])
    cp("dve", M8[:, :, 1, 0:1], A8s[:, :, 1:2], reads=[B_A8, B_M8], writes=[B_M8])
    NBT = 256 // NSB

    def bulk_s(bt):
        rb_, c0_ = bt % NSD, NSB * bt
        cp("act", STG[0:64, rb_, :, :, :, 2], SX[0:64, :, :, 2 + c0_:2 + c0_ + NSB].rearrange("p g r s -> p s g r"),
           reads=[B_SXf[bt]] + B_Sg, writes=[B_stgS[rb_]])
        hi_ = 255 - c0_
        lo_ = hi_ - NSB
        cols_ = slice(hi_, lo_, -1) if lo_ >= 0 else slice(hi_, None, -1)
        cp("act", STG[64:128, rb_, :, :, :, 2], SX[64:128, :, :, cols_].rearrange("p g r s -> p s g r"),
           reads=[B_SXb[NBT - 1 - bt]] + B_Sg, writes=[B_stgS[rb_]])

    def conv_s(bt):
        rb_, c0_ = bt % NSD, NSB * bt
        cp("act", SX[0:64, :, :, 2 + c0_:2 + c0_ + NSB], NEWR[0:64, rb_].rearrange("p s g r -> p g r s"),
           reads=[B_new[rb_]], writes=[B_SXf[bt]])
        cp("act", SX[64:128, :, :, 256 - NSB - c0_:256 - c0_], NEWR[64:128, rb_, ::-1].rearrange("p s g r -> p g r s"),
           reads=[B_new[rb_]], writes=[B_SXb[NBT - 1 - bt]])

    bulk_s(0)
    bulk_s(1)

    def scan_gen():
        prev, bprev = zst[:], B_zst
        for bt in range(NBT):
            rb = bt % NSD
            if bt + 2 < NBT:
                bulk_s(bt + 2)
            for sl in range(NSB):
                tt("dve", STG[:, rb, sl, :, :, 0:2], M8[:], prev.unsqueeze(2).broadcast_to([128, 32, 2, 2]), ALU.mult,
                   reads=[B_M8, bprev], writes=[B_stgA[rb]])
                yield
                new = NEWR[:, rb, sl]
                P.op("dve", (lambda new=new, rb=rb, sl=sl: nc.vector.tensor_reduce(
                    out=new, in_=STG[:, rb, sl], op=ALU.add, axis=AX.X)),
                    [B_stgA[rb], B_stgS[rb]], [B_new[rb]])
                prev, bprev = new, B_new[rb]
                if sl == NSB - 1 and bt >= 1:
                    conv_s(bt - 1)
                yield
        conv_s(NBT - 1)
        yield

    scan_it = scan_gen()

    def adv_scan(n=1):
        for _ in range(n):
            next(scan_it, None)

    m_att = mark()
    QKT = sb("QKT", [128, 3, L], BF16)
    B_QKT = Buf("QKT")
    VT = sb("VT", [128, L], BF16)
    B_VT = Buf("VT")
    VA = sb("VA", [128, 3, 16, 192], BF16)
    B_VA = Buf("VA")
    memset("dve", VA[:, :, :, 64:128], 1.0, writes=[B_VA])
    sq_l = [sb("sq%d" % i, [128, 8, 64], F32) for i in range(3)]
    B_sq_l = [Buf("sq%d" % i) for i in range(3)]
    qn_l = [sb("qn%d" % i, [128, 8, 64], BF16) for i in range(3)]
    wqkb = sb("wqkb", [128, 8, 64], BF16)
    B_wqkb = Buf("wqkb")
    cp("act", wqkb[:], wqk[:], reads=[B_wqk], writes=[B_wqkb])
    B_qn_l = [Buf("qn%d" % i) for i in range(3)]
    qbf_l = [ARENA[:, 5120 + 512 * i:5120 + 512 * (i + 1)].rearrange("p (b d) -> p b d", b=8) for i in range(4)]
    B_qbf_l = [Buf("qbf%d" % i) for i in range(4)]
    nst_l = [sb("nst%d" % i, [128, 4, 8], F32) for i in range(4)]
    B_nst_l = [Buf("nst%d" % i) for i in range(4)]
    rp_l = [ARENA[:, 7168 + 512 * i:7168 + 512 * (i + 1)].bitcast(F32).rearrange("p (a b d) -> p a b d", a=2, b=8) for i in range(2)]
    B_rp_l = [Buf("rp%d" % i) for i in range(2)]
    NPT = 3
    PT = [sb("PT%d" % i, [128, 512], BF16) for i in range(NPT)]
    B_PT = [Buf("PT%d" % i) for i in range(NPT)]
    acc_rr = [0]
    ps6_rr = [0]

    psq_rr = [0]
    pso_rr = [0]

    def ps_q():
        i_ = psq_rr[0]
        psq_rr[0] = (i_ + 1) % 4
        return psum[i_]

    def ps_o():
        i_ = pso_rr[0]
        pso_rr[0] = (i_ + 1) % 4
        return psum[4 + i_]

    def ps_next6():
        i_ = ps6_rr[0]
        ps6_rr[0] = (i_ + 1) % 6
        return psum[i_]

    rden = sb("rden", [128, 512], F32)
    B_rden_h = [Buf("rden0"), Buf("rden1")]
    pt_rr = [0]
    for hp in range(4):
        qst = {}

        def stA(i):
            pq, pqb = ps_q()
            for kt in range(8):
                mm(pq[:], xnT[:, kt, 128 * i:128 * (i + 1)], WA[:, kt, 0:4, :], kt == 0, kt == 7,
                   reads=[B_xnT[i], B_WA], writes=[pqb] if kt == 0 else [])
            pdone(pqb)
            qst[i] = (pq, pqb)

        NSQ, NNS, NQN, NQF, NRP = 3, 4, 3, 4, 2

        def stB1(i):
            pq, pqb = qst[i]
            k_ = hp * 16 + i
            act(sq_l[k_ % NSQ][:], pq[:].rearrange("p (b d) -> p b d", b=8), AF.Square, reads=[pqb], writes=[B_sq_l[k_ % NSQ]])

        def dve_ops_B2(i):
            k_ = hp * 16 + i
            sq, B_sq, nst, B_nst = sq_l[k_ % NSQ], B_sq_l[k_ % NSQ], nst_l[k_ % NNS], B_nst_l[k_ % NNS]
            return [
                lambda: P.op("dve", (lambda nst=nst, sq=sq: nc.vector.tensor_reduce(out=nst[:, 0, :], in_=sq[:], op=ALU.add, axis=AX.X)),
                             [B_sq], [B_nst]),
            ]

        def stB3(i):
            k_ = hp * 16 + i
            nst, B_nst = nst_l[k_ % NNS], B_nst_l[k_ % NNS]
            act(nst[:, 2, :], nst[:, 0, :], AF.Sqrt, reads=[B_nst], writes=[B_nst], scale=1.0 / 64, bias=EPS)

        def dve_ops_C1(i):
            pq, pqb = qst[i]
            pq3 = pq[:].rearrange("p (b d) -> p b d", b=8)
            k_ = hp * 16 + i
            qn, B_qn = qn_l[k_ % NQN], B_qn_l[k_ % NQN]
            nst, B_nst = nst_l[k_ % NNS], B_nst_l[k_ % NNS]
            qbf, B_qbf = qbf_l[k_ % NQF], B_qbf_l[k_ % NQF]
            rp, B_rp = rp_l[k_ % NRP], B_rp_l[k_ % NRP]
            cc = ropec[:, i, :].unsqueeze(1).broadcast_to([128, 8, 16])
            return [
                lambda: recip(nst[:, 3, :], nst[:, 2, :], [B_nst], [B_nst]),
                lambda: tt("dve", qn[:], pq3, nst[:, 3, :].unsqueeze(2).broadcast_to([128, 8, 64]), ALU.mult,
                           reads=[pqb, B_nst], writes=[B_qn]),
                lambda: tt("dve", qbf[:], qn[:], wqkb[:], ALU.mult, reads=[B_qn, B_wqkb], writes=[B_qbf]),
                lambda: tt("dve", rp[:, 0], qbf[:, :, 0:16], cc, ALU.mult, reads=[B_qbf, B_ropec], writes=[B_rp]),
                lambda: tt("dve", rp[:, 1].rearrange("p b (h d) -> p b h d", h=2),
                           qbf[:, :, 0:16].rearrange("p b (h d) -> p b h d", h=2)[:, :, ::-1, :],
                           ropes[:, i, :].rearrange("p (h d) -> p h d", h=2).unsqueeze(1).broadcast_to([128, 8, 2, 8]),
                           ALU.mult, reads=[B_qbf, B_ropes, B_rp], writes=[B_rp]),
                lambda: tt("dve", qbf[:, :, 0:16], rp[:, 0], rp[:, 1], ALU.add, reads=[B_rp, B_qbf], writes=[B_qbf]),
            ]

        def dve_step(i2, i3):
            b2 = dve_ops_B2(i2) if 0 <= i2 < 16 else []
            c1 = dve_ops_C1(i3) if 0 <= i3 < 16 else []
            sc = [lambda: adv_scan(1), lambda: adv_scan(1)] if c1 else []
            order = []
            seq = [(c1, 0), (b2, 0), (c1, 1), (sc, 0), (c1, 2), (sc, 1), (c1, 3), (c1, 4), (c1, 5)]
            for lst, k in seq:
                if k < len(lst):
                    lst[k]()

        def stC3(i):
            k_ = hp * 16 + i
            qn, B_qn, qbf, B_qbf = qn_l[k_ % NQN], B_qn_l[k_ % NQN], qbf_l[k_ % NQF], B_qbf_l[k_ % NQF]
            cp("act", qbf[:, :, 16:64], qn[:, :, 16:64], reads=[B_qn], writes=[B_qbf])

        def stD1(i):
            k_ = hp * 16 + i
            qbf, B_qbf = qbf_l[k_ % NQF], B_qbf_l[k_ % NQF]
            ptt, ptb = ps_o()
            for blk in range(4):
                mm(ptt[:, blk * 128:(blk + 1) * 128], qbf[:, 2 * blk:2 * blk + 2, :].rearrange("p b d -> p (b d)"), ident[:],
                   True, True, reads=[B_qbf, B_ID], writes=[ptb] if blk == 0 else [])
            pdone(ptb)
            qst[("t", i)] = (ptt, ptb)

        def stD2(i):
            ptt, ptb = qst[("t", i)]
            cp("act", QKT[:, :, 128 * i:128 * (i + 1)], ptt[:, 0:384].rearrange("p (b n) -> p b n", b=3), reads=[ptb], writes=[B_QKT])
            cp("act", KTz[0:64, 0, 128 * i:128 * (i + 1)], ptt[0:64, 384:512], reads=[ptb], writes=[B_KTz])
            cp("act", KTz[64:128, 1, 128 * i:128 * (i + 1)], ptt[64:128, 384:512], reads=[ptb], writes=[B_KTz])

        def key_tokens(o, j):
            if o == 0:
                return 128 * j, 1
            if o == 1:
                r4, j4 = j // 4, j % 4
                return 512 * j4 + r4, 4
            return j, 16

        vst = {}

        def v_proj(tb):
            pv_, pvb = ps_o()
            for kt in range(8):
                mm(pv_[:], WA[:, kt, 4, :], xnT[:, kt, tb * 512:(tb + 1) * 512], kt == 0, kt == 7,
                   reads=[B_WA] + B_xnT, writes=[pvb] if kt == 0 else [])
            pdone(pvb)
            vst[("p", tb)] = (pv_, pvb)

        def v_proj_ev(tb):
            pv_, pvb = vst[("p", tb)]
            cp("act", VT[:, tb * 512:(tb + 1) * 512], pv_[:], reads=[pvb], writes=[B_VT])

        def v_tr(k):
            o, j0 = k // 4, 4 * (k % 4)
            pv_, pvb = ps_o()
            for jj in range(4):
                st, step = key_tokens(o, j0 + jj)
                mm(pv_[:, jj * 128:(jj + 1) * 128], VT[:, st:st + 127 * step + 1:step], ident[:], True, True,
                   reads=[B_VT, B_ID], writes=[pvb] if jj == 0 else [])
            pdone(pvb)
            vst[("t", k)] = (pv_, pvb)

        def v_tr_ev(k):
            o, j0 = k // 4, 4 * (k % 4)
            pv_, pvb = vst[("t", k)]
            cp("act", VA[:, o, j0:j0 + 4, :].rearrange("p j (h x) -> p j h x", h=3)[:, :, 0:3:2, :],
               pv_[:].rearrange("p (j h d) -> p j h d", j=4, h=2), reads=[pvb], writes=[B_VA])

        for t_ in range(16 + 5):
            for (f_, lag_) in ((stA, 0), (stB1, 1), (stD2, 5)):
                if 0 <= t_ - lag_ < 16:
                    f_(t_ - lag_)
            b2_i, c1_i = t_ - 2, t_ - 3
            if 0 <= b2_i < 16 and not (0 <= c1_i < 16):
                for f_ in dve_ops_B2(b2_i):
                    f_()
            else:
                dve_step(b2_i, c1_i)
            for (f_, lag_) in ((stB3, 2), (stD1, 4)):
                if 0 <= t_ - lag_ < 16:
                    f_(t_ - lag_)
            if 1 <= t_ <= 4:
                v_proj(t_ - 1)
            if 2 <= t_ <= 5:
                v_proj_ev(t_ - 2)
            if 7 <= t_ <= 18:
                v_tr(t_ - 7)
            if 8 <= t_ <= 19:
                v_tr_ev(t_ - 8)
            if t_ == 17 and hp + 1 < 4:
                load_WA(hp + 1)
            if t_ == 17 and hp == 3:
                dma("pool", ARENA[:, 0:4096].rearrange("p (k n) -> p k n", k=4),
                    dr["w_glu"].rearrange("(kt p) n -> p kt n", p=128), writes=[B_WA, B_wglu]).perm = True
            if t_ in (5, 6, 17, 18, 19, 20):
                adv_scan(2)
        if hp == 3:
            dma("pool", ARENA[:, 4096:8192].rearrange("p (k n) -> p k n", k=8),
                dr["w_in"][:, 512:1024].rearrange("(kt p) n -> p kt n", p=128), writes=[B_WA, B_wza] + B_qbf_l + B_rp_l).perm = True
        glist = []
        for hh in range(2):
            for qb in range(4):
                contribs = []
                for j in range(4 * qb - 1, 4 * qb + 5):
                    if j < 0 or j > 15:
                        continue
                    lo = max(128 * j - 64, 512 * qb, 0)
                    hi = min(128 * j + 192, 512 * qb + 512, L)
                    n = hi - lo
                    m0_ = lo - (128 * j - 64)
                    contribs.append((0, j, lo, 1, n, mb01[:, m0_:m0_ + n], lo - 512 * qb, 1))
                for r4 in range(4):
                    for j4 in (qb - 1, qb, qb + 1):
                        if j4 < 0 or j4 > 3:
                            continue
                        if j4 == qb:
                            mlo, n, mc = 128 * qb, 128, 64
                        elif j4 == qb - 1:
                            mlo, n, mc = 128 * qb, 64, 192
                        else:
                            mlo, n, mc = 128 * qb + 64, 64, 0
                        qs = 4 * mlo + r4
                        contribs.append((1, r4 * 4 + j4, qs, 4, n, mb01[:, mc:mc + n], qs - 512 * qb, 4))
                for r16 in range(16):
                    qs = 512 * qb + r16
                    contribs.append((2, r16, qs, 16, 32, mb2[:, 32 * qb:32 * qb + 32], r16, 16))
                groups, cur, used = [], [], 0
                for cbt in contribs:
                    if used + cbt[4] > 512:
                        groups.append(cur)
                        cur, used = [], 0
                    cur.append((cbt, used))
                    used += cbt[4]
                if cur:
                    groups.append(cur)
                for gi_, grp in enumerate(groups):
                    glist.append((hh, qb, grp, gi_ == 0, gi_ == len(groups) - 1))
        gst = {}
        accst = {}

        def stS(k):
            hh, qb, grp, gfirst, glast = glist[k]
            ps_, psb = ps_next6()
            firstg = True
            tot = 0
            for (cbt, off) in grp:
                o, j, qs, qstep, n, mask_ap, a0, astep = cbt
                st, step = key_tokens(o, j)
                mm(ps_[:, off:off + n], KTz[:, hh, st:st + 127 * step + 1:step],
                   QKT[:, o, qs:qs + (n - 1) * qstep + 1:qstep], True, False,
                   reads=[B_QKT, B_KTz], writes=[psb] if firstg else [])
                firstg = False
                mm(ps_[:, off:off + n], ident[:], mask_ap, False, True, reads=[B_ID, B_mb01, B_mb2])
                tot = off + n
            pdone(psb)
            pi = pt_rr[0]
            pt_rr[0] = (pi + 1) % NPT
            act(PT[pi][:, 0:tot], ps_[:, 0:tot], AF.Exp, reads=[psb], writes=[B_PT[pi]], scale=0.125)
            gst[k] = pi

        def stPV(k):
            adv_scan(3 if k % 3 == 0 else 2)
            hh, qb, grp, gfirst, glast = glist[k]
            rows = slice(hh * 64, hh * 64 + 64)
            pi = gst[k]
            if gfirst:
                ai = acc_rr[0]
                acc_rr[0] ^= 1
                accst[(hh, qb)] = psum[6 + ai]
            acc, accb = accst[(hh, qb)]
            for gi2, (cbt, off) in enumerate(grp):
                o, j, qs, qstep, n, mask_ap, a0, astep = cbt
                fst = gfirst and gi2 == 0
                lst = glast and gi2 == len(grp) - 1
                mm(acc[:, a0:a0 + (n - 1) * astep + 1:astep], VA[:, o, j, 64 * hh:64 * hh + 128], PT[pi][:, off:off + n],
                   fst, lst, reads=[B_VA, B_PT[pi]], writes=[accb] if fst else [])
            if glast:
                pdone(accb)
                B_rden = B_rden_h[hh]
                den_rows = slice(64, 128) if hh == 0 else slice(0, 64)
                act(rden[rows, :], acc[den_rows, :], AF.Ln, reads=[accb], writes=[B_rden])
                act(rden[rows, :], rden[rows, :], AF.Exp, reads=[B_rden], writes=[B_rden], scale=-1.0)
                tt("dve", HB[rows, hp, qb * 512:(qb + 1) * 512], acc[rows, :], rden[rows, :], ALU.mult,
                   reads=[accb, B_rden], writes=[B_HB])

        LAG = 2
        for t_ in range(len(glist) + LAG):
            if t_ < len(glist):
                stS(t_)
            if t_ - LAG >= 0:
                stPV(t_ - LAG)
    if debug:
        dump("attnT", HB[:].rearrange("p a t -> p (a t)"), B_HB, 128, 4 * L)
    for _ in scan_it:
        pass
    if debug:
        dump("X", SX[:].rearrange("p g r c -> p (g r c)"), B_SXz, 128, 32 * 2 * 258)
    global_barrier()
    release_to(m_merged)
    YH = sb("YH", [128, 4, L], BF16, side="left")
    wglu = ARENA[:, 0:4096].rearrange("p (k n) -> p k n", k=4)
    wza = ARENA[:, 4096:8192].rearrange("p (k n) -> p k n", k=8)
    wzb = sb("wzb", [128, 8, 512], BF16, side="left")
    B_wzb = Buf("wzb")
    wpa = sb("wpa", [128, 4, 1024], BF16, side="left")
    B_wpa = Buf("wpa")
    wps = sb("wps", [128, 4, 1024], BF16, side="left")
    B_wps = Buf("wps")
    wg = sb("wg", [128, 8, 2048], BF16, side="left")
    B_wg = Buf("wg")
    w_in_cols(wzb[:], 3584, 512, B_wzb, perm=True)
    wload(wps[:], dr["w_proj_ssm"].rearrange("(kt p) n -> p kt n", p=128), B_wps, perm=True)
    wload(wpa[:], dr["w_proj_attn"].rearrange("(kt p) n -> p kt n", p=128), B_wpa, perm=True)
    w_in_cols(wg[:, :, 0:1024], 4096, 1024, B_wg, perm=True)
    w_in_cols(wg[:, :, 1024:2048], 5120, 1024, B_wg, perm=True)
    hat = sb("hat", [128, 4, 512], BF16)
    B_hat = Buf("hat")
    sgt = sb("sgt", [128, 512], F32)
    B_sgt = Buf("sgt")
    szt = sb("szt", [128, 512], F32)
    B_szt = Buf("szt")
    h1t = sb("h1t", [128, 512], F32)
    B_h1t = Buf("h1t")

    def y_inter(ct):
        for gb in range(8):
            pt, pb = ps_next()
            for gl in range(4):
                g = gb * 4 + gl
                o_ = pt[:, gl * 128:(gl + 1) * 128]
                mm(o_, SX[:, g, 0, 128 * ct + 1:128 * ct + 129], CAY[:, 0, g].rearrange("p j h -> p (j h)"), True, False,
                   reads=[B_CAY] + B_SXall + B_Sg, writes=[pb] if gl == 0 else [])
                mm(o_, SX[:, g, 1, 128 * ct + 1:128 * ct + 129], CAY[:, 1, g].rearrange("p j h -> p (j h)"), False, True)
            pdone(pb)
            yv = Ysb[:, ct, :, gb * 64:(gb + 1) * 64].rearrange("p t (g h) -> p g t h", g=4)
            tt("dve", yv, pt[:].rearrange("p (g t h) -> p g t h", g=4, t=8), yv, ALU.add, reads=[pb, B_UYc[ct]], writes=[B_UYc[ct]])

    def y_tr(tb_):
        ct, ch_ = tb_ // 2, tb_ % 2
        for cht in range(4):
            pt, pb = ps_next()
            for t in range(8):
                mm(pt[:, t:t + 505:8], Ysb[:, ct, t, cht * 128:(cht + 1) * 128], ident[:, 64 * ch_:64 * ch_ + 64], True, True,
                   reads=[B_UYc[ct], B_ID], writes=[pb] if t == 0 else [])
            pdone(pb)
            act(YH[:, cht, 512 * tb_:512 * (tb_ + 1)], pt[:], GELU, reads=[pb], writes=[B_YH[tb_]])

    def glu_blk(tb):
        tsl = slice(tb * 512, (tb + 1) * 512)
        for v in range(4):
            pv_, pvb = ps_next()
            for cht in range(4):
                mm(pv_[:], wglu[:, cht, v * 128:(v + 1) * 128], YH[:, cht, tsl], cht == 0, cht == 3,
                   reads=[B_wglu, B_YH[tb]], writes=[pvb] if cht == 0 else [])
            pdone(pvb)
            pg_, pgb = ps_next()
            for cht in range(4):
                mm(pg_[:], wglu[:, cht, 512 + v * 128:512 + (v + 1) * 128], YH[:, cht, tsl], cht == 0, cht == 3,
                   reads=[B_wglu, B_YH[tb]], writes=[pgb] if cht == 0 else [])
            pdone(pgb)
            pz_, pzb = ps_next()
            for kt in range(8):
                mm(pz_[:], wza[:, kt, v * 128:(v + 1) * 128], xnT[:, kt, tsl], kt == 0, kt == 7,
                   reads=[B_wza] + B_xnT, writes=[pzb] if kt == 0 else [])
            pdone(pzb)
            act(sgt[:], pg_[:], AF.Sigmoid, reads=[pgb, B_bglu], writes=[B_sgt], bias=bglu[:, 4 + v:5 + v])
            act(szt[:], pz_[:], AF.Sigmoid, reads=[pzb], writes=[B_szt])
            tt("dve", szt[:], szt[:], pz_[:], ALU.mult, reads=[pzb, B_szt], writes=[B_szt])
            stt(h1t[:], pv_[:], bglu[:, v:v + 1], sgt[:], ALU.add, ALU.mult, [pvb, B_bglu, B_sgt], [B_h1t])
            tt("dve", hat[:, v, :], h1t[:], szt[:], ALU.mult, reads=[B_h1t, B_szt], writes=[B_hat])
        cp("act", YH[:, :, tsl], hat[:], reads=[B_hat], writes=[B_YH[tb]])

    B_UYc = [Buf("UYc0"), Buf("UYc1")]
    for b_ in B_UYc:
        b_.ws = list(B_UY.ws)
    y_inter(0)
    y_tr(0)
    y_tr(1)
    y_inter(1)
    glu_blk(0)
    y_tr(2)
    glu_blk(1)
    y_tr(3)
    glu_blk(2)
    glu_blk(3)
    if debug:
        dump("haT", YH[:].rearrange("p a t -> p (a t)"), B_YH[3], 128, 4 * L)
    global_barrier()
    release_to(m_ssm)
    maybe_stop("SSM")
    wo = sb("wo", [128, 8, 1024], BF16, side="left")
    B_wo = Buf("wo")
    wload(wo[:], dr["w_out"].rearrange("(kt p) n -> p kt n", p=128), B_wo, perm=True)

    maybe_stop("ATT")

    mixT_l = [sb("mixT%d" % i, [128, 8, 512], BF16) for i in range(2)]
    B_mix_l = [Buf("mixT%d" % i) for i in range(2)]
    sga_l = [sb("sga%d" % i, [128, 512], F32) for i in range(2)]
    B_sga_l = [Buf("sga%d" % i) for i in range(2)]
    sgb_l = [sb("sgb%d" % i, [128, 512], F32) for i in range(2)]
    B_sgb_l = [Buf("sgb%d" % i) for i in range(2)]
    m1_l = [sb("m1%d" % i, [128, 512], F32) for i in range(2)]
    B_m1_l = [Buf("m1%d" % i) for i in range(2)]
    szb_l = [sb("szb%d" % i, [128, 512], BF16) for i in range(2)]
    B_szb_l = [Buf("szb%d" % i) for i in range(2)]
    xres = [sb("xres%d" % i, [128, D], F32) for i in range(2)]
    B_xres = [Buf("xres%d" % i) for i in range(2)]
    B_out = Buf("out")
    out_ops = []
    pending_wout = []
    def emit_wout(tb, mixT, B_mix):
        for il in range(4):
            i = tb * 4 + il
            xr, bxr = xres[i % 2], B_xres[i % 2]
            dma("sp", xr[:], dr["x"][128 * i:128 * (i + 1), :], writes=[bxr])
            for half in range(2):
                po, pob = ps_next()
                for n in range(8):
                    mm(po[:], mixT[:, n, il * 128:(il + 1) * 128], wo[:, n, half * 512:(half + 1) * 512], n == 0, n == 7,
                       reads=[B_mix, B_wo], writes=[pob] if n == 0 else [])
                pdone(pob)
                tt("dve", xr[:, half * 512:(half + 1) * 512], xr[:, half * 512:(half + 1) * 512], po[:], ALU.add,
                   reads=[bxr, pob], writes=[bxr])
            out_ops.append(dma("sp", out_d[128 * i:128 * (i + 1), :], xr[:], reads=[bxr], writes=[]))

    for tb in range(4):
        tsl = slice(tb * 512, (tb + 1) * 512)
        mixT, B_mix = mixT_l[tb % 2], B_mix_l[tb % 2]
        for v in range(4):
            pz_, pzb = ps_next()
            for kt in range(8):
                mm(pz_[:], wzb[:, kt, v * 128:(v + 1) * 128], xnT[:, kt, tsl], kt == 0, kt == 7,
                   reads=[B_wzb] + B_xnT, writes=[pzb] if kt == 0 else [])
            pdone(pzb)
            szb, B_szb = szb_l[v % 2], B_szb_l[v % 2]
            act(szb[:], pz_[:], AF.Silu, reads=[pzb], writes=[B_szb])
            tt("dve", HB[:, v, tsl], HB[:, v, tsl], szb[:], ALU.mult, reads=[B_HB, B_szb], writes=[B_HB])
        while pending_wout:
            emit_wout(*pending_wout.pop(0))
        for n in range(8):
            pya, pyab = ps_next()
            for v in range(4):
                mm(pya[:], wps[:, v, n * 128:(n + 1) * 128], YH[:, v, tsl], v == 0, v == 3,
                   reads=[B_wps, B_YH[tb]], writes=[pyab] if v == 0 else [])
            pdone(pyab)
            pyb_, pybb = ps_next()
            for v in range(4):
                mm(pyb_[:], wpa[:, v, n * 128:(n + 1) * 128], HB[:, v, tsl], v == 0, v == 3,
                   reads=[B_wpa, B_HB], writes=[pybb] if v == 0 else [])
            pdone(pybb)
            pga, pgab = ps_next()
            for kt in range(8):
                mm(pga[:], wg[:, kt, n * 128:(n + 1) * 128], xnT[:, kt, tsl], kt == 0, kt == 7,
                   reads=[B_wg] + B_xnT, writes=[pgab] if kt == 0 else [])
            pdone(pgab)
            pgb_, pgbb = ps_next()
            for kt in range(8):
                mm(pgb_[:], wg[:, kt, 1024 + n * 128:1024 + (n + 1) * 128], xnT[:, kt, tsl], kt == 0, kt == 7,
                   reads=[B_wg] + B_xnT, writes=[pgbb] if kt == 0 else [])
            pdone(pgbb)
            sga, B_sga, sgb, B_sgb, m1, B_m1 = sga_l[n % 2], B_sga_l[n % 2], sgb_l[n % 2], B_sgb_l[n % 2], m1_l[n % 2], B_m1_l[n % 2]
            act(sga[:], pga[:], AF.Sigmoid, reads=[pgab, B_bgate], writes=[B_sga], bias=bgate[:, n:n + 1])
            act(sgb[:], pgb_[:], AF.Sigmoid, reads=[pgbb, B_bgate], writes=[B_sgb], bias=bgate[:, 8 + n:9 + n])
            tt("dve", m1[:], pya[:], sga[:], ALU.mult, reads=[pyab, B_sga], writes=[B_m1])
            tt("dve", sgb[:], pyb_[:], sgb[:], ALU.mult, reads=[pybb, B_sgb], writes=[B_sgb])
            tt("dve", mixT[:, n, :], m1[:], sgb[:], ALU.add, reads=[B_m1, B_sgb], writes=[B_mix])
        pending_wout.append((tb, mixT, B_mix))
    while pending_wout:
        emit_wout(*pending_wout.pop(0))
    fin = P.op("sp", lambda: nc.sync.nop(), dbg_bufs, [])
    for o in out_ops:
        fin.deps.add(o)
    P.emit()
    return nc


_CACHE = {}


def kernel(**inputs):
    consts = host_consts()
    if "nc" not in _CACHE:
        try:
            build(DEBUG if DEBUG else None, STOP)
        except StopBuild:
            pass
    nc = _CACHE["nc"]
    shared = {}
    for name, shape in IN_SPECS:
        if name == "x":
            continue
        a = np.asarray(inputs[name], dtype=np.float32)
        shared[name] = np.ascontiguousarray(a.reshape(shape))
    shared.update(consts)
    x = np.asarray(inputs["x"], dtype=np.float32)
    in_maps = []
    for b in range(NCORES):
        m = dict(shared)
        m["x"] = np.ascontiguousarray(x[b])
        in_maps.append(m)
    res = run_bass_kernel_spmd(nc, in_maps, core_ids=list(range(NCORES)))
    _CACHE["res"] = res
    out = np.stack([np.asarray(res.results[b]["out"], dtype=np.float32) for b in range(NCORES)], axis=0)
    return out
```

```python
import math
import numpy as np
import ml_dtypes
import concourse.bass as bass
import concourse.mybir as mybir
from concourse.bass_utils import run_bass_kernel_spmd

F32 = mybir.dt.float32
BF16 = mybir.dt.bfloat16
I32 = mybir.dt.int32
AF = mybir.ActivationFunctionType
ALU = mybir.AluOpType
AX = mybir.AxisListType

L = 2048
D = 1024
EPS = 1e-6
NCORES = 8
DEBUG = {}
STOP = None
BAN = None
GELU = AF.Gelu_apprx_tanh


class Buf:
    __slots__ = ("name", "ws", "r")

    def __init__(self, name):
        self.name = name
        self.ws = []
        self.r = []


class Op:
    __slots__ = ("eng", "fn", "deps", "signal", "sigval", "dma", "dsem", "dval", "dprev", "perm")

    def __init__(self, eng, fn, dma=False):
        self.eng = eng
        self.fn = fn
        self.deps = set()
        self.signal = False
        self.sigval = 0
        self.dma = dma
        self.dsem = None
        self.dval = 0
        self.dprev = None
        self.perm = False


class Prog:
    ENGS = ("pe", "act", "dve", "pool", "sp")

    def __init__(self, nc, n_dma_sems=24):
        self.nc = nc
        self.e = {"pe": nc.tensor, "act": nc.scalar, "dve": nc.vector, "pool": nc.gpsimd, "sp": nc.sync}
        self.ops = {k: [] for k in self.ENGS}
        self.allops = []
        self.n_dma_sems = n_dma_sems
        self.dma_rr = 0
        self.dma_last = [None] * n_dma_sems
        self.dma_cnt = [0] * n_dma_sems
        self.n_unique = 0

    def op(self, eng, fn, reads=(), writes=(), dma=False):
        o = Op(eng, fn, dma)
        for b in reads:
            for w in b.ws:
                o.deps.add(w)
        for b in writes:
            for w in b.ws:
                if not (dma and w.dma):
                    o.deps.add(w)
            for r in b.r:
                o.deps.add(r)
        for b in writes:
            if dma:
                b.ws = [w for w in b.ws if w.dma] + [o]
            else:
                b.ws = [o]
            b.r = []
        for b in reads:
            if o not in b.ws:
                b.r.append(o)
        o.deps.discard(o)
        if dma and eng == "pool":
            o.dsem = ("u", self.n_unique)
            self.n_unique += 1
            o.dval = 16
        elif dma:
            k = self.dma_rr
            self.dma_rr = (k + 1) % self.n_dma_sems
            o.dsem = k
            self.dma_cnt[k] += 1
            o.dval = 16 * self.dma_cnt[k]
            o.dprev = self.dma_last[k]
            self.dma_last[k] = o
            if o.dprev is not None:
                o.deps.add(o.dprev)
        self.ops[eng].append(o)
        self.allops.append(o)
        return o

    def barrier(self, bufs):
        pass

    def emit(self):
        nc = self.nc
        sems = {k: nc.semaphore("s_" + k).__enter__() for k in self.ENGS}
        dsems = {i: nc.semaphore("d_%d" % i).__enter__() for i in range(self.n_dma_sems)}
        for i in range(self.n_unique):
            dsems[("u", i)] = nc.semaphore("u_%d" % i).__enter__()
        for o in self.allops:
            for d in o.deps:
                if not d.dma:
                    d.signal = True
        for k in self.ENGS:
            c = 0
            for o in self.ops[k]:
                if o.signal and not o.dma:
                    c += 1
                    o.sigval = c
        for k in self.ENGS:
            eng = self.e[k]
            seen = {}
            for o in self.ops[k]:
                need = {}
                for d in o.deps:
                    if d.dma:
                        key = ("d", d.dsem)
                        val = d.dval
                    else:
                        key = ("e", d.eng)
                        val = d.sigval
                    if need.get(key, 0) < val:
                        need[key] = val
                for key, val in need.items():
                    if seen.get(key, 0) >= val:
                        continue
                    seen[key] = val
                    s = dsems[key[1]] if key[0] == "d" else sems[key[1]]
                    eng.wait_ge(s, val)
                ins = o.fn()
                if o.dma:
                    ins.then_inc(dsems[o.dsem], 16)
                elif o.signal:
                    ins.then_inc(sems[k], 1)
                    seen[("e", k)] = max(seen.get(("e", k), 0), 0)


def host_consts():
    c = {}
    ident = np.eye(128, dtype=np.float32)
    c["c_ident"] = ident.astype(ml_dtypes.bfloat16)
    c["c_identj"] = ident[::-1].copy().astype(ml_dtypes.bfloat16)
    c["c_identf"] = ident.copy()
    k = np.arange(128)[:, None]
    q = np.arange(256)[None, :]
    mb01 = np.where(np.abs((q - 64) - k) <= 64, 0.0, -30000.0).astype(np.float32)
    c["c_mb01"] = mb01.astype(ml_dtypes.bfloat16)
    q2 = np.arange(128)[None, :]
    mb2 = np.where(np.abs(q2 - k) <= 64, 0.0, -30000.0).astype(np.float32)
    c["c_mb2"] = mb2.astype(ml_dtypes.bfloat16)
    inv = 500000.0 ** (-np.arange(0, 16, 2, dtype=np.float32) / 16.0)
    pos = np.arange(L, dtype=np.float32)
    ang = pos[:, None] * inv[None, :]
    cos = np.cos(ang).astype(np.float32)
    sin = np.sin(ang).astype(np.float32)
    cc = np.concatenate([cos, cos], axis=1).reshape(16, 128, 16).transpose(1, 0, 2)
    ss = np.concatenate([-sin, sin], axis=1).reshape(16, 128, 16).transpose(1, 0, 2)
    c["c_ropec"] = np.ascontiguousarray(cc)
    c["c_ropes"] = np.ascontiguousarray(ss)
    t1 = np.zeros((128, 9, 32), np.float32)
    for j in range(9):
        t1[:64, j, :] = j
        t1[64:, j, :] = 8 - j
    c["c_tau1"] = t1
    t2 = np.zeros((128, 8, 32), np.float32)
    for s in range(8):
        t2[:64, s, :] = 7 - s
        t2[64:, s, :] = s
    c["c_tau2"] = t2
    e16 = np.zeros((128, 16), np.float32)
    for gq in range(4):
        for h in range(16):
            e16[32 * gq + h, h] = 1.0
    c["c_e16"] = e16
    return c


CONST_SPECS = [("c_ident", [128, 128], BF16), ("c_identj", [128, 128], BF16), ("c_identf", [128, 128], F32),
               ("c_mb01", [128, 256], BF16), ("c_mb2", [128, 128], BF16),
               ("c_ropec", [128, 16, 16], F32), ("c_ropes", [128, 16, 16], F32),
               ("c_tau1", [128, 9, 32], F32), ("c_tau2", [128, 8, 32], F32), ("c_e16", [128, 16], F32)]

IN_SPECS = [("x", [L, D]), ("norm_w", [D]), ("w_in", [D, 6144]), ("b_gate", [2048]), ("q_norm_w", [64]),
            ("k_norm_w", [64]), ("ssm_lam_re", [2, 32, 64]), ("ssm_lam_im", [2, 32, 64]), ("ssm_log_dt", [2, 32]),
            ("ssm_b_re", [2, 32, 64, 16]), ("ssm_b_im", [2, 32, 64, 16]), ("ssm_c_re", [2, 32, 16, 64]),
            ("ssm_c_im", [2, 32, 16, 64]), ("ssm_d", [512]), ("w_glu", [512, 1024]), ("b_glu", [1024]),
            ("w_proj_ssm", [512, 1024]), ("w_proj_attn", [512, 1024]), ("w_out", [D, D])]


class StopBuild(Exception):
    pass


def build(debug=None, stop=None):
    nc = bass.Bass("TRN2", target_bir_lowering=False, dynamic_dma_scratch_size=4096)
    _CACHE['nc'] = nc
    P = Prog(nc)
    dr = {}
    for name, shape in IN_SPECS:
        dr[name] = nc.dram_tensor(name, shape, F32, kind="ExternalInput").ap()
    for name, shape, dt in CONST_SPECS:
        dr[name] = nc.dram_tensor(name, shape, dt, kind="ExternalInput").ap()
    out_d = nc.dram_tensor("out", [L, D], F32, kind="ExternalOutput").ap()
    dbg_d = {}
    if debug:
        for name, shape in debug.items():
            dbg_d[name] = nc.dram_tensor("dbg_" + name, shape, F32, kind="ExternalOutput").ap()

    stacks = {"left": [], "right": []}

    def sb(name, shape, dt, side="right"):
        t = nc.sbuf_tensor(name, shape, dt, side=side)
        h = t.__enter__()
        stacks[side].append(t)
        return h

    def mark(side="right"):
        return len(stacks[side])

    def release_to(n, side="right"):
        while len(stacks[side]) > n:
            stacks[side].pop().__exit__(None, None, None)

    psum = []
    for i in range(8):
        t = nc.psum_tensor("ps%d" % i, [128, 512], F32)
        psum.append((t.__enter__(), Buf("ps%d" % i)))
    ps_rr = [0]

    def ps_next():
        i = ps_rr[0]
        ps_rr[0] = (i + 1) % 8
        return psum[i]

    def dma(eng, out, in_, reads=(), writes=(), **kw):
        q = P.e[eng]
        return P.op(eng, lambda: q.dma_start(out=out, in_=in_, **kw), reads, writes, dma=True)

    def mm(out, lhsT, rhs, start, stop, reads=(), writes=(), tp=None):
        if tp is None:
            return P.op("pe", lambda: nc.tensor.matmul(out, lhsT, rhs, start=start, stop=stop, skip_group_check=True),
                        reads, writes)
        return P.op("pe", lambda: nc.tensor.matmul(out, lhsT, rhs, start=start, stop=stop, skip_group_check=True,
                                                   tile_position=tp), reads, writes)

    def tr32(out, in_, idn, reads=(), writes=()):
        return P.op("pe", lambda: nc.tensor.transpose(out, in_, idn), reads, writes)

    def act(out, in_, func, reads=(), writes=(), **kw):
        return P.op("act", lambda: nc.scalar.activation(out=out, in_=in_, func=func, **kw), reads, writes)

    def tt(eng, out, in0, in1, op, reads=(), writes=()):
        e = P.e[eng]
        return P.op(eng, lambda: e.tensor_tensor(out=out, in0=in0, in1=in1, op=op), reads, writes)

    def ts(eng, out, in0, s1, op0, s2=None, op1=None, reads=(), writes=()):
        e = P.e[eng]
        if op1 is None:
            return P.op(eng, lambda: e.tensor_scalar(out=out, in0=in0, scalar1=s1, scalar2=None, op0=op0), reads, writes)
        return P.op(eng, lambda: e.tensor_scalar(out=out, in0=in0, scalar1=s1, scalar2=s2, op0=op0, op1=op1), reads, writes)

    def stt(out, in0, scalar, in1, op0, op1, reads=(), writes=()):
        return P.op("dve", lambda: nc.vector.scalar_tensor_tensor(out=out, in0=in0, scalar=scalar, in1=in1, op0=op0, op1=op1),
                    reads, writes)

    def recip(out, in_, reads=(), writes=()):
        return P.op("dve", lambda: nc.vector.reciprocal(out=out, in_=in_), reads, writes)

    def cp(eng, out, in_, reads=(), writes=()):
        if eng == "act":
            return P.op("act", lambda: nc.scalar.activation(out=out, in_=in_, func=AF.Copy), reads, writes)
        e = P.e[eng]
        return P.op(eng, lambda: e.tensor_copy(out=out, in_=in_), reads, writes)

    def memset(eng, ap, val, writes=()):
        e = P.e[eng]
        return P.op(eng, lambda: e.memset(ap, val), (), writes)

    def pdone(pb):
        pb.ws = [P.allops[-1]]

    dbg_bufs = []

    def dump(name, src_ap, src_buf, parts, cols):
        if not debug or name not in debug:
            return
        tmp = sb("dbgtmp_" + name, [128, cols], F32, side="left")
        tb = Buf("dbgtmp_" + name)
        cp("dve", tmp[0:parts, :], src_ap, reads=[src_buf], writes=[tb])
        ob = Buf("dbgout_" + name)
        dma("sp", dbg_d[name][0:parts, :], tmp[0:parts, :], reads=[tb], writes=[ob])
        dbg_bufs.append(ob)

    evac_rr = [0]

    def evac_eng():
        evac_rr[0] ^= 1
        return "act" if evac_rr[0] else "dve"

    barrier_mark = [0]

    def global_barrier():
        lasts = []
        for k in Prog.ENGS:
            for o_ in reversed(P.ops[k]):
                if not (o_.dma and o_.perm):
                    lasts.append(o_)
                    break
        n_prev = len(P.allops)
        bb = Buf("barrier")
        o = P.op("sp", lambda: nc.sync.nop(), [], [bb])
        for l in lasts:
            if l is not o:
                o.deps.add(l)
        for d_ in P.allops[barrier_mark[0]:n_prev]:
            if d_.dma and not d_.perm:
                o.deps.add(d_)
        barrier_mark[0] = n_prev
        for k in ("pe", "act", "dve", "pool"):
            P.op(k, (lambda k=k: P.e[k].nop()), [bb], [])
        P.op("sp", lambda: nc.sync.nop(), [bb], [])

    def maybe_stop(name):
        if stop == name:
            P.op("sp", lambda: nc.sync.nop(), dbg_bufs, [])
            P.emit()
            raise StopBuild()

    late_dmas = []

    def load_const(name, side="left", late=False):
        shape, dt = [(s, d) for (n, s, d) in CONST_SPECS if n == name][0]
        t = sb("s_" + name, shape, dt, side=side)
        b = Buf(name)
        if late:
            late_dmas.append(lambda: dma("sp", t[:], dr[name], writes=[b]))
        else:
            dma("sp", t[:], dr[name], writes=[b])
        return t, b

    ident, B_ID = load_const("c_ident")
    identj, B_IDJ = load_const("c_identj", late=True)
    identf, B_IDF = load_const("c_identf")
    mb01, B_mb01 = load_const("c_mb01", late=True)
    mb2, B_mb2 = load_const("c_mb2", late=True)
    ropec, B_ropec = load_const("c_ropec", late=True)
    ropes, B_ropes = load_const("c_ropes", late=True)
    bgate = sb("bgate", [128, 16], F32, side="left")
    B_bgate = Buf("bgate")
    bglu = sb("bglu", [128, 8], F32, side="left")
    B_bglu = Buf("bglu")
    wqk = sb("wqk", [128, 8, 64], F32, side="left")
    B_wqk = Buf("wqk")
    for blk in range(8):
        late_dmas.append(lambda blk=blk: dma("sp", wqk[:, blk, :], (dr["q_norm_w"] if blk < 6 else dr["k_norm_w"]).unsqueeze(0).broadcast_to([128, 64]),
                                             writes=[B_wqk]))
    xnT = sb("xnT", [128, 8, L], BF16, side="left")
    B_xnT = [Buf("xnT%d" % i) for i in range(16)]
    HB = sb("HB", [128, 4, L], BF16, side="left")
    B_HB = Buf("HB")
    B_YH = [Buf("YH%d" % i) for i in range(4)]
    ARENA = sb("ARENA", [128, 8192], BF16, side="left")
    B_wglu = Buf("wglu")
    B_wza = Buf("wza")
    m_left_ssm = mark("left")
    CAre = sb("CAre", [128, 32, 9, 16], BF16, side="left")
    NCAim = sb("NCAim", [128, 32, 9, 16], BF16, side="left")
    Wt = sb("Wt", [128, 32, 128], BF16, side="left")
    Bex = sb("Bex", [128, 32, 2, 128], BF16, side="left")
    wU = sb("wU", [128, 8, 512], BF16, side="left")
    B_wU = Buf("wU")

    m0_left = mark("left")
    normw = sb("normw", [128, D], F32, side="left")
    B_normw = Buf("normw")
    dma("sp", normw[:], dr["norm_w"].unsqueeze(0).broadcast_to([128, D]), writes=[B_normw])
    NXI, NXB = 4, 3
    xin = [sb("xin%d" % i, [128, D], F32, side="left") for i in range(NXI)]
    B_xin = [Buf("xin%d" % i) for i in range(NXI)]
    junk = sb("junk", [128, D], BF16, side="left")
    B_junk = Buf("junk")
    xnb = [sb("xnb%d" % i, [128, D], BF16, side="left") for i in range(NXB)]
    B_xnb = [Buf("xnb%d" % i) for i in range(NXB)]
    stat = sb("stat", [128, 16, 4], F32, side="left")
    B_stat = [Buf("stat%d" % i) for i in range(16)]
    pa_st = {}

    def pa_load(i):
        xt, bx = xin[i % NXI], B_xin[i % NXI]
        dma("act" if i == 0 else "pool", xt[:], dr["x"][128 * i:128 * (i + 1), :], writes=[bx])

    def pa_sq(i):
        xt, bx = xin[i % NXI], B_xin[i % NXI]
        act(junk[:], xt[:], AF.Square, reads=[bx], writes=[B_junk, B_stat[i]], accum_out=stat[:, i, 0:1])

    def pa_ts(i):
        pass

    def pa_sqrt(i):
        act(stat[:, i, 2:3], stat[:, i, 0:1], AF.Sqrt, reads=[B_stat[i]], writes=[B_stat[i]], scale=1.0 / D, bias=EPS)

    def pa_scale(i):
        xt, bx = xin[i % NXI], B_xin[i % NXI]
        xb, bxb = xnb[i % NXB], B_xnb[i % NXB]
        recip(stat[:, i, 3:4], stat[:, i, 2:3], [B_stat[i]], [B_stat[i]])
        stt(xb[:], xt[:], stat[:, i, 3:4], normw[:], ALU.mult, ALU.mult, [bx, B_stat[i], B_normw], [bxb])

    def pa_tr(i):
        xb, bxb = xnb[i % NXB], B_xnb[i % NXB]
        for half in range(2):
            pt, pb = ps_next()
            for j in range(4):
                kt = half * 4 + j
                mm(pt[:, j * 128:(j + 1) * 128], xb[:, kt * 128:(kt + 1) * 128], ident[:], True, True,
                   reads=[bxb, B_ID], writes=[pb] if j == 0 else [])
            pdone(pb)
            pa_st[(i, half)] = (pt, pb)

    def pa_ev(i):
        for half in range(2):
            pt, pb = pa_st[(i, half)]
            cp("act", xnT[:, half * 4:half * 4 + 4, 128 * i:128 * (i + 1)],
               pt[:].rearrange("p (a n) -> p a n", a=4), reads=[pb], writes=[B_xnT[i]])

    def phaseA_gen():
        stages_a = ((pa_load, 0), (pa_sq, 1), (pa_ts, 2), (pa_sqrt, 2), (pa_scale, 3), (pa_tr, 4), (pa_ev, 5))
        for t_ in range(16 + 5):
            for (f_, lag_) in stages_a:
                if 0 <= t_ - lag_ < 16:
                    f_(t_ - lag_)
            yield

    genA = phaseA_gen()
    hook_state = {"on": False, "busy": False, "cnt": 0}
    _orig_op = P.op

    def _hooked_op(eng, fn, reads=(), writes=(), dma=False):
        o = _orig_op(eng, fn, reads, writes, dma)
        if hook_state["on"] and not hook_state["busy"] and eng == "dve":
            hook_state["cnt"] += 1
            if hook_state["cnt"] % 5 == 0:
                hook_state["busy"] = True
                keep = P.allops[-1]
                next(genA, None)
                hook_state["busy"] = False
        return o

    P.op = _hooked_op
    for _ in range(7):
        next(genA, None)
    hook_state["on"] = True
    if debug:
        dump("xnT", xnT[:, 0, :], B_xnT[15], 128, L)
    maybe_stop("A")

    def wload(dst_ap, src_ap, buf, perm=False):
        o_ = dma("pool", dst_ap, src_ap, writes=[buf])
        o_.perm = perm
        return o_

    def w_in_cols(dst, c0, ncols, buf, perm=False):
        wload(dst, dr["w_in"][:, c0:c0 + ncols].rearrange("(kt p) n -> p kt n", p=128), buf, perm)

    w_in_cols(wU[:], 0, 512, B_wU)

    m_ssm = mark()
    B_CA = Buf("CA")
    B_W = Buf("W")
    B_Bex = Buf("Bex")
    A8 = sb("A8", [128, 32, 2], F32)
    A8s = sb("A8s", [128, 32, 2], F32)
    B_A8 = Buf("A8")
    CAY = sb("CAY", [128, 2, 32, 8, 16], BF16)
    B_CAY = Buf("CAY")
    m_pre = mark()
    BBb = sb("BBb", [128, 2, 32, 16], BF16)
    B_BBb = Buf("BBb")
    m_k = mark()
    tau1, B_tau1 = load_const("c_tau1", side="right")
    tau2, B_tau2 = load_const("c_tau2", side="right")
    e16, B_e16 = load_const("c_e16", side="right")
    LL = sb("LL", [32, 2, 128], F32)
    B_LL = Buf("LL")
    for ri, nm in enumerate(("ssm_lam_re", "ssm_lam_im")):
        for d_ in range(2):
            dma("sp", LL[:, ri, d_ * 64:(d_ + 1) * 64], dr[nm][d_], writes=[B_LL])
    LRI = sb("LRI", [128, 2, 32], F32)
    B_LRI = Buf("LRI")
    pt, pb = ps_next()
    for ri in range(2):
        tr32(pt[:, ri * 32:(ri + 1) * 32], LL[:, ri, :], identf[0:32, 0:32], reads=[B_LL, B_IDF], writes=[pb] if ri == 0 else [])
    pdone(pb)
    cp("dve", LRI[:], pt[:, 0:64].rearrange("p (a g) -> p a g", a=2), reads=[pb], writes=[B_LRI])
    DT = sb("DT", [128, 32], F32)
    B_DT = Buf("DT")
    for d_ in range(2):
        dma("sp", DT[d_ * 64:(d_ + 1) * 64, :], dr["ssm_log_dt"][d_:d_ + 1, :].broadcast_to([64, 32]), writes=[B_DT])
    act(DT[:], DT[:], AF.Exp, reads=[B_DT], writes=[B_DT])
    ts("dve", LRI[:, 0, :], LRI[:, 0, :], -1e-4, ALU.min, reads=[B_LRI], writes=[B_LRI])
    E1 = sb("E1", [128, 2, 32], F32)
    B_E1 = Buf("E1")
    tt("dve", E1[:], LRI[:], DT[:].unsqueeze(1).broadcast_to([128, 2, 32]), ALU.mult, reads=[B_LRI, B_DT], writes=[B_E1])

    pex = sb("pw_ex", [128, 9, 32], F32)
    pan = sb("pw_an", [128, 2, 9, 32], F32)
    pki = sb("pw_ki", [128, 2, 9, 32], I32)
    pkf = sb("pw_kf", [128, 2, 9, 32], F32)
    pcm = sb("pw_cm", [128, 2, 9, 32], F32)
    bs = Buf("pw_scratch")

    def power_table(name, tau, btau, nj):
        AR = sb(name + "r", [128, nj, 32], F32)
        AI = sb(name + "i", [128, nj, 32], F32)
        bt = Buf(name)
        ex, an, ki, kf, cm = pex[:, 0:nj], pan[:, :, 0:nj], pki[:, :, 0:nj], pkf[:, :, 0:nj], pcm[:, :, 0:nj]
        e1b = E1[:, 0, :].unsqueeze(1).broadcast_to([128, nj, 32])
        thb = E1[:, 1, :].unsqueeze(1).broadcast_to([128, nj, 32])
        tt("dve", ex, tau[:], e1b, ALU.mult, reads=[btau, B_E1], writes=[bs])
        act(ex, ex, AF.Exp, reads=[bs], writes=[bs])
        tt("dve", an[:, 0], tau[:], thb, ALU.mult, reads=[btau, B_E1, bs], writes=[bs])
        c_hi = float(np.float32(1.0 / (2 * math.pi)))
        c_lo = 1.0 / (2 * math.pi) - c_hi
        ts("dve", cm[:, 0], an[:, 0], c_lo, ALU.mult, reads=[bs], writes=[bs])
        ts("dve", an[:, 1], an[:, 0], c_hi, ALU.mult, 0.25, ALU.add, reads=[bs], writes=[bs])
        ts("dve", an[:, 0], an[:, 0], c_hi, ALU.mult, reads=[bs], writes=[bs])
        tt("dve", an[:, 0], an[:, 0], cm[:, 0], ALU.add, reads=[bs], writes=[bs])
        tt("dve", an[:, 1], an[:, 1], cm[:, 0], ALU.add, reads=[bs], writes=[bs])
        cp("dve", ki, an, reads=[bs], writes=[bs])
        cp("dve", kf, ki, reads=[bs], writes=[bs])
        tt("dve", an, an, kf, ALU.subtract, reads=[bs], writes=[bs])
        ts("dve", cm, an, 0.5, ALU.is_gt, reads=[bs], writes=[bs])
        tt("dve", an, an, cm, ALU.subtract, reads=[bs], writes=[bs])
        ts("dve", cm, an, -0.5, ALU.is_lt, reads=[bs], writes=[bs])
        tt("dve", an, an, cm, ALU.add, reads=[bs], writes=[bs])
        act(an, an, AF.Sin, reads=[bs], writes=[bs], scale=2 * math.pi)
        tt("dve", AI[:], ex, an[:, 0], ALU.mult, reads=[bs], writes=[bt])
        tt("dve", AR[:], ex, an[:, 1], ALU.mult, reads=[bs], writes=[bt])
        return AR, AI, bt

    AR1, AI1, B_A1 = power_table("pw1", tau1, B_tau1, 9)
    AR2, AI2, B_A2 = power_table("pw2", tau2, B_tau2, 8)
    for (lo, hi, j) in ((0, 64, 8), (64, 128, 0)):
        cp("dve", A8[lo:hi, :, :], AR1[lo:hi, j, :].unsqueeze(2).broadcast_to([hi - lo, 32, 2]), reads=[B_A1], writes=[B_A8])
        ts("dve", A8s[lo:hi, :, 0:1], AI1[lo:hi, j, :].unsqueeze(2), -1.0, ALU.mult, reads=[B_A1], writes=[B_A8])
        cp("dve", A8s[lo:hi, :, 1:2], AI1[lo:hi, j, :].unsqueeze(2), reads=[B_A1], writes=[B_A8])
    maybe_stop("PRE1")
    ZZ = sb("ZZ", [128, 8, 32], F32)
    B_ZZ = Buf("ZZ")
    for (lo, hi, j) in ((0, 64, 1), (64, 128, 7)):
        ts("dve", ZZ[lo:hi, 0, :], AR1[lo:hi, j, :], -1.0, ALU.add, reads=[B_A1], writes=[B_ZZ])
        cp("dve", ZZ[lo:hi, 1, :], AI1[lo:hi, j, :], reads=[B_A1], writes=[B_ZZ])
    lr_, li_ = LRI[:, 0, :], LRI[:, 1, :]
    RW = [B_ZZ, B_LRI]
    tt("dve", ZZ[:, 2, :], lr_, lr_, ALU.mult, reads=RW, writes=[B_ZZ])
    tt("dve", ZZ[:, 3, :], li_, li_, ALU.mult, reads=RW, writes=[B_ZZ])
    tt("dve", ZZ[:, 2, :], ZZ[:, 2, :], ZZ[:, 3, :], ALU.add, reads=RW, writes=[B_ZZ])
    recip(ZZ[:, 2, :], ZZ[:, 2, :], RW, [B_ZZ])
    tt("dve", ZZ[:, 4, :], ZZ[:, 0, :], lr_, ALU.mult, reads=RW, writes=[B_ZZ])
    tt("dve", ZZ[:, 6, :], ZZ[:, 1, :], li_, ALU.mult, reads=RW, writes=[B_ZZ])
    tt("dve", ZZ[:, 4, :], ZZ[:, 4, :], ZZ[:, 6, :], ALU.add, reads=RW, writes=[B_ZZ])
    tt("dve", ZZ[:, 4, :], ZZ[:, 4, :], ZZ[:, 2, :], ALU.mult, reads=RW, writes=[B_ZZ])
    tt("dve", ZZ[:, 5, :], ZZ[:, 1, :], lr_, ALU.mult, reads=RW, writes=[B_ZZ])
    tt("dve", ZZ[:, 7, :], ZZ[:, 0, :], li_, ALU.mult, reads=RW, writes=[B_ZZ])
    tt("dve", ZZ[:, 5, :], ZZ[:, 5, :], ZZ[:, 7, :], ALU.subtract, reads=RW, writes=[B_ZZ])
    tt("dve", ZZ[:, 5, :], ZZ[:, 5, :], ZZ[:, 2, :], ALU.mult, reads=RW, writes=[B_ZZ])
    Braw = sb("Braw", [128, 2, 32, 16], F32)
    B_Braw = Buf("Braw")
    for ri, nm in enumerate(("ssm_b_re", "ssm_b_im")):
        for d_ in range(2):
            for g4 in range(0, 32, 4):
                dma("sp", Braw[d_ * 64:(d_ + 1) * 64, ri, g4:g4 + 4], dr[nm][d_][g4:g4 + 4].rearrange("g p h -> p g h"),
                    writes=[B_Braw])
    BB = sb("BB", [128, 2, 32, 16], F32)
    B_BB = Buf("BB")
    SC1 = sb("SC1", [128, 1152], F32)
    SC2 = sb("SC2", [128, 1152], F32)
    B_SC = Buf("SC")
    Tm = SC1[:, 0:1024].rearrange("p (a g h) -> p a g h", a=2, g=32)
    fre = ZZ[:, 4, :].unsqueeze(2).broadcast_to([128, 32, 16])
    fim = ZZ[:, 5, :].unsqueeze(2).broadcast_to([128, 32, 16])
    tt("dve", BB[:, 0], Braw[:, 0], fre, ALU.mult, reads=[B_Braw, B_ZZ], writes=[B_BB])
    tt("dve", Tm[:, 0], Braw[:, 1], fim, ALU.mult, reads=[B_Braw, B_ZZ], writes=[B_SC])
    tt("dve", BB[:, 0], BB[:, 0], Tm[:, 0], ALU.subtract, reads=[B_SC, B_BB], writes=[B_BB])
    tt("dve", BB[:, 1], Braw[:, 1], fre, ALU.mult, reads=[B_Braw, B_ZZ, B_BB], writes=[B_BB])
    tt("dve", Tm[:, 1], Braw[:, 0], fim, ALU.mult, reads=[B_Braw, B_ZZ, B_SC], writes=[B_SC])
    tt("dve", BB[:, 1], BB[:, 1], Tm[:, 1], ALU.add, reads=[B_SC, B_BB], writes=[B_BB])
    cp("dve", BBb[:], BB[:], reads=[B_BB], writes=[B_BBb])
    CC = sb("CC", [128, 2, 4, 128], F32)
    B_CC = Buf("CC")
    for ri, nm in enumerate(("ssm_c_re", "ssm_c_im")):
        for d_ in range(2):
            dma("sp", CC[:, ri, :, d_ * 64:(d_ + 1) * 64],
                dr[nm][d_].rearrange("(gq g8) h p -> (g8 h) gq p", gq=4), writes=[B_CC])
    CT = sb("CT", [128, 2, 32, 16], F32)
    B_CT = Buf("CT")
    for ri in range(2):
        pt, pb = ps_next()
        for gq in range(4):
            tr32(pt[:, gq * 128:(gq + 1) * 128], CC[:, ri, gq, :], identf[:], reads=[B_CC, B_IDF], writes=[pb] if gq == 0 else [])
        pdone(pb)
        cp("dve", CT[:, ri].rearrange("p g h -> p (g h)"), pt[:], reads=[pb], writes=[B_CT])
    maybe_stop("PRE2")
    for f_ in late_dmas:
        f_()
    dma("sp", bgate[:], dr["b_gate"].rearrange("(n p) -> p n", p=128), writes=[B_bgate], allow_slow_non_contiguous=True)
    dma("sp", bglu[:], dr["b_glu"].rearrange("(n p) -> p n", p=128), writes=[B_bglu], allow_slow_non_contiguous=True)
    for gqr in range(4):
        gs = slice(gqr * 8, gqr * 8 + 8)
        T1 = SC1[:].rearrange("p (g j h) -> p g j h", g=8, j=9)
        T2 = SC2[:].rearrange("p (g j h) -> p g j h", g=8, j=9)
        cre = CT[:, 0, gs, :].unsqueeze(2).broadcast_to([128, 8, 9, 16])
        cim = CT[:, 1, gs, :].unsqueeze(2).broadcast_to([128, 8, 9, 16])
        arb = AR1[:, :, gs].rearrange("p j g -> p g j").unsqueeze(3).broadcast_to([128, 8, 9, 16])
        aib = AI1[:, :, gs].rearrange("p j g -> p g j").unsqueeze(3).broadcast_to([128, 8, 9, 16])
        RD = [B_CT, B_A1, B_SC]
        tt("dve", T1, cre, arb, ALU.mult, reads=RD, writes=[B_SC])
        tt("dve", T2, cim, aib, ALU.mult, reads=RD, writes=[B_SC])
        tt("dve", CAre[:, gs], T1, T2, ALU.subtract, reads=RD, writes=[B_CA])
        tt("dve", T1, cre, aib, ALU.mult, reads=RD + [B_CA], writes=[B_SC])
        tt("dve", T2, cim, arb, ALU.mult, reads=RD, writes=[B_SC])
        stt(NCAim[:, gs], T1, -1.0, T2, ALU.mult, ALU.subtract, RD, [B_CA])
    maybe_stop("PRE2b")
    BAq = sb("BAq", [128, 2, 8, 8, 16], BF16)
    ba_cnt = [0]
    B_BAq = Buf("BAq")
    for gqr in range(4):
        gs = slice(gqr * 8, gqr * 8 + 8)
        T1 = SC1[:, 0:1024].rearrange("p (g j h) -> p g j h", g=8, j=8)
        T2 = SC2[:, 0:1024].rearrange("p (g j h) -> p g j h", g=8, j=8)
        bre = BB[:, 0, gs, :].unsqueeze(2).broadcast_to([128, 8, 8, 16])
        bim = BB[:, 1, gs, :].unsqueeze(2).broadcast_to([128, 8, 8, 16])
        arb = AR2[:, :, gs].rearrange("p j g -> p g j").unsqueeze(3).broadcast_to([128, 8, 8, 16])
        aib = AI2[:, :, gs].rearrange("p j g -> p g j").unsqueeze(3).broadcast_to([128, 8, 8, 16])
        RD = [B_BB, B_A2, B_SC]
        ba_ops = [
            lambda: tt("dve", T1, bre, arb, ALU.mult, reads=RD, writes=[B_SC]),
            lambda: tt("dve", T2, bim, aib, ALU.mult, reads=RD, writes=[B_SC]),
            lambda: tt("dve", BAq[:, 0], T1, T2, ALU.subtract, reads=RD, writes=[B_BAq]),
            lambda: tt("dve", T1, bre, aib, ALU.mult, reads=RD + [B_BAq], writes=[B_SC]),
            lambda: tt("dve", T2, bim, arb, ALU.mult, reads=RD, writes=[B_SC]),
            lambda: tt("dve", BAq[:, 1], T1, T2, ALU.add, reads=RD, writes=[B_BAq]),
        ]
        for f_ in ba_ops:
            if BAN is not None and ba_cnt[0] >= BAN:
                maybe_stop("PRE3x")
            f_()
            ba_cnt[0] += 1
        for g2 in range(0, 8, 2):
            if STOP == "PRE3x":
                continue
            pt, pb = ps_next()
            k = 0
            for gg in (g2, g2 + 1):
                for ri in range(2):
                    mm(pt[:, k * 128:(k + 1) * 128], BAq[:, ri, gg].rearrange("p s h -> p (s h)"), ident[:], True, True,
                       reads=[B_BAq, B_ID], writes=[pb] if k == 0 else [])
                    k += 1
            pdone(pb)
            g = gqr * 8 + g2
            cp("act", Bex[:, g:g + 2].rearrange("p g r q -> p (g r q)"), pt[:], reads=[pb], writes=[B_Bex])
    maybe_stop("PRE3")
    maybe_stop("PRE3x")
    hook_state["on"] = False
    for _ in genA:
        pass
    global_barrier()
    release_to(m_k)
    release_to(m0_left, side="left")
    UT = sb("UT", [128, 32, 256], BF16, side="left")
    B_UTg = [Buf("UT%d" % i) for i in range(16)]
    U = sb("U", [128, 2, 32, 8, 16], BF16)
    B_Us = [Buf("U%d" % i) for i in range(16)]
    for ct in range(2):
        for s in range(8):
            pt, pb = ps_next()
            c0 = 1024 * ct + s
            for kt in range(8):
                mm(pt[:], xnT[:, kt, c0:c0 + 1017:8], wU[:, kt, :], kt == 0, kt == 7,
                   reads=B_xnT + [B_wU], writes=[pb] if kt == 0 else [])
            pdone(pb)
            cp("act", U[:, ct, :, s, :], pt[:].rearrange("p (g h) -> p g h", g=32), reads=[pb], writes=[B_Us[ct * 8 + s]])
    if debug:
        dump("U", U[:].rearrange("p a g s c -> p (a g s c)"), B_Us[15], 128, 8192)
    for g0 in range(0, 32, 2):
        pt, pb = ps_next()
        k = 0
        for g in (g0, g0 + 1):
            for ct in range(2):
                mm(pt[:, k * 128:(k + 1) * 128], U[:, ct, g].rearrange("p s h -> p (s h)"), ident[:], True, True,
                   reads=B_Us + [B_ID], writes=[pb] if k == 0 else [])
                k += 1
        pdone(pb)
        cp("act", UT[:, g0:g0 + 2, :].rearrange("p g c -> p (g c)"), pt[:], reads=[pb], writes=[B_UTg[g0 // 2]])
    Kall = sb("Kall", [16, 32, 15, 16], BF16)
    B_Kall_l = [Buf("Kall%d" % i) for i in range(8)]
    dT = sb("dT", [16, 32], F32)
    B_dT = Buf("dT")
    dma("sp", dT[:], dr["ssm_d"].rearrange("(g h) -> h g", h=16), writes=[B_dT], allow_slow_non_contiguous=True)
    Dm = sb("Dm", [16, 32, 16], F32)
    B_Dm = Buf("Dm")
    tt("dve", Dm[:], identf[0:16, 0:16].unsqueeze(1).broadcast_to([16, 32, 16]), dT[:].unsqueeze(2).broadcast_to([16, 32, 16]),
       ALU.mult, reads=[B_IDF, B_dT], writes=[B_Dm])
    Ktmp = sb("Ktmp", [16, 4, 16], F32)
    B_Ktmp = Buf("Ktmp")
    BBz = sb("BBz", [128, 32, 2, 128], BF16)
    B_BBz = Buf("BBz")
    memset("pool", BBz[:], 0.0, writes=[B_BBz])
    for ri in range(2):
        cp("dve", BBz[0:64, :, ri, 0:16], BBb[0:64, ri, :, :], reads=[B_BBb, B_BBz], writes=[B_BBz])
        cp("dve", BBz[64:128, :, ri, 32:48], BBb[64:128, ri, :, :], reads=[B_BBb, B_BBz], writes=[B_BBz])
    CAK = sb("CAK", [128, 2, 32, 8, 16], BF16)
    B_CAK = Buf("CAK")
    B_CAK2 = [Buf("CAK0"), Buf("CAK1")]
    cp("dve", CAK[0:64, 0], CAre[0:64, :, 0:8, :], reads=[B_CA], writes=[B_CAK2[0]])
    cp("dve", CAK[64:128, 0], CAre[64:128, :, 1:9, :], reads=[B_CA, B_CAK2[0]], writes=[B_CAK2[0]])
    cp("act", CAK[0:64, 1], NCAim[0:64, :, 0:8, :], reads=[B_CA], writes=[B_CAK2[1]])
    cp("act", CAK[64:128, 1], NCAim[64:128, :, 1:9, :], reads=[B_CA, B_CAK2[1]], writes=[B_CAK2[1]])
    for k_, src in enumerate((CAre, NCAim)):
        cp("act", CAY[0:64, k_], src[0:64, :, 1:9, :], reads=[B_CA], writes=[B_CAY])
        cp("act", CAY[64:128, k_], src[64:128, :, 0:8, :], reads=[B_CA, B_CAY], writes=[B_CAY])
    maybe_stop("PRE3b")
    for g0 in range(0, 32, 4):
        pt, pb = ps_next()
        for gl in range(4):
            g = g0 + gl
            o_ = pt[:, gl * 128:(gl + 1) * 128]
            mm(o_, BBz[:, g, 0, :], CAK[:, 0, g].rearrange("p j h -> p (j h)"), True, False,
               reads=[B_BBz] + B_CAK2, writes=[pb] if gl == 0 else [])
            mm(o_, BBz[:, g, 1, :], CAK[:, 1, g].rearrange("p j h -> p (j h)"), False, True, reads=[B_BBz] + B_CAK2)
        pdone(pb)
        pf = pt[0:16, :].rearrange("p (a i h) -> p a i h", a=4, i=8)
        pbw = pt[32:48, :].rearrange("p (a i h) -> p a i h", a=4, i=8)
        gsl = slice(g0, g0 + 4)
        B_Kall = B_Kall_l[g0 // 4]
        cp("dve", Kall[:, gsl, 0:7, :], pbw[:, :, 0:7, :], reads=[pb], writes=[B_Kall])
        cp("dve", Kall[:, gsl, 8:15, :], pf[:, :, 1:8, :], reads=[pb], writes=[B_Kall])
        tt("dve", Ktmp[:], pf[:, :, 0, :], Dm[:, gsl, :], ALU.add, reads=[pb, B_Dm, B_Ktmp], writes=[B_Ktmp])
        tt("dve", Kall[:, gsl, 7, :], pbw[:, :, 7, :], Ktmp[:], ALU.add, reads=[pb, B_Ktmp], writes=[B_Kall])
        if g0 in (12, 28):
            hsl = slice(g0 - 12, g0 + 4)
            for s_ in range(8):
                dma("sp", Wt[s_ * 16:(s_ + 1) * 16, hsl, :].rearrange("p g (t h) -> p g t h", t=8),
                    Kall[:, hsl, 7 - s_:15 - s_, :], reads=B_Kall_l[(g0 - 12) // 4:(g0 + 4) // 4], writes=[B_W])
    maybe_stop("PRE4")
    if debug:
        dump("Wt", Wt[:].rearrange("p g c -> p (g c)"), B_W, 128, 4096)
        dump("CAre", CAre[:].rearrange("p g j h -> p (g j h)"), B_CA, 128, 4608)
        dump("Bex", Bex[:].rearrange("p g r q -> p (g r q)"), B_Bex, 128, 8192)
        dump("A8", A8[:].rearrange("p g r -> p (g r)"), B_A8, 128, 64)
    global_barrier()
    release_to(m_pre)
    maybe_stop("PRE")

    UY = sb("UY", [128, 8192], BF16)
    B_UY = Buf("UY")
    Ysb = UY[:].rearrange("p (a t c) -> p a t c", a=2, t=8)
    SX = sb("SX", [128, 32, 2, 258], BF16)
    B_SXf = [Buf("SXf%d" % i) for i in range(64)]
    B_SXb = [Buf("SXb%d" % i) for i in range(64)]
    B_SXz = Buf("SXz")
    B_SXall = B_SXf + B_SXb + [B_SXz]
    B_Sg = [Buf("Sg%d" % i) for i in range(32)]
    memset("dve", SX[:, :, :, 0:2], 0.0, writes=B_SXall)
    memset("dve", SX[:, :, :, 256:258], 0.0, writes=B_SXall)
    m_merged = mark()
    WA = ARENA[:, 0:5120].rearrange("p (k b n) -> p k b n", k=8, b=5)
    B_WA = Buf("WA")

    def load_WA(hp_):
        for blk, c0 in enumerate((1024 + hp_ * 128, 1536 + hp_ * 128, 2048 + hp_ * 128, 2560 + hp_ * 128, 3072 + hp_ * 128)):
            wload(WA[:, :, blk, :], dr["w_in"][:, c0:c0 + 128].rearrange("(kt p) n -> p kt n", p=128), B_WA, perm=True)

    load_WA(0)
    KTz = sb("KTz", [128, 2, L], BF16)
    B_KTz = Buf("KTz")
    memset("pool", KTz[:], 0.0, writes=[B_KTz])
    for g in range(32):
        pt, pb = ps_next()
        for ri in range(2):
            mm(pt[:, ri * 256:(ri + 1) * 256], Bex[:, g, ri, :], UT[:, g, :], True, True,
               reads=[B_Bex, B_UTg[g // 2]], writes=[pb] if ri == 0 else [])
        pdone(pb)
        cp("act", SX[0:64, g, :, 2:258], pt[0:64, :].rearrange("p (r c) -> p r c", r=2), reads=[pb, B_SXz], writes=[B_Sg[g]])
        cp("dve", SX[64:128, g, :, 0:256], pt[64:128, :].rearrange("p (r c) -> p r c", r=2), reads=[pb, B_SXz], writes=[B_Sg[g]])
    if debug:
        dump("S", SX[:].rearrange("p g r c -> p (g r c)"), B_SXz, 128, 32 * 2 * 258)
    for ct in range(2):
        for gb in range(8):
            pt, pb = ps_next()
            for gl in range(4):
                g = gb * 4 + gl
                mm(pt[:, gl * 128:(gl + 1) * 128], UT[:, g, 128 * ct:128 * ct + 128], Wt[:, g, :], True, True,
                   reads=[B_UTg[g // 2], B_W], writes=[pb] if gl == 0 else [])
            pdone(pb)
            cp("act", Ysb[:, ct, :, gb * 64:(gb + 1) * 64].rearrange("p t (g h) -> p g t h", g=4),
               pt[:].rearrange("p (g t h) -> p g t h", g=4, t=8), reads=[pb], writes=[B_UY])
    global_barrier()
    release_to(m_left_ssm, side="left")
    NSB, NSD = 8, 3
    STG = sb("STG", [128, NSD, NSB, 32, 2, 3], F32)
    B_stgA = [Buf("stgA%d" % i) for i in range(NSD)]
    B_stgS = [Buf("stgS%d" % i) for i in range(NSD)]
    NEWR = sb("NEWR", [128, NSD, NSB, 32, 2], F32)
    B_new = [Buf("new%d" % i) for i in range(NSD)]
    zst = sb("zst", [128, 32, 2], F32)
    B_zst = Buf("zst")
    memset("dve", zst[:], 0.0, writes=[B_zst])
    M8 = sb("M8", [128, 32, 2, 2], F32)
    B_M8 = Buf("M8")
    cp("dve", M8[:, :, 0, 0:1], A8[:, :, 0:1], reads=[B_A8], writes=[B_M8])
    cp("dve", M8[:, :, 1, 1:2], A8[:, :, 0:1], reads=[B_A8, B_M8], writes=[B_M8])
    cp("dve", M8[:, :, 0, 1:2], A8s[:, :, 0:1], reads=[B_A8, B_M8], writes=[B_M8])
    cp("dve", M8[:, :, 1, 0:1], A8s[:, :, 1:2], reads=[B_A8, B_M8], writes=[B_M8])
    NBT = 256 // NSB

    def bulk_s(bt):
        rb_, c0_ = bt % NSD, NSB * bt
        cp("act", STG[0:64, rb_, :, :, :, 2], SX[0:64, :, :, 2 + c0_:2 + c0_ + NSB].rearrange("p g r s -> p s g r"),
           reads=[B_SXf[bt]] + B_Sg, writes=[B_stgS[rb_]])
        hi_ = 255 - c0_
        lo_ = hi_ - NSB
        cols_ = slice(hi_, lo_, -1) if lo_ >= 0 else slice(hi_, None, -1)
        cp("act", STG[64:128, rb_, :, :, :, 2], SX[64:128, :, :, cols_].rearrange("p g r s -> p s g r"),
           reads=[B_SXb[NBT - 1 - bt]] + B_Sg, writes=[B_stgS[rb_]])

    def conv_s(bt):
        rb_, c0_ = bt % NSD, NSB * bt
        cp("act", SX[0:64, :, :, 2 + c0_:2 + c0_ + NSB], NEWR[0:64, rb_].rearrange("p s g r -> p g r s"),
           reads=[B_new[rb_]], writes=[B_SXf[bt]])
        cp("act", SX[64:128, :, :, 256 - NSB - c0_:256 - c0_], NEWR[64:128, rb_, ::-1].rearrange("p s g r -> p g r s"),
           reads=[B_new[rb_]], writes=[B_SXb[NBT - 1 - bt]])

    bulk_s(0)
    bulk_s(1)

    def scan_gen():
        prev, bprev = zst[:], B_zst
        for bt in range(NBT):
            rb = bt % NSD
            if bt + 2 < NBT:
                bulk_s(bt + 2)
            for sl in range(NSB):
                tt("dve", STG[:, rb, sl, :, :, 0:2], M8[:], prev.unsqueeze(2).broadcast_to([128, 32, 2, 2]), ALU.mult,
                   reads=[B_M8, bprev], writes=[B_stgA[rb]])
                yield
                new = NEWR[:, rb, sl]
                P.op("dve", (lambda new=new, rb=rb, sl=sl: nc.vector.tensor_reduce(
                    out=new, in_=STG[:, rb, sl], op=ALU.add, axis=AX.X)),
                    [B_stgA[rb], B_stgS[rb]], [B_new[rb]])
                prev, bprev = new, B_new[rb]
                if sl == NSB - 1 and bt >= 1:
                    conv_s(bt - 1)
                yield
        conv_s(NBT - 1)
        yield

    scan_it = scan_gen()

    def adv_scan(n=1):
        for _ in range(n):
            next(scan_it, None)

    m_att = mark()
    QKT = sb("QKT", [128, 3, L], BF16)
    B_QKT = Buf("QKT")
    VT = sb("VT", [128, L], BF16)
    B_VT = Buf("VT")
    VA = sb("VA", [128, 3, 16, 192], BF16)
    B_VA = Buf("VA")
    memset("dve", VA[:, :, :, 64:128], 1.0, writes=[B_VA])
    sq_l = [sb("sq%d" % i, [128, 8, 64], F32) for i in range(3)]
    B_sq_l = [Buf("sq%d" % i) for i in range(3)]
    qn_l = [sb("qn%d" % i, [128, 8, 64], BF16) for i in range(3)]
    wqkb = sb("wqkb", [128, 8, 64], BF16)
    B_wqkb = Buf("wqkb")
    cp("act", wqkb[:], wqk[:], reads=[B_wqk], writes=[B_wqkb])
    B_qn_l = [Buf("qn%d" % i) for i in range(3)]
    qbf_l = [ARENA[:, 5120 + 512 * i:5120 + 512 * (i + 1)].rearrange("p (b d) -> p b d", b=8) for i in range(4)]
    B_qbf_l = [Buf("qbf%d" % i) for i in range(4)]
    nst_l = [sb("nst%d" % i, [128, 4, 8], F32) for i in range(4)]
    B_nst_l = [Buf("nst%d" % i) for i in range(4)]
    rp_l = [ARENA[:, 7168 + 512 * i:7168 + 512 * (i + 1)].bitcast(F32).rearrange("p (a b d) -> p a b d", a=2, b=8) for i in range(2)]
    B_rp_l = [Buf("rp%d" % i) for i in range(2)]
    NPT = 3
    PT = [sb("PT%d" % i, [128, 512], BF16) for i in range(NPT)]
    B_PT = [Buf("PT%d" % i) for i in range(NPT)]
    acc_rr = [0]
    ps6_rr = [0]

    psq_rr = [0]
    pso_rr = [0]

    def ps_q():
        i_ = psq_rr[0]
        psq_rr[0] = (i_ + 1) % 4
        return psum[i_]

    def ps_o():
        i_ = pso_rr[0]
        pso_rr[0] = (i_ + 1) % 4
        return psum[4 + i_]

    def ps_next6():
        i_ = ps6_rr[0]
        ps6_rr[0] = (i_ + 1) % 6
        return psum[i_]

    rden = sb("rden", [128, 512], F32)
    B_rden_h = [Buf("rden0"), Buf("rden1")]
    pt_rr = [0]
    for hp in range(4):
        qst = {}

        def stA(i):
            pq, pqb = ps_q()
            for kt in range(8):
                mm(pq[:], xnT[:, kt, 128 * i:128 * (i + 1)], WA[:, kt, 0:4, :], kt == 0, kt == 7,
                   reads=[B_xnT[i], B_WA], writes=[pqb] if kt == 0 else [])
            pdone(pqb)
            qst[i] = (pq, pqb)

        NSQ, NNS, NQN, NQF, NRP = 3, 4, 3, 4, 2

        def stB1(i):
            pq, pqb = qst[i]
            k_ = hp * 16 + i
            act(sq_l[k_ % NSQ][:], pq[:].rearrange("p (b d) -> p b d", b=8), AF.Square, reads=[pqb], writes=[B_sq_l[k_ % NSQ]])

        def dve_ops_B2(i):
            k_ = hp * 16 + i
            sq, B_sq, nst, B_nst = sq_l[k_ % NSQ], B_sq_l[k_ % NSQ], nst_l[k_ % NNS], B_nst_l[k_ % NNS]
            return [
                lambda: P.op("dve", (lambda nst=nst, sq=sq: nc.vector.tensor_reduce(out=nst[:, 0, :], in_=sq[:], op=ALU.add, axis=AX.X)),
                             [B_sq], [B_nst]),
            ]

        def stB3(i):
            k_ = hp * 16 + i
            nst, B_nst = nst_l[k_ % NNS], B_nst_l[k_ % NNS]
            act(nst[:, 2, :], nst[:, 0, :], AF.Sqrt, reads=[B_nst], writes=[B_nst], scale=1.0 / 64, bias=EPS)

        def dve_ops_C1(i):
            pq, pqb = qst[i]
            pq3 = pq[:].rearrange("p (b d) -> p b d", b=8)
            k_ = hp * 16 + i
            qn, B_qn = qn_l[k_ % NQN], B_qn_l[k_ % NQN]
            nst, B_nst = nst_l[k_ % NNS], B_nst_l[k_ % NNS]
            qbf, B_qbf = qbf_l[k_ % NQF], B_qbf_l[k_ % NQF]
            rp, B_rp = rp_l[k_ % NRP], B_rp_l[k_ % NRP]
            cc = ropec[:, i, :].unsqueeze(1).broadcast_to([128, 8, 16])
            return [
                lambda: recip(nst[:, 3, :], nst[:, 2, :], [B_nst], [B_nst]),
                lambda: tt("dve", qn[:], pq3, nst[:, 3, :].unsqueeze(2).broadcast_to([128, 8, 64]), ALU.mult,
                           reads=[pqb, B_nst], writes=[B_qn]),
                lambda: tt("dve", qbf[:], qn[:], wqkb[:], ALU.mult, reads=[B_qn, B_wqkb], writes=[B_qbf]),
                lambda: tt("dve", rp[:, 0], qbf[:, :, 0:16], cc, ALU.mult, reads=[B_qbf, B_ropec], writes=[B_rp]),
                lambda: tt("dve", rp[:, 1].rearrange("p b (h d) -> p b h d", h=2),
                           qbf[:, :, 0:16].rearrange("p b (h d) -> p b h d", h=2)[:, :, ::-1, :],
                           ropes[:, i, :].rearrange("p (h d) -> p h d", h=2).unsqueeze(1).broadcast_to([128, 8, 2, 8]),
                           ALU.mult, reads=[B_qbf, B_ropes, B_rp], writes=[B_rp]),
                lambda: tt("dve", qbf[:, :, 0:16], rp[:, 0], rp[:, 1], ALU.add, reads=[B_rp, B_qbf], writes=[B_qbf]),
            ]

        def dve_step(i2, i3):
            b2 = dve_ops_B2(i2) if 0 <= i2 < 16 else []
            c1 = dve_ops_C1(i3) if 0 <= i3 < 16 else []
            sc = [lambda: adv_scan(1), lambda: adv_scan(1)] if c1 else []
            order = []
            seq = [(c1, 0), (b2, 0), (c1, 1), (sc, 0), (c1, 2), (sc, 1), (c1, 3), (c1, 4), (c1, 5)]
            for lst, k in seq:
                if k < len(lst):
                    lst[k]()

        def stC3(i):
            k_ = hp * 16 + i
            qn, B_qn, qbf, B_qbf = qn_l[k_ % NQN], B_qn_l[k_ % NQN], qbf_l[k_ % NQF], B_qbf_l[k_ % NQF]
            cp("act", qbf[:, :, 16:64], qn[:, :, 16:64], reads=[B_qn], writes=[B_qbf])

        def stD1(i):
            k_ = hp * 16 + i
            qbf, B_qbf = qbf_l[k_ % NQF], B_qbf_l[k_ % NQF]
            ptt, ptb = ps_o()
            for blk in range(4):
                mm(ptt[:, blk * 128:(blk + 1) * 128], qbf[:, 2 * blk:2 * blk + 2, :].rearrange("p b d -> p (b d)"), ident[:],
                   True, True, reads=[B_qbf, B_ID], writes=[ptb] if blk == 0 else [])
            pdone(ptb)
            qst[("t", i)] = (ptt, ptb)

        def stD2(i):
            ptt, ptb = qst[("t", i)]
            cp("act", QKT[:, :, 128 * i:128 * (i + 1)], ptt[:, 0:384].rearrange("p (b n) -> p b n", b=3), reads=[ptb], writes=[B_QKT])
            cp("act", KTz[0:64, 0, 128 * i:128 * (i + 1)], ptt[0:64, 384:512], reads=[ptb], writes=[B_KTz])
            cp("act", KTz[64:128, 1, 128 * i:128 * (i + 1)], ptt[64:128, 384:512], reads=[ptb], writes=[B_KTz])

        def key_tokens(o, j):
            if o == 0:
                return 128 * j, 1
            if o == 1:
                r4, j4 = j // 4, j % 4
                return 512 * j4 + r4, 4
            return j, 16

        vst = {}

        def v_proj(tb):
            pv_, pvb = ps_o()
            for kt in range(8):
                mm(pv_[:], WA[:, kt, 4, :], xnT[:, kt, tb * 512:(tb + 1) * 512], kt == 0, kt == 7,
                   reads=[B_WA] + B_xnT, writes=[pvb] if kt == 0 else [])
            pdone(pvb)
            vst[("p", tb)] = (pv_, pvb)

        def v_proj_ev(tb):
            pv_, pvb = vst[("p", tb)]
            cp("act", VT[:, tb * 512:(tb + 1) * 512], pv_[:], reads=[pvb], writes=[B_VT])

        def v_tr(k):
            o, j0 = k // 4, 4 * (k % 4)
            pv_, pvb = ps_o()
            for jj in range(4):
                st, step = key_tokens(o, j0 + jj)
                mm(pv_[:, jj * 128:(jj + 1) * 128], VT[:, st:st + 127 * step + 1:step], ident[:], True, True,
                   reads=[B_VT, B_ID], writes=[pvb] if jj == 0 else [])
            pdone(pvb)
            vst[("t", k)] = (pv_, pvb)

        def v_tr_ev(k):
            o, j0 = k // 4, 4 * (k % 4)
            pv_, pvb = vst[("t", k)]
            cp("act", VA[:, o, j0:j0 + 4, :].rearrange("p j (h x) -> p j h x", h=3)[:, :, 0:3:2, :],
               pv_[:].rearrange("p (j h d) -> p j h d", j=4, h=2), reads=[pvb], writes=[B_VA])

        for t_ in range(16 + 5):
            for (f_, lag_) in ((stA, 0), (stB1, 1), (stD2, 5)):
                if 0 <= t_ - lag_ < 16:
                    f_(t_ - lag_)
            b2_i, c1_i = t_ - 2, t_ - 3
            if 0 <= b2_i < 16 and not (0 <= c1_i < 16):
                for f_ in dve_ops_B2(b2_i):
                    f_()
            else:
                dve_step(b2_i, c1_i)
            for (f_, lag_) in ((stB3, 2), (stD1, 4)):
                if 0 <= t_ - lag_ < 16:
                    f_(t_ - lag_)
            if 1 <= t_ <= 4:
                v_proj(t_ - 1)
            if 2 <= t_ <= 5:
                v_proj_ev(t_ - 2)
            if 7 <= t_ <= 18:
                v_tr(t_ - 7)
            if 8 <= t_ <= 19:
                v_tr_ev(t_ - 8)
            if t_ == 17 and hp + 1 < 4:
                load_WA(hp + 1)
            if t_ == 17 and hp == 3:
                dma("pool", ARENA[:, 0:4096].rearrange("p (k n) -> p k n", k=4),
                    dr["w_glu"].rearrange("(kt p) n -> p kt n", p=128), writes=[B_WA, B_wglu]).perm = True
            if t_ in (5, 6, 17, 18, 19, 20):
                adv_scan(2)
        if hp == 3:
            dma("pool", ARENA[:, 4096:8192].rearrange("p (k n) -> p k n", k=8),
                dr["w_in"][:, 512:1024].rearrange("(kt p) n -> p kt n", p=128), writes=[B_WA, B_wza] + B_qbf_l + B_rp_l).perm = True
        glist = []
        for hh in range(2):
            for qb in range(4):
                contribs = []
                for j in range(4 * qb - 1, 4 * qb + 5):
                    if j < 0 or j > 15:
                        continue
                    lo = max(128 * j - 64, 512 * qb, 0)
                    hi = min(128 * j + 192, 512 * qb + 512, L)
                    n = hi - lo
                    m0_ = lo - (128 * j - 64)
                    contribs.append((0, j, lo, 1, n, mb01[:, m0_:m0_ + n], lo - 512 * qb, 1))
                for r4 in range(4):
                    for j4 in (qb - 1, qb, qb + 1):
                        if j4 < 0 or j4 > 3:
                            continue
                        if j4 == qb:
                            mlo, n, mc = 128 * qb, 128, 64
                        elif j4 == qb - 1:
                            mlo, n, mc = 128 * qb, 64, 192
                        else:
                            mlo, n, mc = 128 * qb + 64, 64, 0
                        qs = 4 * mlo + r4
                        contribs.append((1, r4 * 4 + j4, qs, 4, n, mb01[:, mc:mc + n], qs - 512 * qb, 4))
                for r16 in range(16):
                    qs = 512 * qb + r16
                    contribs.append((2, r16, qs, 16, 32, mb2[:, 32 * qb:32 * qb + 32], r16, 16))
                groups, cur, used = [], [], 0
                for cbt in contribs:
                    if used + cbt[4] > 512:
                        groups.append(cur)
                        cur, used = [], 0
                    cur.append((cbt, used))
                    used += cbt[4]
                if cur:
                    groups.append(cur)
                for gi_, grp in enumerate(groups):
                    glist.append((hh, qb, grp, gi_ == 0, gi_ == len(groups) - 1))
        gst = {}
        accst = {}

        def stS(k):
            hh, qb, grp, gfirst, glast = glist[k]
            ps_, psb = ps_next6()
            firstg = True
            tot = 0
            for (cbt, off) in grp:
                o, j, qs, qstep, n, mask_ap, a0, astep = cbt
                st, step = key_tokens(o, j)
                mm(ps_[:, off:off + n], KTz[:, hh, st:st + 127 * step + 1:step],
                   QKT[:, o, qs:qs + (n - 1) * qstep + 1:qstep], True, False,
                   reads=[B_QKT, B_KTz], writes=[psb] if firstg else [])
                firstg = False
                mm(ps_[:, off:off + n], ident[:], mask_ap, False, True, reads=[B_ID, B_mb01, B_mb2])
                tot = off + n
            pdone(psb)
            pi = pt_rr[0]
            pt_rr[0] = (pi + 1) % NPT
            act(PT[pi][:, 0:tot], ps_[:, 0:tot], AF.Exp, reads=[psb], writes=[B_PT[pi]], scale=0.125)
            gst[k] = pi

        def stPV(k):
            adv_scan(3 if k % 3 == 0 else 2)
            hh, qb, grp, gfirst, glast = glist[k]
            rows = slice(hh * 64, hh * 64 + 64)
            pi = gst[k]
            if gfirst:
                ai = acc_rr[0]
                acc_rr[0] ^= 1
                accst[(hh, qb)] = psum[6 + ai]
            acc, accb = accst[(hh, qb)]
            for gi2, (cbt, off) in enumerate(grp):
                o, j, qs, qstep, n, mask_ap, a0, astep = cbt
                fst = gfirst and gi2 == 0
                lst = glast and gi2 == len(grp) - 1
                mm(acc[:, a0:a0 + (n - 1) * astep + 1:astep], VA[:, o, j, 64 * hh:64 * hh + 128], PT[pi][:, off:off + n],
                   fst, lst, reads=[B_VA, B_PT[pi]], writes=[accb] if fst else [])
            if glast:
                pdone(accb)
                B_rden = B_rden_h[hh]
                den_rows = slice(64, 128) if hh == 0 else slice(0, 64)
                act(rden[rows, :], acc[den_rows, :], AF.Ln, reads=[accb], writes=[B_rden])
                act(rden[rows, :], rden[rows, :], AF.Exp, reads=[B_rden], writes=[B_rden], scale=-1.0)
                tt("dve", HB[rows, hp, qb * 512:(qb + 1) * 512], acc[rows, :], rden[rows, :], ALU.mult,
                   reads=[accb, B_rden], writes=[B_HB])

        LAG = 2
        for t_ in range(len(glist) + LAG):
            if t_ < len(glist):
                stS(t_)
            if t_ - LAG >= 0:
                stPV(t_ - LAG)
    if debug:
        dump("attnT", HB[:].rearrange("p a t -> p (a t)"), B_HB, 128, 4 * L)
    for _ in scan_it:
        pass
    if debug:
        dump("X", SX[:].rearrange("p g r c -> p (g r c)"), B_SXz, 128, 32 * 2 * 258)
    global_barrier()
    release_to(m_merged)
    YH = sb("YH", [128, 4, L], BF16, side="left")
    wglu = ARENA[:, 0:4096].rearrange("p (k n) -> p k n", k=4)
    wza = ARENA[:, 4096:8192].rearrange("p (k n) -> p k n", k=8)
    wzb = sb("wzb", [128, 8, 512], BF16, side="left")
    B_wzb = Buf("wzb")
    wpa = sb("wpa", [128, 4, 1024], BF16, side="left")
    B_wpa = Buf("wpa")
    wps = sb("wps", [128, 4, 1024], BF16, side="left")
    B_wps = Buf("wps")
    wg = sb("wg", [128, 8, 2048], BF16, side="left")
    B_wg = Buf("wg")
    w_in_cols(wzb[:], 3584, 512, B_wzb, perm=True)
    wload(wps[:], dr["w_proj_ssm"].rearrange("(kt p) n -> p kt n", p=128), B_wps, perm=True)
    wload(wpa[:], dr["w_proj_attn"].rearrange("(kt p) n -> p kt n", p=128), B_wpa, perm=True)
    w_in_cols(wg[:, :, 0:1024], 4096, 1024, B_wg, perm=True)
    w_in_cols(wg[:, :, 1024:2048], 5120, 1024, B_wg, perm=True)
    hat = sb("hat", [128, 4, 512], BF16)
    B_hat = Buf("hat")
    sgt = sb("sgt", [128, 512], F32)
    B_sgt = Buf("sgt")
    szt = sb("szt", [128, 512], F32)
    B_szt = Buf("szt")
    h1t = sb("h1t", [128, 512], F32)
    B_h1t = Buf("h1t")

    def y_inter(ct):
        for gb in range(8):
            pt, pb = ps_next()
            for gl in range(4):
                g = gb * 4 + gl
                o_ = pt[:, gl * 128:(gl + 1) * 128]
                mm(o_, SX[:, g, 0, 128 * ct + 1:128 * ct + 129], CAY[:, 0, g].rearrange("p j h -> p (j h)"), True, False,
                   reads=[B_CAY] + B_SXall + B_Sg, writes=[pb] if gl == 0 else [])
                mm(o_, SX[:, g, 1, 128 * ct + 1:128 * ct + 129], CAY[:, 1, g].rearrange("p j h -> p (j h)"), False, True)
            pdone(pb)
            yv = Ysb[:, ct, :, gb * 64:(gb + 1) * 64].rearrange("p t (g h) -> p g t h", g=4)
            tt("dve", yv, pt[:].rearrange("p (g t h) -> p g t h", g=4, t=8), yv, ALU.add, reads=[pb, B_UYc[ct]], writes=[B_UYc[ct]])

    def y_tr(tb_):
        ct, ch_ = tb_ // 2, tb_ % 2
        for cht in range(4):
            pt, pb = ps_next()
            for t in range(8):
                mm(pt[:, t:t + 505:8], Ysb[:, ct, t, cht * 128:(cht + 1) * 128], ident[:, 64 * ch_:64 * ch_ + 64], True, True,
                   reads=[B_UYc[ct], B_ID], writes=[pb] if t == 0 else [])
            pdone(pb)
            act(YH[:, cht, 512 * tb_:512 * (tb_ + 1)], pt[:], GELU, reads=[pb], writes=[B_YH[tb_]])

    def glu_blk(tb):
        tsl = slice(tb * 512, (tb + 1) * 512)
        for v in range(4):
            pv_, pvb = ps_next()
            for cht in range(4):
                mm(pv_[:], wglu[:, cht, v * 128:(v + 1) * 128], YH[:, cht, tsl], cht == 0, cht == 3,
                   reads=[B_wglu, B_YH[tb]], writes=[pvb] if cht == 0 else [])
            pdone(pvb)
            pg_, pgb = ps_next()
            for cht in range(4):
                mm(pg_[:], wglu[:, cht, 512 + v * 128:512 + (v + 1) * 128], YH[:, cht, tsl], cht == 0, cht == 3,
                   reads=[B_wglu, B_YH[tb]], writes=[pgb] if cht == 0 else [])
            pdone(pgb)
            pz_, pzb = ps_next()
            for kt in range(8):
                mm(pz_[:], wza[:, kt, v * 128:(v + 1) * 128], xnT[:, kt, tsl], kt == 0, kt == 7,
                   reads=[B_wza] + B_xnT, writes=[pzb] if kt == 0 else [])
            pdone(pzb)
            act(sgt[:], pg_[:], AF.Sigmoid, reads=[pgb, B_bglu], writes=[B_sgt], bias=bglu[:, 4 + v:5 + v])
            act(szt[:], pz_[:], AF.Sigmoid, reads=[pzb], writes=[B_szt])
            tt("dve", szt[:], szt[:], pz_[:], ALU.mult, reads=[pzb, B_szt], writes=[B_szt])
            stt(h1t[:], pv_[:], bglu[:, v:v + 1], sgt[:], ALU.add, ALU.mult, [pvb, B_bglu, B_sgt], [B_h1t])
            tt("dve", hat[:, v, :], h1t[:], szt[:], ALU.mult, reads=[B_h1t, B_szt], writes=[B_hat])
        cp("act", YH[:, :, tsl], hat[:], reads=[B_hat], writes=[B_YH[tb]])

    B_UYc = [Buf("UYc0"), Buf("UYc1")]
    for b_ in B_UYc:
        b_.ws = list(B_UY.ws)
    y_inter(0)
    y_tr(0)
    y_tr(1)
    y_inter(1)
    glu_blk(0)
    y_tr(2)
    glu_blk(1)
    y_tr(3)
    glu_blk(2)
    glu_blk(3)
    if debug:
        dump("haT", YH[:].rearrange("p a t -> p (a t)"), B_YH[3], 128, 4 * L)
    global_barrier()
    release_to(m_ssm)
    maybe_stop("SSM")
    wo = sb("wo", [128, 8, 1024], BF16, side="left")
    B_wo = Buf("wo")
    wload(wo[:], dr["w_out"].rearrange("(kt p) n -> p kt n", p=128), B_wo, perm=True)

    maybe_stop("ATT")

    mixT_l = [sb("mixT%d" % i, [128, 8, 512], BF16) for i in range(2)]
    B_mix_l = [Buf("mixT%d" % i) for i in range(2)]
    sga_l = [sb("sga%d" % i, [128, 512], F32) for i in range(2)]
    B_sga_l = [Buf("sga%d" % i) for i in range(2)]
    sgb_l = [sb("sgb%d" % i, [128, 512], F32) for i in range(2)]
    B_sgb_l = [Buf("sgb%d" % i) for i in range(2)]
    m1_l = [sb("m1%d" % i, [128, 512], F32) for i in range(2)]
    B_m1_l = [Buf("m1%d" % i) for i in range(2)]
    szb_l = [sb("szb%d" % i, [128, 512], BF16) for i in range(2)]
    B_szb_l = [Buf("szb%d" % i) for i in range(2)]
    xres = [sb("xres%d" % i, [128, D], F32) for i in range(2)]
    B_xres = [Buf("xres%d" % i) for i in range(2)]
    B_out = Buf("out")
    out_ops = []
    pending_wout = []
    def emit_wout(tb, mixT, B_mix):
        for il in range(4):
            i = tb * 4 + il
            xr, bxr = xres[i % 2], B_xres[i % 2]
            dma("sp", xr[:], dr["x"][128 * i:128 * (i + 1), :], writes=[bxr])
            for half in range(2):
                po, pob = ps_next()
                for n in range(8):
                    mm(po[:], mixT[:, n, il * 128:(il + 1) * 128], wo[:, n, half * 512:(half + 1) * 512], n == 0, n == 7,
                       reads=[B_mix, B_wo], writes=[pob] if n == 0 else [])
                pdone(pob)
                tt("dve", xr[:, half * 512:(half + 1) * 512], xr[:, half * 512:(half + 1) * 512], po[:], ALU.add,
                   reads=[bxr, pob], writes=[bxr])
            out_ops.append(dma("sp", out_d[128 * i:128 * (i + 1), :], xr[:], reads=[bxr], writes=[]))

    for tb in range(4):
        tsl = slice(tb * 512, (tb + 1) * 512)
        mixT, B_mix = mixT_l[tb % 2], B_mix_l[tb % 2]
        for v in range(4):
            pz_, pzb = ps_next()
            for kt in range(8):
                mm(pz_[:], wzb[:, kt, v * 128:(v + 1) * 128], xnT[:, kt, tsl], kt == 0, kt == 7,
                   reads=[B_wzb] + B_xnT, writes=[pzb] if kt == 0 else [])
            pdone(pzb)
            szb, B_szb = szb_l[v % 2], B_szb_l[v % 2]
            act(szb[:], pz_[:], AF.Silu, reads=[pzb], writes=[B_szb])
            tt("dve", HB[:, v, tsl], HB[:, v, tsl], szb[:], ALU.mult, reads=[B_HB, B_szb], writes=[B_HB])
        while pending_wout:
            emit_wout(*pending_wout.pop(0))
        for n in range(8):
            pya, pyab = ps_next()
            for v in range(4):
                mm(pya[:], wps[:, v, n * 128:(n + 1) * 128], YH[:, v, tsl], v == 0, v == 3,
                   reads=[B_wps, B_YH[tb]], writes=[pyab] if v == 0 else [])
            pdone(pyab)
            pyb_, pybb = ps_next()
            for v in range(4):
                mm(pyb_[:], wpa[:, v, n * 128:(n + 1) * 128], HB[:, v, tsl], v == 0, v == 3,
                   reads=[B_wpa, B_HB], writes=[pybb] if v == 0 else [])
            pdone(pybb)
            pga, pgab = ps_next()
            for kt in range(8):
                mm(pga[:], wg[:, kt, n * 128:(n + 1) * 128], xnT[:, kt, tsl], kt == 0, kt == 7,
                   reads=[B_wg] + B_xnT, writes=[pgab] if kt == 0 else [])
            pdone(pgab)
            pgb_, pgbb = ps_next()
            for kt in range(8):
                mm(pgb_[:], wg[:, kt, 1024 + n * 128:1024 + (n + 1) * 128], xnT[:, kt, tsl], kt == 0, kt == 7,
                   reads=[B_wg] + B_xnT, writes=[pgbb] if kt == 0 else [])
            pdone(pgbb)
            sga, B_sga, sgb, B_sgb, m1, B_m1 = sga_l[n % 2], B_sga_l[n % 2], sgb_l[n % 2], B_sgb_l[n % 2], m1_l[n % 2], B_m1_l[n % 2]
            act(sga[:], pga[:], AF.Sigmoid, reads=[pgab, B_bgate], writes=[B_sga], bias=bgate[:, n:n + 1])
            act(sgb[:], pgb_[:], AF.Sigmoid, reads=[pgbb, B_bgate], writes=[B_sgb], bias=bgate[:, 8 + n:9 + n])
            tt("dve", m1[:], pya[:], sga[:], ALU.mult, reads=[pyab, B_sga], writes=[B_m1])
            tt("dve", sgb[:], pyb_[:], sgb[:], ALU.mult, reads=[pybb, B_sgb], writes=[B_sgb])
            tt("dve", mixT[:, n, :], m1[:], sgb[:], ALU.add, reads=[B_m1, B_sgb], writes=[B_mix])
        pending_wout.append((tb, mixT, B_mix))
    while pending_wout:
        emit_wout(*pending_wout.pop(0))
    fin = P.op("sp", lambda: nc.sync.nop(), dbg_bufs, [])
    for o in out_ops:
        fin.deps.add(o)
    P.emit()
    return nc


_CACHE = {}


def kernel(**inputs):
    consts = host_consts()
    if "nc" not in _CACHE:
        try:
            build(DEBUG if DEBUG else None, STOP)
        except StopBuild:
            pass
    nc = _CACHE["nc"]
    shared = {}
    for name, shape in IN_SPECS:
        if name == "x":
            continue
        a = np.asarray(inputs[name], dtype=np.float32)
        shared[name] = np.ascontiguousarray(a.reshape(shape))
    shared.update(consts)
    x = np.asarray(inputs["x"], dtype=np.float32)
    in_maps = []
    for b in range(NCORES):
        m = dict(shared)
        m["x"] = np.ascontiguousarray(x[b])
        in_maps.append(m)
    res = run_bass_kernel_spmd(nc, in_maps, core_ids=list(range(NCORES)))
    _CACHE["res"] = res
    out = np.stack([np.asarray(res.results[b]["out"], dtype=np.float32) for b in range(NCORES)], axis=0)
    return out
```

```python
import math
import numpy as np
import ml_dtypes
import concourse.bass as bass
import concourse.mybir as mybir
from concourse.bass_utils import run_bass_kernel_spmd

F32 = mybir.dt.float32
BF16 = mybir.dt.bfloat16
I32 = mybir.dt.int32
AF = mybir.ActivationFunctionType
ALU = mybir.AluOpType
AX = mybir.AxisListType

L = 2048
D = 1024
EPS = 1e-6
NCORES = 8
DEBUG = {}
STOP = None
BAN = None
GELU = AF.Gelu_apprx_tanh


class Buf:
    __slots__ = ("name", "ws", "r")

    def __init__(self, name):
        self.name = name
        self.ws = []
        self.r = []


class Op:
    __slots__ = ("eng", "fn", "deps", "signal", "sigval", "dma", "dsem", "dval", "dprev", "perm")

    def __init__(self, eng, fn, dma=False):
        self.eng = eng
        self.fn = fn
        self.deps = set()
        self.signal = False
        self.sigval = 0
        self.dma = dma
        self.dsem = None
        self.dval = 0
        self.dprev = None
        self.perm = False


class Prog:
    ENGS = ("pe", "act", "dve", "pool", "sp")

    def __init__(self, nc, n_dma_sems=24):
        self.nc = nc
        self.e = {"pe": nc.tensor, "act": nc.scalar, "dve": nc.vector, "pool": nc.gpsimd, "sp": nc.sync}
        self.ops = {k: [] for k in self.ENGS}
        self.allops = []
        self.n_dma_sems = n_dma_sems
        self.dma_rr = 0
        self.dma_last = [None] * n_dma_sems
        self.dma_cnt = [0] * n_dma_sems
        self.n_unique = 0

    def op(self, eng, fn, reads=(), writes=(), dma=False):
        o = Op(eng, fn, dma)
        for b in reads:
            for w in b.ws:
                o.deps.add(w)
        for b in writes:
            for w in b.ws:
                if not (dma and w.dma):
                    o.deps.add(w)
            for r in b.r:
                o.deps.add(r)
        for b in writes:
            if dma:
                b.ws = [w for w in b.ws if w.dma] + [o]
            else:
                b.ws = [o]
            b.r = []
        for b in reads:
            if o not in b.ws:
                b.r.append(o)
        o.deps.discard(o)
        if dma and eng == "pool":
            o.dsem = ("u", self.n_unique)
            self.n_unique += 1
            o.dval = 16
        elif dma:
            k = self.dma_rr
            self.dma_rr = (k + 1) % self.n_dma_sems
            o.dsem = k
            self.dma_cnt[k] += 1
            o.dval = 16 * self.dma_cnt[k]
            o.dprev = self.dma_last[k]
            self.dma_last[k] = o
            if o.dprev is not None:
                o.deps.add(o.dprev)
        self.ops[eng].append(o)
        self.allops.append(o)
        return o

    def barrier(self, bufs):
        pass

    def emit(self):
        nc = self.nc
        sems = {k: nc.semaphore("s_" + k).__enter__() for k in self.ENGS}
        dsems = {i: nc.semaphore("d_%d" % i).__enter__() for i in range(self.n_dma_sems)}
        for i in range(self.n_unique):
            dsems[("u", i)] = nc.semaphore("u_%d" % i).__enter__()
        for o in self.allops:
            for d in o.deps:
                if not d.dma:
                    d.signal = True
        for k in self.ENGS:
            c = 0
            for o in self.ops[k]:
                if o.signal and not o.dma:
                    c += 1
                    o.sigval = c
        for k in self.ENGS:
            eng = self.e[k]
            seen = {}
            for o in self.ops[k]:
                need = {}
                for d in o.deps:
                    if d.dma:
                        key = ("d", d.dsem)
                        val = d.dval
                    else:
                        key = ("e", d.eng)
                        val = d.sigval
                    if need.get(key, 0) < val:
                        need[key] = val
                for key, val in need.items():
                    if seen.get(key, 0) >= val:
                        continue
                    seen[key] = val
                    s = dsems[key[1]] if key[0] == "d" else sems[key[1]]
                    eng.wait_ge(s, val)
                ins = o.fn()
                if o.dma:
                    ins.then_inc(dsems[o.dsem], 16)
                elif o.signal:
                    ins.then_inc(sems[k], 1)
                    seen[("e", k)] = max(seen.get(("e", k), 0), 0)


def host_consts():
    c = {}
    ident = np.eye(128, dtype=np.float32)
    c["c_ident"] = ident.astype(ml_dtypes.bfloat16)
    c["c_identj"] = ident[::-1].copy().astype(ml_dtypes.bfloat16)
    c["c_identf"] = ident.copy()
    k = np.arange(128)[:, None]
    q = np.arange(256)[None, :]
    mb01 = np.where(np.abs((q - 64) - k) <= 64, 0.0, -30000.0).astype(np.float32)
    c["c_mb01"] = mb01.astype(ml_dtypes.bfloat16)
    q2 = np.arange(128)[None, :]
    mb2 = np.where(np.abs(q2 - k) <= 64, 0.0, -30000.0).astype(np.float32)
    c["c_mb2"] = mb2.astype(ml_dtypes.bfloat16)
    inv = 500000.0 ** (-np.arange(0, 16, 2, dtype=np.float32) / 16.0)
    pos = np.arange(L, dtype=np.float32)
    ang = pos[:, None] * inv[None, :]
    cos = np.cos(ang).astype(np.float32)
    sin = np.sin(ang).astype(np.float32)
    cc = np.concatenate([cos, cos], axis=1).reshape(16, 128, 16).transpose(1, 0, 2)
    ss = np.concatenate([-sin, sin], axis=1).reshape(16, 128, 16).transpose(1, 0, 2)
    c["c_ropec"] = np.ascontiguousarray(cc)
    c["c_ropes"] = np.ascontiguousarray(ss)
    t1 = np.zeros((128, 9, 32), np.float32)
    for j in range(9):
        t1[:64, j, :] = j
        t1[64:, j, :] = 8 - j
    c["c_tau1"] = t1
    t2 = np.zeros((128, 8, 32), np.float32)
    for s in range(8):
        t2[:64, s, :] = 7 - s
        t2[64:, s, :] = s
    c["c_tau2"] = t2
    e16 = np.zeros((128, 16), np.float32)
    for gq in range(4):
        for h in range(16):
            e16[32 * gq + h, h] = 1.0
    c["c_e16"] = e16
    return c


CONST_SPECS = [("c_ident", [128, 128], BF16), ("c_identj", [128, 128], BF16), ("c_identf", [128, 128], F32),
               ("c_mb01", [128, 256], BF16), ("c_mb2", [128, 128], BF16),
               ("c_ropec", [128, 16, 16], F32), ("c_ropes", [128, 16, 16], F32),
               ("c_tau1", [128, 9, 32], F32), ("c_tau2", [128, 8, 32], F32), ("c_e16", [128, 16], F32)]

IN_SPECS = [("x", [L, D]), ("norm_w", [D]), ("w_in", [D, 6144]), ("b_gate", [2048]), ("q_norm_w", [64]),
            ("k_norm_w", [64]), ("ssm_lam_re", [2, 32, 64]), ("ssm_lam_im", [2, 32, 64]), ("ssm_log_dt", [2, 32]),
            ("ssm_b_re", [2, 32, 64, 16]), ("ssm_b_im", [2, 32, 64, 16]), ("ssm_c_re", [2, 32, 16, 64]),
            ("ssm_c_im", [2, 32, 16, 64]), ("ssm_d", [512]), ("w_glu", [512, 1024]), ("b_glu", [1024]),
            ("w_proj_ssm", [512, 1024]), ("w_proj_attn", [512, 1024]), ("w_out", [D, D])]


class StopBuild(Exception):
    pass


def build(debug=None, stop=None):
    nc = bass.Bass("TRN2", target_bir_lowering=False, dynamic_dma_scratch_size=4096)
    _CACHE['nc'] = nc
    P = Prog(nc)
    dr = {}
    for name, shape in IN_SPECS:
        dr[name] = nc.dram_tensor(name, shape, F32, kind="ExternalInput").ap()
    for name, shape, dt in CONST_SPECS:
        dr[name] = nc.dram_tensor(name, shape, dt, kind="ExternalInput").ap()
    out_d = nc.dram_tensor("out", [L, D], F32, kind="ExternalOutput").ap()
    dbg_d = {}
    if debug:
        for name, shape in debug.items():
            dbg_d[name] = nc.dram_tensor("dbg_" + name, shape, F32, kind="ExternalOutput").ap()

    stacks = {"left": [], "right": []}

    def sb(name, shape, dt, side="right"):
        t = nc.sbuf_tensor(name, shape, dt, side=side)
        h = t.__enter__()
        stacks[side].append(t)
        return h

    def mark(side="right"):
        return len(stacks[side])

    def release_to(n, side="right"):
        while len(stacks[side]) > n:
            stacks[side].pop().__exit__(None, None, None)

    psum = []
    for i in range(8):
        t = nc.psum_tensor("ps%d" % i, [128, 512], F32)
        psum.append((t.__enter__(), Buf("ps%d" % i)))
    ps_rr = [0]

    def ps_next():
        i = ps_rr[0]
        ps_rr[0] = (i + 1) % 8
        return psum[i]

    def dma(eng, out, in_, reads=(), writes=(), **kw):
        q = P.e[eng]
        return P.op(eng, lambda: q.dma_start(out=out, in_=in_, **kw), reads, writes, dma=True)

    def mm(out, lhsT, rhs, start, stop, reads=(), writes=(), tp=None):
        if tp is None:
            return P.op("pe", lambda: nc.tensor.matmul(out, lhsT, rhs, start=start, stop=stop, skip_group_check=True),
                        reads, writes)
        return P.op("pe", lambda: nc.tensor.matmul(out, lhsT, rhs, start=start, stop=stop, skip_group_check=True,
                                                   tile_position=tp), reads, writes)

    def tr32(out, in_, idn, reads=(), writes=()):
        return P.op("pe", lambda: nc.tensor.transpose(out, in_, idn), reads, writes)

    def act(out, in_, func, reads=(), writes=(), **kw):
        return P.op("act", lambda: nc.scalar.activation(out=out, in_=in_, func=func, **kw), reads, writes)

    def tt(eng, out, in0, in1, op, reads=(), writes=()):
        e = P.e[eng]
        return P.op(eng, lambda: e.tensor_tensor(out=out, in0=in0, in1=in1, op=op), reads, writes)

    def ts(eng, out, in0, s1, op0, s2=None, op1=None, reads=(), writes=()):
        e = P.e[eng]
        if op1 is None:
            return P.op(eng, lambda: e.tensor_scalar(out=out, in0=in0, scalar1=s1, scalar2=None, op0=op0), reads, writes)
        return P.op(eng, lambda: e.tensor_scalar(out=out, in0=in0, scalar1=s1, scalar2=s2, op0=op0, op1=op1), reads, writes)

    def stt(out, in0, scalar, in1, op0, op1, reads=(), writes=()):
        return P.op("dve", lambda: nc.vector.scalar_tensor_tensor(out=out, in0=in0, scalar=scalar, in1=in1, op0=op0, op1=op1),
                    reads, writes)

    def recip(out, in_, reads=(), writes=()):
        return P.op("dve", lambda: nc.vector.reciprocal(out=out, in_=in_), reads, writes)

    def cp(eng, out, in_, reads=(), writes=()):
        if eng == "act":
            return P.op("act", lambda: nc.scalar.activation(out=out, in_=in_, func=AF.Copy), reads, writes)
        e = P.e[eng]
        return P.op(eng, lambda: e.tensor_copy(out=out, in_=in_), reads, writes)

    def memset(eng, ap, val, writes=()):
        e = P.e[eng]
        return P.op(eng, lambda: e.memset(ap, val), (), writes)

    def pdone(pb):
        pb.ws = [P.allops[-1]]

    dbg_bufs = []

    def dump(name, src_ap, src_buf, parts, cols):
        if not debug or name not in debug:
            return
        tmp = sb("dbgtmp_" + name, [128, cols], F32, side="left")
        tb = Buf("dbgtmp_" + name)
        cp("dve", tmp[0:parts, :], src_ap, reads=[src_buf], writes=[tb])
        ob = Buf("dbgout_" + name)
        dma("sp", dbg_d[name][0:parts, :], tmp[0:parts, :], reads=[tb], writes=[ob])
        dbg_bufs.append(ob)

    evac_rr = [0]

    def evac_eng():
        evac_rr[0] ^= 1
        return "act" if evac_rr[0] else "dve"

    barrier_mark = [0]

    def global_barrier():
        lasts = []
        for k in Prog.ENGS:
            for o_ in reversed(P.ops[k]):
                if not (o_.dma and o_.perm):
                    lasts.append(o_)
                    break
        n_prev = len(P.allops)
        bb = Buf("barrier")
        o = P.op("sp", lambda: nc.sync.nop(), [], [bb])
        for l in lasts:
            if l is not o:
                o.deps.add(l)
        for d_ in P.allops[barrier_mark[0]:n_prev]:
            if d_.dma and not d_.perm:
                o.deps.add(d_)
        barrier_mark[0] = n_prev
        for k in ("pe", "act", "dve", "pool"):
            P.op(k, (lambda k=k: P.e[k].nop()), [bb], [])
        P.op("sp", lambda: nc.sync.nop(), [bb], [])

    def maybe_stop(name):
        if stop == name:
            P.op("sp", lambda: nc.sync.nop(), dbg_bufs, [])
            P.emit()
            raise StopBuild()

    late_dmas = []

    def load_const(name, side="left", late=False):
        shape, dt = [(s, d) for (n, s, d) in CONST_SPECS if n == name][0]
        t = sb("s_" + name, shape, dt, side=side)
        b = Buf(name)
        if late:
            late_dmas.append(lambda: dma("sp", t[:], dr[name], writes=[b]))
        else:
            dma("sp", t[:], dr[name], writes=[b])
        return t, b

    ident, B_ID = load_const("c_ident")
    identj, B_IDJ = load_const("c_identj", late=True)
    identf, B_IDF = load_const("c_identf")
    mb01, B_mb01 = load_const("c_mb01", late=True)
    mb2, B_mb2 = load_const("c_mb2", late=True)
    ropec, B_ropec = load_const("c_ropec", late=True)
    ropes, B_ropes = load_const("c_ropes", late=True)
    bgate = sb("bgate", [128, 16], F32, side="left")
    B_bgate = Buf("bgate")
    bglu = sb("bglu", [128, 8], F32, side="left")
    B_bglu = Buf("bglu")
    wqk = sb("wqk", [128, 8, 64], F32, side="left")
    B_wqk = Buf("wqk")
    for blk in range(8):
        late_dmas.append(lambda blk=blk: dma("sp", wqk[:, blk, :], (dr["q_norm_w"] if blk < 6 else dr["k_norm_w"]).unsqueeze(0).broadcast_to([128, 64]),
                                             writes=[B_wqk]))
    xnT = sb("xnT", [128, 8, L], BF16, side="left")
    B_xnT = [Buf("xnT%d" % i) for i in range(16)]
    HB = sb("HB", [128, 4, L], BF16, side="left")
    B_HB = Buf("HB")
    B_YH = [Buf("YH%d" % i) for i in range(4)]
    ARENA = sb("ARENA", [128, 8192], BF16, side="left")
    B_wglu = Buf("wglu")
    B_wza = Buf("wza")
    m_left_ssm = mark("left")
    CAre = sb("CAre", [128, 32, 9, 16], BF16, side="left")
    NCAim = sb("NCAim", [128, 32, 9, 16], BF16, side="left")
    Wt = sb("Wt", [128, 32, 128], BF16, side="left")
    Bex = sb("Bex", [128, 32, 2, 128], BF16, side="left")
    wU = sb("wU", [128, 8, 512], BF16, side="left")
    B_wU = Buf("wU")

    m0_left = mark("left")
    normw = sb("normw", [128, D], F32, side="left")
    B_normw = Buf("normw")
    dma("sp", normw[:], dr["norm_w"].unsqueeze(0).broadcast_to([128, D]), writes=[B_normw])
    NXI, NXB = 4, 3
    xin = [sb("xin%d" % i, [128, D], F32, side="left") for i in range(NXI)]
    B_xin = [Buf("xin%d" % i) for i in range(NXI)]
    junk = sb("junk", [128, D], BF16, side="left")
    B_junk = Buf("junk")
    xnb = [sb("xnb%d" % i, [128, D], BF16, side="left") for i in range(NXB)]
    B_xnb = [Buf("xnb%d" % i) for i in range(NXB)]
    stat = sb("stat", [128, 16, 4], F32, side="left")
    B_stat = [Buf("stat%d" % i) for i in range(16)]
    pa_st = {}

    def pa_load(i):
        xt, bx = xin[i % NXI], B_xin[i % NXI]
        dma("act" if i == 0 else "pool", xt[:], dr["x"][128 * i:128 * (i + 1), :], writes=[bx])

    def pa_sq(i):
        xt, bx = xin[i % NXI], B_xin[i % NXI]
        act(junk[:], xt[:], AF.Square, reads=[bx], writes=[B_junk, B_stat[i]], accum_out=stat[:, i, 0:1])

    def pa_ts(i):
        pass

    def pa_sqrt(i):
        act(stat[:, i, 2:3], stat[:, i, 0:1], AF.Sqrt, reads=[B_stat[i]], writes=[B_stat[i]], scale=1.0 / D, bias=EPS)

    def pa_scale(i):
        xt, bx = xin[i % NXI], B_xin[i % NXI]
        xb, bxb = xnb[i % NXB], B_xnb[i % NXB]
        recip(stat[:, i, 3:4], stat[:, i, 2:3], [B_stat[i]], [B_stat[i]])
        stt(xb[:], xt[:], stat[:, i, 3:4], normw[:], ALU.mult, ALU.mult, [bx, B_stat[i], B_normw], [bxb])

    def pa_tr(i):
        xb, bxb = xnb[i % NXB], B_xnb[i % NXB]
        for half in range(2):
            pt, pb = ps_next()
            for j in range(4):
                kt = half * 4 + j
                mm(pt[:, j * 128:(j + 1) * 128], xb[:, kt * 128:(kt + 1) * 128], ident[:], True, True,
                   reads=[bxb, B_ID], writes=[pb] if j == 0 else [])
            pdone(pb)
            pa_st[(i, half)] = (pt, pb)

    def pa_ev(i):
        for half in range(2):
            pt, pb = pa_st[(i, half)]
            cp("act", xnT[:, half * 4:half * 4 + 4, 128 * i:128 * (i + 1)],
               pt[:].rearrange("p (a n) -> p a n", a=4), reads=[pb], writes=[B_xnT[i]])

    def phaseA_gen():
        stages_a = ((pa_load, 0), (pa_sq, 1), (pa_ts, 2), (pa_sqrt, 2), (pa_scale, 3), (pa_tr, 4), (pa_ev, 5))
        for t_ in range(16 + 5):
            for (f_, lag_) in stages_a:
                if 0 <= t_ - lag_ < 16:
                    f_(t_ - lag_)
            yield

    genA = phaseA_gen()
    hook_state = {"on": False, "busy": False, "cnt": 0}
    _orig_op = P.op

    def _hooked_op(eng, fn, reads=(), writes=(), dma=False):
        o = _orig_op(eng, fn, reads, writes, dma)
        if hook_state["on"] and not hook_state["busy"] and eng == "dve":
            hook_state["cnt"] += 1
            if hook_state["cnt"] % 5 == 0:
                hook_state["busy"] = True
                keep = P.allops[-1]
                next(genA, None)
                hook_state["busy"] = False
        return o

    P.op = _hooked_op
    for _ in range(7):
        next(genA, None)
    hook_state["on"] = True
    if debug:
        dump("xnT", xnT[:, 0, :], B_xnT[15], 128, L)
    maybe_stop("A")

    def wload(dst_ap, src_ap, buf, perm=False):
        o_ = dma("pool", dst_ap, src_ap, writes=[buf])
        o_.perm = perm
        return o_

    def w_in_cols(dst, c0, ncols, buf, perm=False):
        wload(dst, dr["w_in"][:, c0:c0 + ncols].rearrange("(kt p) n -> p kt n", p=128), buf, perm)

    w_in_cols(wU[:], 0, 512, B_wU)

    m_ssm = mark()
    B_CA = Buf("CA")
    B_W = Buf("W")
    B_Bex = Buf("Bex")
    A8 = sb("A8", [128, 32, 2], F32)
    A8s = sb("A8s", [128, 32, 2], F32)
    B_A8 = Buf("A8")
    CAY = sb("CAY", [128, 2, 32, 8, 16], BF16)
    B_CAY = Buf("CAY")
    m_pre = mark()
    BBb = sb("BBb", [128, 2, 32, 16], BF16)
    B_BBb = Buf("BBb")
    m_k = mark()
    tau1, B_tau1 = load_const("c_tau1", side="right")
    tau2, B_tau2 = load_const("c_tau2", side="right")
    e16, B_e16 = load_const("c_e16", side="right")
    LL = sb("LL", [32, 2, 128], F32)
    B_LL = Buf("LL")
    for ri, nm in enumerate(("ssm_lam_re", "ssm_lam_im")):
        for d_ in range(2):
            dma("sp", LL[:, ri, d_ * 64:(d_ + 1) * 64], dr[nm][d_], writes=[B_LL])
    LRI = sb("LRI", [128, 2, 32], F32)
    B_LRI = Buf("LRI")
    pt, pb = ps_next()
    for ri in range(2):
        tr32(pt[:, ri * 32:(ri + 1) * 32], LL[:, ri, :], identf[0:32, 0:32], reads=[B_LL, B_IDF], writes=[pb] if ri == 0 else [])
    pdone(pb)
    cp("dve", LRI[:], pt[:, 0:64].rearrange("p (a g) -> p a g", a=2), reads=[pb], writes=[B_LRI])
    DT = sb("DT", [128, 32], F32)
    B_DT = Buf("DT")
    for d_ in range(2):
        dma("sp", DT[d_ * 64:(d_ + 1) * 64, :], dr["ssm_log_dt"][d_:d_ + 1, :].broadcast_to([64, 32]), writes=[B_DT])
    act(DT[:], DT[:], AF.Exp, reads=[B_DT], writes=[B_DT])
    ts("dve", LRI[:, 0, :], LRI[:, 0, :], -1e-4, ALU.min, reads=[B_LRI], writes=[B_LRI])
    E1 = sb("E1", [128, 2, 32], F32)
    B_E1 = Buf("E1")
    tt("dve", E1[:], LRI[:], DT[:].unsqueeze(1).broadcast_to([128, 2, 32]), ALU.mult, reads=[B_LRI, B_DT], writes=[B_E1])

    pex = sb("pw_ex", [128, 9, 32], F32)
    pan = sb("pw_an", [128, 2, 9, 32], F32)
    pki = sb("pw_ki", [128, 2, 9, 32], I32)
    pkf = sb("pw_kf", [128, 2, 9, 32], F32)
    pcm = sb("pw_cm", [128, 2, 9, 32], F32)
    bs = Buf("pw_scratch")

    def power_table(name, tau, btau, nj):
        AR = sb(name + "r", [128, nj, 32], F32)
        AI = sb(name + "i", [128, nj, 32], F32)
        bt = Buf(name)
        ex, an, ki, kf, cm = pex[:, 0:nj], pan[:, :, 0:nj], pki[:, :, 0:nj], pkf[:, :, 0:nj], pcm[:, :, 0:nj]
        e1b = E1[:, 0, :].unsqueeze(1).broadcast_to([128, nj, 32])
        thb = E1[:, 1, :].unsqueeze(1).broadcast_to([128, nj, 32])
        tt("dve", ex, tau[:], e1b, ALU.mult, reads=[btau, B_E1], writes=[bs])
        act(ex, ex, AF.Exp, reads=[bs], writes=[bs])
        tt("dve", an[:, 0], tau[:], thb, ALU.mult, reads=[btau, B_E1, bs], writes=[bs])
        c_hi = float(np.float32(1.0 / (2 * math.pi)))
        c_lo = 1.0 / (2 * math.pi) - c_hi
        ts("dve", cm[:, 0], an[:, 0], c_lo, ALU.mult, reads=[bs], writes=[bs])
        ts("dve", an[:, 1], an[:, 0], c_hi, ALU.mult, 0.25, ALU.add, reads=[bs], writes=[bs])
        ts("dve", an[:, 0], an[:, 0], c_hi, ALU.mult, reads=[bs], writes=[bs])
        tt("dve", an[:, 0], an[:, 0], cm[:, 0], ALU.add, reads=[bs], writes=[bs])
        tt("dve", an[:, 1], an[:, 1], cm[:, 0], ALU.add, reads=[bs], writes=[bs])
        cp("dve", ki, an, reads=[bs], writes=[bs])
        cp("dve", kf, ki, reads=[bs], writes=[bs])
        tt("dve", an, an, kf, ALU.subtract, reads=[bs], writes=[bs])
        ts("dve", cm, an, 0.5, ALU.is_gt, reads=[bs], writes=[bs])
        tt("dve", an, an, cm, ALU.subtract, reads=[bs], writes=[bs])
        ts("dve", cm, an, -0.5, ALU.is_lt, reads=[bs], writes=[bs])
        tt("dve", an, an, cm, ALU.add, reads=[bs], writes=[bs])
        act(an, an, AF.Sin, reads=[bs], writes=[bs], scale=2 * math.pi)
        tt("dve", AI[:], ex, an[:, 0], ALU.mult, reads=[bs], writes=[bt])
        tt("dve", AR[:], ex, an[:, 1], ALU.mult, reads=[bs], writes=[bt])
        return AR, AI, bt

    AR1, AI1, B_A1 = power_table("pw1", tau1, B_tau1, 9)
    AR2, AI2, B_A2 = power_table("pw2", tau2, B_tau2, 8)
    for (lo, hi, j) in ((0, 64, 8), (64, 128, 0)):
        cp("dve", A8[lo:hi, :, :], AR1[lo:hi, j, :].unsqueeze(2).broadcast_to([hi - lo, 32, 2]), reads=[B_A1], writes=[B_A8])
        ts("dve", A8s[lo:hi, :, 0:1], AI1[lo:hi, j, :].unsqueeze(2), -1.0, ALU.mult, reads=[B_A1], writes=[B_A8])
        cp("dve", A8s[lo:hi, :, 1:2], AI1[lo:hi, j, :].unsqueeze(2), reads=[B_A1], writes=[B_A8])
    maybe_stop("PRE1")
    ZZ = sb("ZZ", [128, 8, 32], F32)
    B_ZZ = Buf("ZZ")
    for (lo, hi, j) in ((0, 64, 1), (64, 128, 7)):
        ts("dve", ZZ[lo:hi, 0, :], AR1[lo:hi, j, :], -1.0, ALU.add, reads=[B_A1], writes=[B_ZZ])
        cp("dve", ZZ[lo:hi, 1, :], AI1[lo:hi, j, :], reads=[B_A1], writes=[B_ZZ])
    lr_, li_ = LRI[:, 0, :], LRI[:, 1, :]
    RW = [B_ZZ, B_LRI]
    tt("dve", ZZ[:, 2, :], lr_, lr_, ALU.mult, reads=RW, writes=[B_ZZ])
    tt("dve", ZZ[:, 3, :], li_, li_, ALU.mult, reads=RW, writes=[B_ZZ])
    tt("dve", ZZ[:, 2, :], ZZ[:, 2, :], ZZ[:, 3, :], ALU.add, reads=RW, writes=[B_ZZ])
    recip(ZZ[:, 2, :], ZZ[:, 2, :], RW, [B_ZZ])
    tt("dve", ZZ[:, 4, :], ZZ[:, 0, :], lr_, ALU.mult, reads=RW, writes=[B_ZZ])
    tt("dve", ZZ[:, 6, :], ZZ[:, 1, :], li_, ALU.mult, reads=RW, writes=[B_ZZ])
    tt("dve", ZZ[:, 4, :], ZZ[:, 4, :], ZZ[:, 6, :], ALU.add, reads=RW, writes=[B_ZZ])
    tt("dve", ZZ[:, 4, :], ZZ[:, 4, :], ZZ[:, 2, :], ALU.mult, reads=RW, writes=[B_ZZ])
    tt("dve", ZZ[:, 5, :], ZZ[:, 1, :], lr_, ALU.mult, reads=RW, writes=[B_ZZ])
    tt("dve", ZZ[:, 7, :], ZZ[:, 0, :], li_, ALU.mult, reads=RW, writes=[B_ZZ])
    tt("dve", ZZ[:, 5, :], ZZ[:, 5, :], ZZ[:, 7, :], ALU.subtract, reads=RW, writes=[B_ZZ])
    tt("dve", ZZ[:, 5, :], ZZ[:, 5, :], ZZ[:, 2, :], ALU.mult, reads=RW, writes=[B_ZZ])
    Braw = sb("Braw", [128, 2, 32, 16], F32)
    B_Braw = Buf("Braw")
    for ri, nm in enumerate(("ssm_b_re", "ssm_b_im")):
        for d_ in range(2):
            for g4 in range(0, 32, 4):
                dma("sp", Braw[d_ * 64:(d_ + 1) * 64, ri, g4:g4 + 4], dr[nm][d_][g4:g4 + 4].rearrange("g p h -> p g h"),
                    writes=[B_Braw])
    BB = sb("BB", [128, 2, 32, 16], F32)
    B_BB = Buf("BB")
    SC1 = sb("SC1", [128, 1152], F32)
    SC2 = sb("SC2", [128, 1152], F32)
    B_SC = Buf("SC")
    Tm = SC1[:, 0:1024].rearrange("p (a g h) -> p a g h", a=2, g=32)
    fre = ZZ[:, 4, :].unsqueeze(2).broadcast_to([128, 32, 16])
    fim = ZZ[:, 5, :].unsqueeze(2).broadcast_to([128, 32, 16])
    tt("dve", BB[:, 0], Braw[:, 0], fre, ALU.mult, reads=[B_Braw, B_ZZ], writes=[B_BB])
    tt("dve", Tm[:, 0], Braw[:, 1], fim, ALU.mult, reads=[B_Braw, B_ZZ], writes=[B_SC])
    tt("dve", BB[:, 0], BB[:, 0], Tm[:, 0], ALU.subtract, reads=[B_SC, B_BB], writes=[B_BB])
    tt("dve", BB[:, 1], Braw[:, 1], fre, ALU.mult, reads=[B_Braw, B_ZZ, B_BB], writes=[B_BB])
    tt("dve", Tm[:, 1], Braw[:, 0], fim, ALU.mult, reads=[B_Braw, B_ZZ, B_SC], writes=[B_SC])
    tt("dve", BB[:, 1], BB[:, 1], Tm[:, 1], ALU.add, reads=[B_SC, B_BB], writes=[B_BB])
    cp("dve", BBb[:], BB[:], reads=[B_BB], writes=[B_BBb])
    CC = sb("CC", [128, 2, 4, 128], F32)
    B_CC = Buf("CC")
    for ri, nm in enumerate(("ssm_c_re", "ssm_c_im")):
        for d_ in range(2):
            dma("sp", CC[:, ri, :, d_ * 64:(d_ + 1) * 64],
                dr[nm][d_].rearrange("(gq g8) h p -> (g8 h) gq p", gq=4), writes=[B_CC])
    CT = sb("CT", [128, 2, 32, 16], F32)
    B_CT = Buf("CT")
    for ri in range(2):
        pt, pb = ps_next()
        for gq in range(4):
            tr32(pt[:, gq * 128:(gq + 1) * 128], CC[:, ri, gq, :], identf[:], reads=[B_CC, B_IDF], writes=[pb] if gq == 0 else [])
        pdone(pb)
        cp("dve", CT[:, ri].rearrange("p g h -> p (g h)"), pt[:], reads=[pb], writes=[B_CT])
    maybe_stop("PRE2")
    for f_ in late_dmas:
        f_()
    dma("sp", bgate[:], dr["b_gate"].rearrange("(n p) -> p n", p=128), writes=[B_bgate], allow_slow_non_contiguous=True)
    dma("sp", bglu[:], dr["b_glu"].rearrange("(n p) -> p n", p=128), writes=[B_bglu], allow_slow_non_contiguous=True)
    for gqr in range(4):
        gs = slice(gqr * 8, gqr * 8 + 8)
        T1 = SC1[:].rearrange("p (g j h) -> p g j h", g=8, j=9)
        T2 = SC2[:].rearrange("p (g j h) -> p g j h", g=8, j=9)
        cre = CT[:, 0, gs, :].unsqueeze(2).broadcast_to([128, 8, 9, 16])
        cim = CT[:, 1, gs, :].unsqueeze(2).broadcast_to([128, 8, 9, 16])
        arb = AR1[:, :, gs].rearrange("p j g -> p g j").unsqueeze(3).broadcast_to([128, 8, 9, 16])
        aib = AI1[:, :, gs].rearrange("p j g -> p g j").unsqueeze(3).broadcast_to([128, 8, 9, 16])
        RD = [B_CT, B_A1, B_SC]
        tt("dve", T1, cre, arb, ALU.mult, reads=RD, writes=[B_SC])
        tt("dve", T2, cim, aib, ALU.mult, reads=RD, writes=[B_SC])
        tt("dve", CAre[:, gs], T1, T2, ALU.subtract, reads=RD, writes=[B_CA])
        tt("dve", T1, cre, aib, ALU.mult, reads=RD + [B_CA], writes=[B_SC])
        tt("dve", T2, cim, arb, ALU.mult, reads=RD, writes=[B_SC])
        stt(NCAim[:, gs], T1, -1.0, T2, ALU.mult, ALU.subtract, RD, [B_CA])
    maybe_stop("PRE2b")
    BAq = sb("BAq", [128, 2, 8, 8, 16], BF16)
    ba_cnt = [0]
    B_BAq = Buf("BAq")
    for gqr in range(4):
        gs = slice(gqr * 8, gqr * 8 + 8)
        T1 = SC1[:, 0:1024].rearrange("p (g j h) -> p g j h", g=8, j=8)
        T2 = SC2[:, 0:1024].rearrange("p (g j h) -> p g j h", g=8, j=8)
        bre = BB[:, 0, gs, :].unsqueeze(2).broadcast_to([128, 8, 8, 16])
        bim = BB[:, 1, gs, :].unsqueeze(2).broadcast_to([128, 8, 8, 16])
        arb = AR2[:, :, gs].rearrange("p j g -> p g j").unsqueeze(3).broadcast_to([128, 8, 8, 16])
        aib = AI2[:, :, gs].rearrange("p j g -> p g j").unsqueeze(3).broadcast_to([128, 8, 8, 16])
        RD = [B_BB, B_A2, B_SC]
        ba_ops = [
            lambda: tt("dve", T1, bre, arb, ALU.mult, reads=RD, writes=[B_SC]),
            lambda: tt("dve", T2, bim, aib, ALU.mult, reads=RD, writes=[B_SC]),
            lambda: tt("dve", BAq[:, 0], T1, T2, ALU.subtract, reads=RD, writes=[B_BAq]),
            lambda: tt("dve", T1, bre, aib, ALU.mult, reads=RD + [B_BAq], writes=[B_SC]),
            lambda: tt("dve", T2, bim, arb, ALU.mult, reads=RD, writes=[B_SC]),
            lambda: tt("dve", BAq[:, 1], T1, T2, ALU.add, reads=RD, writes=[B_BAq]),
        ]
        for f_ in ba_ops:
            if BAN is not None and ba_cnt[0] >= BAN:
                maybe_stop("PRE3x")
            f_()
            ba_cnt[0] += 1
        for g2 in range(0, 8, 2):
            if STOP == "PRE3x":
                continue
            pt, pb = ps_next()
            k = 0
            for gg in (g2, g2 + 1):
                for ri in range(2):
                    mm(pt[:, k * 128:(k + 1) * 128], BAq[:, ri, gg].rearrange("p s h -> p (s h)"), ident[:], True, True,
                       reads=[B_BAq, B_ID], writes=[pb] if k == 0 else [])
                    k += 1
            pdone(pb)
            g = gqr * 8 + g2
            cp("act", Bex[:, g:g + 2].rearrange("p g r q -> p (g r q)"), pt[:], reads=[pb], writes=[B_Bex])
    maybe_stop("PRE3")
    maybe_stop("PRE3x")
    hook_state["on"] = False
    for _ in genA:
        pass
    global_barrier()
    release_to(m_k)
    release_to(m0_left, side="left")
    UT = sb("UT", [128, 32, 256], BF16, side="left")
    B_UTg = [Buf("UT%d" % i) for i in range(16)]
    U = sb("U", [128, 2, 32, 8, 16], BF16)
    B_Us = [Buf("U%d" % i) for i in range(16)]
    for ct in range(2):
        for s in range(8):
            pt, pb = ps_next()
            c0 = 1024 * ct + s
            for kt in range(8):
                mm(pt[:], xnT[:, kt, c0:c0 + 1017:8], wU[:, kt, :], kt == 0, kt == 7,
                   reads=B_xnT + [B_wU], writes=[pb] if kt == 0 else [])
            pdone(pb)
            cp("act", U[:, ct, :, s, :], pt[:].rearrange("p (g h) -> p g h", g=32), reads=[pb], writes=[B_Us[ct * 8 + s]])
    if debug:
        dump("U", U[:].rearrange("p a g s c -> p (a g s c)"), B_Us[15], 128, 8192)
    for g0 in range(0, 32, 2):
        pt, pb = ps_next()
        k = 0
        for g in (g0, g0 + 1):
            for ct in range(2):
                mm(pt[:, k * 128:(k + 1) * 128], U[:, ct, g].rearrange("p s h -> p (s h)"), ident[:], True, True,
                   reads=B_Us + [B_ID], writes=[pb] if k == 0 else [])
                k += 1
        pdone(pb)
        cp("act", UT[:, g0:g0 + 2, :].rearrange("p g c -> p (g c)"), pt[:], reads=[pb], writes=[B_UTg[g0 // 2]])
    Kall = sb("Kall", [16, 32, 15, 16], BF16)
    B_Kall_l = [Buf("Kall%d" % i) for i in range(8)]
    dT = sb("dT", [16, 32], F32)
    B_dT = Buf("dT")
    dma("sp", dT[:], dr["ssm_d"].rearrange("(g h) -> h g", h=16), writes=[B_dT], allow_slow_non_contiguous=True)
    Dm = sb("Dm", [16, 32, 16], F32)
    B_Dm = Buf("Dm")
    tt("dve", Dm[:], identf[0:16, 0:16].unsqueeze(1).broadcast_to([16, 32, 16]), dT[:].unsqueeze(2).broadcast_to([16, 32, 16]),
       ALU.mult, reads=[B_IDF, B_dT], writes=[B_Dm])
    Ktmp = sb("Ktmp", [16, 4, 16], F32)
    B_Ktmp = Buf("Ktmp")
    BBz = sb("BBz", [128, 32, 2, 128], BF16)
    B_BBz = Buf("BBz")
    memset("pool", BBz[:], 0.0, writes=[B_BBz])
    for ri in range(2):
        cp("dve", BBz[0:64, :, ri, 0:16], BBb[0:64, ri, :, :], reads=[B_BBb, B_BBz], writes=[B_BBz])
        cp("dve", BBz[64:128, :, ri, 32:48], BBb[64:128, ri, :, :], reads=[B_BBb, B_BBz], writes=[B_BBz])
    CAK = sb("CAK", [128, 2, 32, 8, 16], BF16)
    B_CAK = Buf("CAK")
    for k_, src in enumerate((CAre, NCAim)):
        cp("dve", CAK[0:64, k_], src[0:64, :, 0:8, :], reads=[B_CA], writes=[B_CAK])
        cp("dve", CAK[64:128, k_], src[64:128, :, 1:9, :], reads=[B_CA, B_CAK], writes=[B_CAK])
        cp("act", CAY[0:64, k_], src[0:64, :, 1:9, :], reads=[B_CA], writes=[B_CAY])
        cp("act", CAY[64:128, k_], src[64:128, :, 0:8, :], reads=[B_CA, B_CAY], writes=[B_CAY])
    maybe_stop("PRE3b")
    for g0 in range(0, 32, 4):
        pt, pb = ps_next()
        for gl in range(4):
            g = g0 + gl
            o_ = pt[:, gl * 128:(gl + 1) * 128]
            mm(o_, BBz[:, g, 0, :], CAK[:, 0, g].rearrange("p j h -> p (j h)"), True, False,
               reads=[B_BBz, B_CAK], writes=[pb] if gl == 0 else [])
            mm(o_, BBz[:, g, 1, :], CAK[:, 1, g].rearrange("p j h -> p (j h)"), False, True, reads=[B_BBz, B_CAK])
        pdone(pb)
        pf = pt[0:16, :].rearrange("p (a i h) -> p a i h", a=4, i=8)
        pbw = pt[32:48, :].rearrange("p (a i h) -> p a i h", a=4, i=8)
        gsl = slice(g0, g0 + 4)
        B_Kall = B_Kall_l[g0 // 4]
        cp("dve", Kall[:, gsl, 0:7, :], pbw[:, :, 0:7, :], reads=[pb], writes=[B_Kall])
        cp("dve", Kall[:, gsl, 8:15, :], pf[:, :, 1:8, :], reads=[pb], writes=[B_Kall])
        tt("dve", Ktmp[:], pf[:, :, 0, :], Dm[:, gsl, :], ALU.add, reads=[pb, B_Dm, B_Ktmp], writes=[B_Ktmp])
        tt("dve", Kall[:, gsl, 7, :], pbw[:, :, 7, :], Ktmp[:], ALU.add, reads=[pb, B_Ktmp], writes=[B_Kall])
        if g0 in (12, 28):
            hsl = slice(g0 - 12, g0 + 4)
            for s_ in range(8):
                dma("sp", Wt[s_ * 16:(s_ + 1) * 16, hsl, :].rearrange("p g (t h) -> p g t h", t=8),
                    Kall[:, hsl, 7 - s_:15 - s_, :], reads=B_Kall_l[(g0 - 12) // 4:(g0 + 4) // 4], writes=[B_W])
    maybe_stop("PRE4")
    if debug:
        dump("Wt", Wt[:].rearrange("p g c -> p (g c)"), B_W, 128, 4096)
        dump("CAre", CAre[:].rearrange("p g j h -> p (g j h)"), B_CA, 128, 4608)
        dump("Bex", Bex[:].rearrange("p g r q -> p (g r q)"), B_Bex, 128, 8192)
        dump("A8", A8[:].rearrange("p g r -> p (g r)"), B_A8, 128, 64)
    global_barrier()
    release_to(m_pre)
    maybe_stop("PRE")

    UY = sb("UY", [128, 8192], BF16)
    B_UY = Buf("UY")
    Ysb = UY[:].rearrange("p (a t c) -> p a t c", a=2, t=8)
    SX = sb("SX", [128, 32, 2, 258], BF16)
    B_SXf = [Buf("SXf%d" % i) for i in range(64)]
    B_SXb = [Buf("SXb%d" % i) for i in range(64)]
    B_SXz = Buf("SXz")
    B_SXall = B_SXf + B_SXb + [B_SXz]
    B_Sg = [Buf("Sg%d" % i) for i in range(32)]
    memset("dve", SX[:, :, :, 0:2], 0.0, writes=B_SXall)
    memset("dve", SX[:, :, :, 256:258], 0.0, writes=B_SXall)
    m_merged = mark()
    WA = ARENA[:, 0:5120].rearrange("p (k b n) -> p k b n", k=8, b=5)
    B_WA = Buf("WA")

    def load_WA(hp_):
        for blk, c0 in enumerate((1024 + hp_ * 128, 1536 + hp_ * 128, 2048 + hp_ * 128, 2560 + hp_ * 128, 3072 + hp_ * 128)):
            wload(WA[:, :, blk, :], dr["w_in"][:, c0:c0 + 128].rearrange("(kt p) n -> p kt n", p=128), B_WA, perm=True)

    load_WA(0)
    KTz = sb("KTz", [128, 2, L], BF16)
    B_KTz = Buf("KTz")
    memset("pool", KTz[:], 0.0, writes=[B_KTz])
    for g in range(32):
        pt, pb = ps_next()
        for ri in range(2):
            mm(pt[:, ri * 256:(ri + 1) * 256], Bex[:, g, ri, :], UT[:, g, :], True, True,
               reads=[B_Bex, B_UTg[g // 2]], writes=[pb] if ri == 0 else [])
        pdone(pb)
        cp("act", SX[0:64, g, :, 2:258], pt[0:64, :].rearrange("p (r c) -> p r c", r=2), reads=[pb, B_SXz], writes=[B_Sg[g]])
        cp("dve", SX[64:128, g, :, 0:256], pt[64:128, :].rearrange("p (r c) -> p r c", r=2), reads=[pb, B_SXz], writes=[B_Sg[g]])
    if debug:
        dump("S", SX[:].rearrange("p g r c -> p (g r c)"), B_SXz, 128, 32 * 2 * 258)
    for ct in range(2):
        for gb in range(8):
            pt, pb = ps_next()
            for gl in range(4):
                g = gb * 4 + gl
                mm(pt[:, gl * 128:(gl + 1) * 128], UT[:, g, 128 * ct:128 * ct + 128], Wt[:, g, :], True, True,
                   reads=[B_UTg[g // 2], B_W], writes=[pb] if gl == 0 else [])
            pdone(pb)
            cp("dve", Ysb[:, ct, :, gb * 64:(gb + 1) * 64].rearrange("p t (g h) -> p g t h", g=4),
               pt[:].rearrange("p (g t h) -> p g t h", g=4, t=8), reads=[pb], writes=[B_UY])
    global_barrier()
    release_to(m_left_ssm, side="left")
    NSB, NSD = 8, 3
    STG = sb("STG", [128, NSD, NSB, 32, 2, 3], F32)
    B_stgA = [Buf("stgA%d" % i) for i in range(NSD)]
    B_stgS = [Buf("stgS%d" % i) for i in range(NSD)]
    NEWR = sb("NEWR", [128, NSD, NSB, 32, 2], F32)
    B_new = [Buf("new%d" % i) for i in range(NSD)]
    zst = sb("zst", [128, 32, 2], F32)
    B_zst = Buf("zst")
    memset("dve", zst[:], 0.0, writes=[B_zst])
    M8 = sb("M8", [128, 32, 2, 2], F32)
    B_M8 = Buf("M8")
    cp("dve", M8[:, :, 0, 0:1], A8[:, :, 0:1], reads=[B_A8], writes=[B_M8])
    cp("dve", M8[:, :, 1, 1:2], A8[:, :, 0:1], reads=[B_A8, B_M8], writes=[B_M8])
    cp("dve", M8[:, :, 0, 1:2], A8s[:, :, 0:1], reads=[B_A8, B_M8], writes=[B_M8])
    cp("dve", M8[:, :, 1, 0:1], A8s[:, :, 1:2], reads=[B_A8, B_M8], writes=[B_M8])
    NBT = 256 // NSB

    def bulk_s(bt):
        rb_, c0_ = bt % NSD, NSB * bt
        cp("act", STG[0:64, rb_, :, :, :, 2], SX[0:64, :, :, 2 + c0_:2 + c0_ + NSB].rearrange("p g r s -> p s g r"),
           reads=[B_SXf[bt]] + B_Sg, writes=[B_stgS[rb_]])
        hi_ = 255 - c0_
        lo_ = hi_ - NSB
        cols_ = slice(hi_, lo_, -1) if lo_ >= 0 else slice(hi_, None, -1)
        cp("act", STG[64:128, rb_, :, :, :, 2], SX[64:128, :, :, cols_].rearrange("p g r s -> p s g r"),
           reads=[B_SXb[NBT - 1 - bt]] + B_Sg, writes=[B_stgS[rb_]])

    def conv_s(bt):
        rb_, c0_ = bt % NSD, NSB * bt
        cp("act", SX[0:64, :, :, 2 + c0_:2 + c0_ + NSB], NEWR[0:64, rb_].rearrange("p s g r -> p g r s"),
           reads=[B_new[rb_]], writes=[B_SXf[bt]])
        cp("act", SX[64:128, :, :, 256 - NSB - c0_:256 - c0_], NEWR[64:128, rb_, ::-1].rearrange("p s g r -> p g r s"),
           reads=[B_new[rb_]], writes=[B_SXb[NBT - 1 - bt]])

    bulk_s(0)
    bulk_s(1)

    def scan_gen():
        prev, bprev = zst[:], B_zst
        for bt in range(NBT):
            rb = bt % NSD
            if bt + 2 < NBT:
                bulk_s(bt + 2)
            for sl in range(NSB):
                tt("dve", STG[:, rb, sl, :, :, 0:2], M8[:], prev.unsqueeze(2).broadcast_to([128, 32, 2, 2]), ALU.mult,
                   reads=[B_M8, bprev], writes=[B_stgA[rb]])
                yield
                new = NEWR[:, rb, sl]
                P.op("dve", (lambda new=new, rb=rb, sl=sl: nc.vector.tensor_reduce(
                    out=new, in_=STG[:, rb, sl], op=ALU.add, axis=AX.X)),
                    [B_stgA[rb], B_stgS[rb]], [B_new[rb]])
                prev, bprev = new, B_new[rb]
                if sl == NSB - 1 and bt >= 1:
                    conv_s(bt - 1)
                yield
        conv_s(NBT - 1)
        yield

    scan_it = scan_gen()

    def adv_scan(n=1):
        for _ in range(n):
            next(scan_it, None)

    m_att = mark()
    QKT = sb("QKT", [128, 3, L], BF16)
    B_QKT = Buf("QKT")
    VT = sb("VT", [128, L], BF16)
    B_VT = Buf("VT")
    VA = sb("VA", [128, 3, 16, 192], BF16)
    B_VA = Buf("VA")
    memset("dve", VA[:, :, :, 64:128], 1.0, writes=[B_VA])
    sq_l = [sb("sq%d" % i, [128, 8, 64], F32) for i in range(3)]
    B_sq_l = [Buf("sq%d" % i) for i in range(3)]
    qn_l = [sb("qn%d" % i, [128, 8, 64], BF16) for i in range(3)]
    wqkb = sb("wqkb", [128, 8, 64], BF16)
    B_wqkb = Buf("wqkb")
    cp("act", wqkb[:], wqk[:], reads=[B_wqk], writes=[B_wqkb])
    B_qn_l = [Buf("qn%d" % i) for i in range(3)]
    qbf_l = [ARENA[:, 5120 + 512 * i:5120 + 512 * (i + 1)].rearrange("p (b d) -> p b d", b=8) for i in range(4)]
    B_qbf_l = [Buf("qbf%d" % i) for i in range(4)]
    nst_l = [sb("nst%d" % i, [128, 4, 8], F32) for i in range(4)]
    B_nst_l = [Buf("nst%d" % i) for i in range(4)]
    rp_l = [ARENA[:, 7168 + 512 * i:7168 + 512 * (i + 1)].bitcast(F32).rearrange("p (a b d) -> p a b d", a=2, b=8) for i in range(2)]
    B_rp_l = [Buf("rp%d" % i) for i in range(2)]
    NPT = 3
    PT = [sb("PT%d" % i, [128, 512], BF16) for i in range(NPT)]
    B_PT = [Buf("PT%d" % i) for i in range(NPT)]
    acc_rr = [0]
    ps6_rr = [0]

    psq_rr = [0]
    pso_rr = [0]

    def ps_q():
        i_ = psq_rr[0]
        psq_rr[0] = (i_ + 1) % 4
        return psum[i_]

    def ps_o():
        i_ = pso_rr[0]
        pso_rr[0] = (i_ + 1) % 4
        return psum[4 + i_]

    def ps_next6():
        i_ = ps6_rr[0]
        ps6_rr[0] = (i_ + 1) % 6
        return psum[i_]

    rden = sb("rden", [128, 512], F32)
    B_rden_h = [Buf("rden0"), Buf("rden1")]
    pt_rr = [0]
    for hp in range(4):
        qst = {}

        def stA(i):
            pq, pqb = ps_q()
            for kt in range(8):
                mm(pq[:], xnT[:, kt, 128 * i:128 * (i + 1)], WA[:, kt, 0:4, :], kt == 0, kt == 7,
                   reads=[B_xnT[i], B_WA], writes=[pqb] if kt == 0 else [])
            pdone(pqb)
            qst[i] = (pq, pqb)

        NSQ, NNS, NQN, NQF, NRP = 3, 4, 3, 4, 2

        def stB1(i):
            pq, pqb = qst[i]
            k_ = hp * 16 + i
            act(sq_l[k_ % NSQ][:], pq[:].rearrange("p (b d) -> p b d", b=8), AF.Square, reads=[pqb], writes=[B_sq_l[k_ % NSQ]])

        def dve_ops_B2(i):
            k_ = hp * 16 + i
            sq, B_sq, nst, B_nst = sq_l[k_ % NSQ], B_sq_l[k_ % NSQ], nst_l[k_ % NNS], B_nst_l[k_ % NNS]
            return [
                lambda: P.op("dve", (lambda nst=nst, sq=sq: nc.vector.tensor_reduce(out=nst[:, 0, :], in_=sq[:], op=ALU.add, axis=AX.X)),
                             [B_sq], [B_nst]),
            ]

        def stB3(i):
            k_ = hp * 16 + i
            nst, B_nst = nst_l[k_ % NNS], B_nst_l[k_ % NNS]
            act(nst[:, 2, :], nst[:, 0, :], AF.Sqrt, reads=[B_nst], writes=[B_nst], scale=1.0 / 64, bias=EPS)

        def dve_ops_C1(i):
            pq, pqb = qst[i]
            pq3 = pq[:].rearrange("p (b d) -> p b d", b=8)
            k_ = hp * 16 + i
            qn, B_qn = qn_l[k_ % NQN], B_qn_l[k_ % NQN]
            nst, B_nst = nst_l[k_ % NNS], B_nst_l[k_ % NNS]
            qbf, B_qbf = qbf_l[k_ % NQF], B_qbf_l[k_ % NQF]
            rp, B_rp = rp_l[k_ % NRP], B_rp_l[k_ % NRP]
            cc = ropec[:, i, :].unsqueeze(1).broadcast_to([128, 8, 16])
            return [
                lambda: recip(nst[:, 3, :], nst[:, 2, :], [B_nst], [B_nst]),
                lambda: tt("dve", qn[:], pq3, nst[:, 3, :].unsqueeze(2).broadcast_to([128, 8, 64]), ALU.mult,
                           reads=[pqb, B_nst], writes=[B_qn]),
                lambda: tt("dve", qbf[:], qn[:], wqkb[:], ALU.mult, reads=[B_qn, B_wqkb], writes=[B_qbf]),
                lambda: tt("dve", rp[:, 0], qbf[:, :, 0:16], cc, ALU.mult, reads=[B_qbf, B_ropec], writes=[B_rp]),
                lambda: tt("dve", rp[:, 1].rearrange("p b (h d) -> p b h d", h=2),
                           qbf[:, :, 0:16].rearrange("p b (h d) -> p b h d", h=2)[:, :, ::-1, :],
                           ropes[:, i, :].rearrange("p (h d) -> p h d", h=2).unsqueeze(1).broadcast_to([128, 8, 2, 8]),
                           ALU.mult, reads=[B_qbf, B_ropes, B_rp], writes=[B_rp]),
                lambda: tt("dve", qbf[:, :, 0:16], rp[:, 0], rp[:, 1], ALU.add, reads=[B_rp, B_qbf], writes=[B_qbf]),
            ]

        def dve_step(i2, i3):
            b2 = dve_ops_B2(i2) if 0 <= i2 < 16 else []
            c1 = dve_ops_C1(i3) if 0 <= i3 < 16 else []
            sc = [lambda: adv_scan(1), lambda: adv_scan(1)] if c1 else []
            order = []
            seq = [(c1, 0), (b2, 0), (c1, 1), (sc, 0), (c1, 2), (sc, 1), (c1, 3), (c1, 4), (c1, 5)]
            for lst, k in seq:
                if k < len(lst):
                    lst[k]()

        def stC3(i):
            k_ = hp * 16 + i
            qn, B_qn, qbf, B_qbf = qn_l[k_ % NQN], B_qn_l[k_ % NQN], qbf_l[k_ % NQF], B_qbf_l[k_ % NQF]
            cp("act", qbf[:, :, 16:64], qn[:, :, 16:64], reads=[B_qn], writes=[B_qbf])

        def stD1(i):
            k_ = hp * 16 + i
            qbf, B_qbf = qbf_l[k_ % NQF], B_qbf_l[k_ % NQF]
            ptt, ptb = ps_o()
            for blk in range(4):
                mm(ptt[:, blk * 128:(blk + 1) * 128], qbf[:, 2 * blk:2 * blk + 2, :].rearrange("p b d -> p (b d)"), ident[:],
                   True, True, reads=[B_qbf, B_ID], writes=[ptb] if blk == 0 else [])
            pdone(ptb)
            qst[("t", i)] = (ptt, ptb)

        def stD2(i):
            ptt, ptb = qst[("t", i)]
            cp("act", QKT[:, :, 128 * i:128 * (i + 1)], ptt[:, 0:384].rearrange("p (b n) -> p b n", b=3), reads=[ptb], writes=[B_QKT])
            cp("act", KTz[0:64, 0, 128 * i:128 * (i + 1)], ptt[0:64, 384:512], reads=[ptb], writes=[B_KTz])
            cp("act", KTz[64:128, 1, 128 * i:128 * (i + 1)], ptt[64:128, 384:512], reads=[ptb], writes=[B_KTz])

        def key_tokens(o, j):
            if o == 0:
                return 128 * j, 1
            if o == 1:
                r4, j4 = j // 4, j % 4
                return 512 * j4 + r4, 4
            return j, 16

        vst = {}

        def v_proj(tb):
            pv_, pvb = ps_o()
            for kt in range(8):
                mm(pv_[:], WA[:, kt, 4, :], xnT[:, kt, tb * 512:(tb + 1) * 512], kt == 0, kt == 7,
                   reads=[B_WA] + B_xnT, writes=[pvb] if kt == 0 else [])
            pdone(pvb)
            vst[("p", tb)] = (pv_, pvb)

        def v_proj_ev(tb):
            pv_, pvb = vst[("p", tb)]
            cp("act", VT[:, tb * 512:(tb + 1) * 512], pv_[:], reads=[pvb], writes=[B_VT])

        def v_tr(k):
            o, j0 = k // 4, 4 * (k % 4)
            pv_, pvb = ps_o()
            for jj in range(4):
                st, step = key_tokens(o, j0 + jj)
                mm(pv_[:, jj * 128:(jj + 1) * 128], VT[:, st:st + 127 * step + 1:step], ident[:], True, True,
                   reads=[B_VT, B_ID], writes=[pvb] if jj == 0 else [])
            pdone(pvb)
            vst[("t", k)] = (pv_, pvb)

        def v_tr_ev(k):
            o, j0 = k // 4, 4 * (k % 4)
            pv_, pvb = vst[("t", k)]
            cp("act", VA[:, o, j0:j0 + 4, :].rearrange("p j (h x) -> p j h x", h=3)[:, :, 0:3:2, :],
               pv_[:].rearrange("p (j h d) -> p j h d", j=4, h=2), reads=[pvb], writes=[B_VA])

        for t_ in range(16 + 5):
            for (f_, lag_) in ((stA, 0), (stB1, 1), (stD2, 5)):
                if 0 <= t_ - lag_ < 16:
                    f_(t_ - lag_)
            b2_i, c1_i = t_ - 2, t_ - 3
            if 0 <= b2_i < 16 and not (0 <= c1_i < 16):
                for f_ in dve_ops_B2(b2_i):
                    f_()
            else:
                dve_step(b2_i, c1_i)
            for (f_, lag_) in ((stB3, 2), (stD1, 4)):
                if 0 <= t_ - lag_ < 16:
                    f_(t_ - lag_)
            if 1 <= t_ <= 4:
                v_proj(t_ - 1)
            if 2 <= t_ <= 5:
                v_proj_ev(t_ - 2)
            if 7 <= t_ <= 18:
                v_tr(t_ - 7)
            if 8 <= t_ <= 19:
                v_tr_ev(t_ - 8)
            if t_ == 17 and hp + 1 < 4:
                load_WA(hp + 1)
            if t_ == 17 and hp == 3:
                dma("pool", ARENA[:, 0:4096].rearrange("p (k n) -> p k n", k=4),
                    dr["w_glu"].rearrange("(kt p) n -> p kt n", p=128), writes=[B_WA, B_wglu]).perm = True
            if t_ in (5, 6, 17, 18, 19, 20):
                adv_scan(2)
        if hp == 3:
            dma("pool", ARENA[:, 4096:8192].rearrange("p (k n) -> p k n", k=8),
                dr["w_in"][:, 512:1024].rearrange("(kt p) n -> p kt n", p=128), writes=[B_WA, B_wza] + B_qbf_l + B_rp_l).perm = True
        glist = []
        for hh in range(2):
            for qb in range(4):
                contribs = []
                for j in range(4 * qb - 1, 4 * qb + 5):
                    if j < 0 or j > 15:
                        continue
                    lo = max(128 * j - 64, 512 * qb, 0)
                    hi = min(128 * j + 192, 512 * qb + 512, L)
                    n = hi - lo
                    m0_ = lo - (128 * j - 64)
                    contribs.append((0, j, lo, 1, n, mb01[:, m0_:m0_ + n], lo - 512 * qb, 1))
                for r4 in range(4):
                    for j4 in (qb - 1, qb, qb + 1):
                        if j4 < 0 or j4 > 3:
                            continue
                        if j4 == qb:
                            mlo, n, mc = 128 * qb, 128, 64
                        elif j4 == qb - 1:
                            mlo, n, mc = 128 * qb, 64, 192
                        else:
                            mlo, n, mc = 128 * qb + 64, 64, 0
                        qs = 4 * mlo + r4
                        contribs.append((1, r4 * 4 + j4, qs, 4, n, mb01[:, mc:mc + n], qs - 512 * qb, 4))
                for r16 in range(16):
                    qs = 512 * qb + r16
                    contribs.append((2, r16, qs, 16, 32, mb2[:, 32 * qb:32 * qb + 32], r16, 16))
                groups, cur, used = [], [], 0
                for cbt in contribs:
                    if used + cbt[4] > 512:
                        groups.append(cur)
                        cur, used = [], 0
                    cur.append((cbt, used))
                    used += cbt[4]
                if cur:
                    groups.append(cur)
                for gi_, grp in enumerate(groups):
                    glist.append((hh, qb, grp, gi_ == 0, gi_ == len(groups) - 1))
        gst = {}
        accst = {}

        def stS(k):
            hh, qb, grp, gfirst, glast = glist[k]
            ps_, psb = ps_next6()
            firstg = True
            tot = 0
            for (cbt, off) in grp:
                o, j, qs, qstep, n, mask_ap, a0, astep = cbt
                st, step = key_tokens(o, j)
                mm(ps_[:, off:off + n], KTz[:, hh, st:st + 127 * step + 1:step],
                   QKT[:, o, qs:qs + (n - 1) * qstep + 1:qstep], True, False,
                   reads=[B_QKT, B_KTz], writes=[psb] if firstg else [])
                firstg = False
                mm(ps_[:, off:off + n], ident[:], mask_ap, False, True, reads=[B_ID, B_mb01, B_mb2])
                tot = off + n
            pdone(psb)
            pi = pt_rr[0]
            pt_rr[0] = (pi + 1) % NPT
            act(PT[pi][:, 0:tot], ps_[:, 0:tot], AF.Exp, reads=[psb], writes=[B_PT[pi]], scale=0.125)
            gst[k] = pi

        def stPV(k):
            adv_scan(3 if k % 3 == 0 else 2)
            hh, qb, grp, gfirst, glast = glist[k]
            rows = slice(hh * 64, hh * 64 + 64)
            pi = gst[k]
            if gfirst:
                ai = acc_rr[0]
                acc_rr[0] ^= 1
                accst[(hh, qb)] = psum[6 + ai]
            acc, accb = accst[(hh, qb)]
            for gi2, (cbt, off) in enumerate(grp):
                o, j, qs, qstep, n, mask_ap, a0, astep = cbt
                fst = gfirst and gi2 == 0
                lst = glast and gi2 == len(grp) - 1
                mm(acc[:, a0:a0 + (n - 1) * astep + 1:astep], VA[:, o, j, 64 * hh:64 * hh + 128], PT[pi][:, off:off + n],
                   fst, lst, reads=[B_VA, B_PT[pi]], writes=[accb] if fst else [])
            if glast:
                pdone(accb)
                B_rden = B_rden_h[hh]
                den_rows = slice(64, 128) if hh == 0 else slice(0, 64)
                act(rden[rows, :], acc[den_rows, :], AF.Ln, reads=[accb], writes=[B_rden])
                act(rden[rows, :], rden[rows, :], AF.Exp, reads=[B_rden], writes=[B_rden], scale=-1.0)
                tt("dve", HB[rows, hp, qb * 512:(qb + 1) * 512], acc[rows, :], rden[rows, :], ALU.mult,
                   reads=[accb, B_rden], writes=[B_HB])

        LAG = 2
        for t_ in range(len(glist) + LAG):
            if t_ < len(glist):
                stS(t_)
            if t_ - LAG >= 0:
                stPV(t_ - LAG)
    if debug:
        dump("attnT", HB[:].rearrange("p a t -> p (a t)"), B_HB, 128, 4 * L)
    for _ in scan_it:
        pass
    if debug:
        dump("X", SX[:].rearrange("p g r c -> p (g r c)"), B_SXz, 128, 32 * 2 * 258)
    global_barrier()
    release_to(m_merged)
    YH = sb("YH", [128, 4, L], BF16, side="left")
    wglu = ARENA[:, 0:4096].rearrange("p (k n) -> p k n", k=4)
    wza = ARENA[:, 4096:8192].rearrange("p (k n) -> p k n", k=8)
    wzb = sb("wzb", [128, 8, 512], BF16, side="left")
    B_wzb = Buf("wzb")
    wpa = sb("wpa", [128, 4, 1024], BF16, side="left")
    B_wpa = Buf("wpa")
    wps = sb("wps", [128, 4, 1024], BF16, side="left")
    B_wps = Buf("wps")
    wg = sb("wg", [128, 8, 2048], BF16, side="left")
    B_wg = Buf("wg")
    w_in_cols(wzb[:], 3584, 512, B_wzb, perm=True)
    wload(wps[:], dr["w_proj_ssm"].rearrange("(kt p) n -> p kt n", p=128), B_wps, perm=True)
    wload(wpa[:], dr["w_proj_attn"].rearrange("(kt p) n -> p kt n", p=128), B_wpa, perm=True)
    w_in_cols(wg[:, :, 0:1024], 4096, 1024, B_wg, perm=True)
    w_in_cols(wg[:, :, 1024:2048], 5120, 1024, B_wg, perm=True)
    hat = sb("hat", [128, 4, 512], BF16)
    B_hat = Buf("hat")
    sgt = sb("sgt", [128, 512], F32)
    B_sgt = Buf("sgt")
    szt = sb("szt", [128, 512], F32)
    B_szt = Buf("szt")
    h1t = sb("h1t", [128, 512], F32)
    B_h1t = Buf("h1t")

    def y_inter(ct):
        for gb in range(8):
            pt, pb = ps_next()
            for gl in range(4):
                g = gb * 4 + gl
                o_ = pt[:, gl * 128:(gl + 1) * 128]
                mm(o_, SX[:, g, 0, 128 * ct + 1:128 * ct + 129], CAY[:, 0, g].rearrange("p j h -> p (j h)"), True, False,
                   reads=[B_CAY] + B_SXall + B_Sg, writes=[pb] if gl == 0 else [])
                mm(o_, SX[:, g, 1, 128 * ct + 1:128 * ct + 129], CAY[:, 1, g].rearrange("p j h -> p (j h)"), False, True)
            pdone(pb)
            yv = Ysb[:, ct, :, gb * 64:(gb + 1) * 64].rearrange("p t (g h) -> p g t h", g=4)
            tt("dve", yv, pt[:].rearrange("p (g t h) -> p g t h", g=4, t=8), yv, ALU.add, reads=[pb, B_UYc[ct]], writes=[B_UYc[ct]])

    def y_tr(tb_):
        ct, ch_ = tb_ // 2, tb_ % 2
        for cht in range(4):
            pt, pb = ps_next()
            for t in range(8):
                mm(pt[:, t:t + 505:8], Ysb[:, ct, t, cht * 128:(cht + 1) * 128], ident[:, 64 * ch_:64 * ch_ + 64], True, True,
                   reads=[B_UYc[ct], B_ID], writes=[pb] if t == 0 else [])
            pdone(pb)
            act(YH[:, cht, 512 * tb_:512 * (tb_ + 1)], pt[:], GELU, reads=[pb], writes=[B_YH[tb_]])

    def glu_blk(tb):
        tsl = slice(tb * 512, (tb + 1) * 512)
        for v in range(4):
            pv_, pvb = ps_next()
            for cht in range(4):
                mm(pv_[:], wglu[:, cht, v * 128:(v + 1) * 128], YH[:, cht, tsl], cht == 0, cht == 3,
                   reads=[B_wglu, B_YH[tb]], writes=[pvb] if cht == 0 else [])
            pdone(pvb)
            pg_, pgb = ps_next()
            for cht in range(4):
                mm(pg_[:], wglu[:, cht, 512 + v * 128:512 + (v + 1) * 128], YH[:, cht, tsl], cht == 0, cht == 3,
                   reads=[B_wglu, B_YH[tb]], writes=[pgb] if cht == 0 else [])
            pdone(pgb)
            pz_, pzb = ps_next()
            for kt in range(8):
                mm(pz_[:], wza[:, kt, v * 128:(v + 1) * 128], xnT[:, kt, tsl], kt == 0, kt == 7,
                   reads=[B_wza] + B_xnT, writes=[pzb] if kt == 0 else [])
            pdone(pzb)
            act(sgt[:], pg_[:], AF.Sigmoid, reads=[pgb, B_bglu], writes=[B_sgt], bias=bglu[:, 4 + v:5 + v])
            act(szt[:], pz_[:], AF.Sigmoid, reads=[pzb], writes=[B_szt])
            tt("dve", szt[:], szt[:], pz_[:], ALU.mult, reads=[pzb, B_szt], writes=[B_szt])
            stt(h1t[:], pv_[:], bglu[:, v:v + 1], sgt[:], ALU.add, ALU.mult, [pvb, B_bglu, B_sgt], [B_h1t])
            tt("dve", hat[:, v, :], h1t[:], szt[:], ALU.mult, reads=[B_h1t, B_szt], writes=[B_hat])
        cp("act", YH[:, :, tsl], hat[:], reads=[B_hat], writes=[B_YH[tb]])

    B_UYc = [Buf("UYc0"), Buf("UYc1")]
    for b_ in B_UYc:
        b_.ws = list(B_UY.ws)
    y_inter(0)
    y_tr(0)
    y_tr(1)
    y_inter(1)
    glu_blk(0)
    y_tr(2)
    glu_blk(1)
    y_tr(3)
    glu_blk(2)
    glu_blk(3)
    if debug:
        dump("haT", YH[:].rearrange("p a t -> p (a t)"), B_YH[3], 128, 4 * L)
    global_barrier()
    release_to(m_ssm)
    maybe_stop("SSM")
    wo = sb("wo", [128, 8, 1024], BF16, side="left")
    B_wo = Buf("wo")
    wload(wo[:], dr["w_out"].rearrange("(kt p) n -> p kt n", p=128), B_wo, perm=True)

    maybe_stop("ATT")

    mixT_l = [sb("mixT%d" % i, [128, 8, 512], BF16) for i in range(2)]
    B_mix_l = [Buf("mixT%d" % i) for i in range(2)]
    sga_l = [sb("sga%d" % i, [128, 512], F32) for i in range(2)]
    B_sga_l = [Buf("sga%d" % i) for i in range(2)]
    sgb_l = [sb("sgb%d" % i, [128, 512], F32) for i in range(2)]
    B_sgb_l = [Buf("sgb%d" % i) for i in range(2)]
    m1_l = [sb("m1%d" % i, [128, 512], F32) for i in range(2)]
    B_m1_l = [Buf("m1%d" % i) for i in range(2)]
    szb_l = [sb("szb%d" % i, [128, 512], BF16) for i in range(2)]
    B_szb_l = [Buf("szb%d" % i) for i in range(2)]
    xres = [sb("xres%d" % i, [128, D], F32) for i in range(2)]
    B_xres = [Buf("xres%d" % i) for i in range(2)]
    B_out = Buf("out")
    out_ops = []
    pending_wout = []
    def emit_wout(tb, mixT, B_mix):
        for il in range(4):
            i = tb * 4 + il
            xr, bxr = xres[i % 2], B_xres[i % 2]
            dma("sp", xr[:], dr["x"][128 * i:128 * (i + 1), :], writes=[bxr])
            for half in range(2):
                po, pob = ps_next()
                for n in range(8):
                    mm(po[:], mixT[:, n, il * 128:(il + 1) * 128], wo[:, n, half * 512:(half + 1) * 512], n == 0, n == 7,
                       reads=[B_mix, B_wo], writes=[pob] if n == 0 else [])
                pdone(pob)
                tt("dve", xr[:, half * 512:(half + 1) * 512], xr[:, half * 512:(half + 1) * 512], po[:], ALU.add,
                   reads=[bxr, pob], writes=[bxr])
            out_ops.append(dma("sp", out_d[128 * i:128 * (i + 1), :], xr[:], reads=[bxr], writes=[]))

    for tb in range(4):
        tsl = slice(tb * 512, (tb + 1) * 512)
        mixT, B_mix = mixT_l[tb % 2], B_mix_l[tb % 2]
        for v in range(4):
            pz_, pzb = ps_next()
            for kt in range(8):
                mm(pz_[:], wzb[:, kt, v * 128:(v + 1) * 128], xnT[:, kt, tsl], kt == 0, kt == 7,
                   reads=[B_wzb] + B_xnT, writes=[pzb] if kt == 0 else [])
            pdone(pzb)
            szb, B_szb = szb_l[v % 2], B_szb_l[v % 2]
            act(szb[:], pz_[:], AF.Silu, reads=[pzb], writes=[B_szb])
            tt("dve", HB[:, v, tsl], HB[:, v, tsl], szb[:], ALU.mult, reads=[B_HB, B_szb], writes=[B_HB])
        while pending_wout:
            emit_wout(*pending_wout.pop(0))
        for n in range(8):
            pya, pyab = ps_next()
            for v in range(4):
                mm(pya[:], wps[:, v, n * 128:(n + 1) * 128], YH[:, v, tsl], v == 0, v == 3,
                   reads=[B_wps, B_YH[tb]], writes=[pyab] if v == 0 else [])
            pdone(pyab)
            pyb_, pybb = ps_next()
            for v in range(4):
                mm(pyb_[:], wpa[:, v, n * 128:(n + 1) * 128], HB[:, v, tsl], v == 0, v == 3,
                   reads=[B_wpa, B_HB], writes=[pybb] if v == 0 else [])
            pdone(pybb)
            pga, pgab = ps_next()
            for kt in range(8):
                mm(pga[:], wg[:, kt, n * 128:(n + 1) * 128], xnT[:, kt, tsl], kt == 0, kt == 7,
                   reads=[B_wg] + B_xnT, writes=[pgab] if kt == 0 else [])
            pdone(pgab)
            pgb_, pgbb = ps_next()
            for kt in range(8):
                mm(pgb_[:], wg[:, kt, 1024 + n * 128:1024 + (n + 1) * 128], xnT[:, kt, tsl], kt == 0, kt == 7,
                   reads=[B_wg] + B_xnT, writes=[pgbb] if kt == 0 else [])
            pdone(pgbb)
            sga, B_sga, sgb, B_sgb, m1, B_m1 = sga_l[n % 2], B_sga_l[n % 2], sgb_l[n % 2], B_sgb_l[n % 2], m1_l[n % 2], B_m1_l[n % 2]
            act(sga[:], pga[:], AF.Sigmoid, reads=[pgab, B_bgate], writes=[B_sga], bias=bgate[:, n:n + 1])
            act(sgb[:], pgb_[:], AF.Sigmoid, reads=[pgbb, B_bgate], writes=[B_sgb], bias=bgate[:, 8 + n:9 + n])
            tt("dve", m1[:], pya[:], sga[:], ALU.mult, reads=[pyab, B_sga], writes=[B_m1])
            tt("dve", sgb[:], pyb_[:], sgb[:], ALU.mult, reads=[pybb, B_sgb], writes=[B_sgb])
            tt("dve", mixT[:, n, :], m1[:], sgb[:], ALU.add, reads=[B_m1, B_sgb], writes=[B_mix])
        pending_wout.append((tb, mixT, B_mix))
    while pending_wout:
        emit_wout(*pending_wout.pop(0))
    fin = P.op("sp", lambda: nc.sync.nop(), dbg_bufs, [])
    for o in out_ops:
        fin.deps.add(o)
    P.emit()
    return nc


_CACHE = {}


def kernel(**inputs):
    consts = host_consts()
    if "nc" not in _CACHE:
        try:
            build(DEBUG if DEBUG else None, STOP)
        except StopBuild:
            pass
    nc = _CACHE["nc"]
    shared = {}
    for name, shape in IN_SPECS:
        if name == "x":
            continue
        a = np.asarray(inputs[name], dtype=np.float32)
        shared[name] = np.ascontiguousarray(a.reshape(shape))
    shared.update(consts)
    x = np.asarray(inputs["x"], dtype=np.float32)
    in_maps = []
    for b in range(NCORES):
        m = dict(shared)
        m["x"] = np.ascontiguousarray(x[b])
        in_maps.append(m)
    res = run_bass_kernel_spmd(nc, in_maps, core_ids=list(range(NCORES)))
    _CACHE["res"] = res
    out = np.stack([np.asarray(res.results[b]["out"], dtype=np.float32) for b in range(NCORES)], axis=0)
    return out
```
